# Optimizing a Trainium2 kernel written in Bass

```python
import math
import jax, jax.numpy as jnp
from jax import lax
import numpy as np

D_MODEL = 2048
BATCH = 2
SEQ = 4096
DEPTH = 4

CHUNK = 64
PE_DIM = 256
EPS = 1e-6
RNN_WIDTH = 2048
RNN_BLOCKS = 16
RNN_BLOCK = 128
CONV_WIDTH = 4
LRU_C = 8.0
ATT_HEADS = 16
HEAD_DIM = 128
KV_HEADS = 4
ATT_WIDTH = 2048
KV_WIDTH = 512
IDX_HEADS = 16
IDX_DIM = 64
TOPK_MAX = 256
Q_BLOCK = 128
N_BUCKETS = 32
MAX_DISTANCE = 128
SG_CHUNK = 128
SG_GROUPS = 16
SG_GROUP = 128
SG_WIDTH = 2048
EVEN_SPLITS = (RNN_WIDTH, RNN_WIDTH, ATT_WIDTH, KV_WIDTH, KV_WIDTH, ATT_WIDTH, IDX_HEADS * IDX_DIM, IDX_DIM, IDX_HEADS)
EVEN_IN = 2 * RNN_WIDTH + 2 * ATT_WIDTH + 2 * KV_WIDTH + IDX_HEADS * IDX_DIM + IDX_DIM + IDX_HEADS
ODD_IN = 3 * SG_WIDTH
N_EVEN = (DEPTH + 1) // 2
N_ODD = DEPTH // 2

kernel_name = 'hybrid_rglru_dsa_sgu_trunk'


def rms_norm(x, gain):
    xf = x.astype(jnp.float32)
    y = xf * lax.rsqrt(jnp.mean(xf * xf, axis=-1, keepdims=True) + EPS)
    return (y * gain.astype(jnp.float32)).astype(x.dtype)


def layer_norm(x, gain, bias):
    xf = x.astype(jnp.float32)
    mu = jnp.mean(xf, axis=-1, keepdims=True)
    xc = xf - mu
    y = xc * lax.rsqrt(jnp.mean(xc * xc, axis=-1, keepdims=True) + EPS)
    return (y * gain.astype(jnp.float32) + bias.astype(jnp.float32)).astype(x.dtype)


def t5_bucket(rel):
    half = N_BUCKETS // 2
    max_exact = half // 2
    ret = jnp.where(rel > 0, half, 0)
    n = jnp.abs(rel)
    nf = jnp.maximum(n, 1).astype(jnp.float32)
    large = max_exact + (jnp.log(nf / max_exact) / math.log(MAX_DISTANCE / max_exact) * (half - max_exact)).astype(jnp.int32)
    large = jnp.minimum(large, half - 1)
    return ret + jnp.where(n < max_exact, n, large)


def causal_dwconv(x, w, b):
    c = x.shape[-1]
    xp = jnp.pad(x, ((0, 0), (CONV_WIDTH - 1, 0), (0, 0)))
    y = lax.conv_general_dilated(xp, w.astype(x.dtype)[:, None, :], window_strides=(1,), padding='VALID',
                                 dimension_numbers=('NWC', 'WIO', 'NWC'), feature_group_count=c)
    return y + b.astype(x.dtype)


def rg_lru(x, w_r, b_r, w_i, b_i, lam):
    bsz, t = x.shape[:2]
    xb = x.reshape(bsz, t, RNN_BLOCKS, RNN_BLOCK)
    r = jax.nn.sigmoid((jnp.einsum('btgi,gij->btgj', xb, w_r) + b_r).astype(jnp.float32)).reshape(bsz, t, RNN_WIDTH)
    i = jax.nn.sigmoid((jnp.einsum('btgi,gij->btgj', xb, w_i) + b_i).astype(jnp.float32)).reshape(bsz, t, RNN_WIDTH)
    log_a = -LRU_C * r * jax.nn.softplus(-lam.astype(jnp.float32))
    a = jnp.exp(log_a)
    mult = jnp.sqrt(jnp.maximum(-jnp.expm1(2.0 * log_a), 0.0))
    bx = mult * i * x.astype(jnp.float32)

    def combine(left, right):
        a1, b1 = left
        a2, b2 = right
        return a1 * a2, a2 * b1 + b2

    _, h = lax.associative_scan(combine, (a, bx), axis=1)
    return h.astype(x.dtype)


def dsa_attention(q, k, v, iq, ik, iw, rel_table, topk):
    bsz, t = q.shape[:2]
    n_blocks = t // Q_BLOCK
    rep = ATT_HEADS // KV_HEADS
    key_pos = jnp.arange(t)
    idx_scale = (IDX_DIM ** -0.5) * (IDX_HEADS ** -0.5)
    att_scale = HEAD_DIM ** -0.5

    def block(bi):
        t0 = bi * Q_BLOCK
        qb = lax.dynamic_slice_in_dim(q, t0, Q_BLOCK, axis=1)
        iqb = lax.dynamic_slice_in_dim(iq, t0, Q_BLOCK, axis=1)
        iwb = lax.dynamic_slice_in_dim(iw, t0, Q_BLOCK, axis=1)
        q_pos = t0 + jnp.arange(Q_BLOCK)
        limit = (q_pos // CHUNK + 1) * CHUNK
        s_h = jax.nn.relu(jnp.einsum('bqhd,bsd->bqhs', iqb, ik).astype(jnp.float32))
        score = jnp.einsum('bqhs,bqh->bqs', s_h, iwb.astype(jnp.float32)) * idx_scale
        admissible = key_pos[None, :] < limit[:, None]
        score = jnp.where(admissible[None], score, -jnp.inf)
        _, sel = lax.top_k(score, topk)
        valid = sel < limit[None, :, None]
        kg = jax.vmap(lambda kb, ib: kb[ib])(k, sel)
        vg = jax.vmap(lambda vb, ib: vb[ib])(v, sel)
        qg = qb.reshape(bsz, Q_BLOCK, KV_HEADS, rep, HEAD_DIM)
        logits = jnp.einsum('bqgrd,bqkgd->bqgrk', qg, kg).astype(jnp.float32) * att_scale
        bias = rel_table[t5_bucket(sel - q_pos[None, :, None])].astype(jnp.float32)
        bias = bias.reshape(bsz, Q_BLOCK, topk, KV_HEADS, rep).transpose(0, 1, 3, 4, 2)
        logits = jnp.where(valid[:, :, None, None, :], logits + bias, -jnp.inf)
        probs = jax.nn.softmax(logits, axis=-1).astype(v.dtype)
        o = jnp.einsum('bqgrk,bqkgd->bqgrd', probs, vg)
        return o.reshape(bsz, Q_BLOCK, ATT_WIDTH)

    out = lax.map(block, jnp.arange(n_blocks))
    return out.transpose(1, 0, 2, 3).reshape(bsz, t, ATT_WIDTH)


def even_mixer(hn, w_in, conv_w, conv_b, w_r, b_r, w_i, b_i, lam, q_gain, k_gain, rel_table, w_out, topk):
    bsz, t, _ = hn.shape
    offsets = [int(o) for o in np.cumsum(EVEN_SPLITS)[:-1]]
    xa, ga, q, k, v, gb, iq, ik, iw = jnp.split(hn @ w_in, offsets, axis=-1)
    ya = rg_lru(causal_dwconv(xa, conv_w, conv_b), w_r, b_r, w_i, b_i, lam) * jax.nn.silu(ga)
    q = rms_norm(q.reshape(bsz, t, ATT_HEADS, HEAD_DIM), q_gain)
    k = rms_norm(k.reshape(bsz, t, KV_HEADS, HEAD_DIM), k_gain)
    v = v.reshape(bsz, t, KV_HEADS, HEAD_DIM)
    iq = iq.reshape(bsz, t, IDX_HEADS, IDX_DIM)
    yb = dsa_attention(q, k, v, iq, ik, iw, rel_table, topk) * jax.nn.silu(gb)
    return jnp.concatenate([ya, yb], axis=-1) @ w_out


def odd_mixer(hn, w_in, ln_g, ln_b, w_s, b_s, w_out):
    bsz, t, _ = hn.shape
    u, v, g = jnp.split(hn @ w_in, 3, axis=-1)
    u = jax.nn.gelu(u)
    v = layer_norm(jax.nn.gelu(v), ln_g, ln_b)
    n_chunks = t // SG_CHUNK
    vc = v.reshape(bsz, n_chunks, SG_CHUNK, SG_GROUPS, SG_GROUP)
    cpos = jnp.arange(SG_CHUNK) // CHUNK
    mask = cpos[:, None] >= cpos[None, :]
    ws = jnp.where(mask[None], w_s, 0.0).astype(v.dtype)
    mixed = jnp.einsum('gts,bcsgd->bctgd', ws, vc) + b_s.T[:, :, None].astype(v.dtype)
    y = u * mixed.reshape(bsz, t, SG_WIDTH) * jax.nn.silu(g)
    return y @ w_out


def per_layer_embedding(h, p_i, w_pe, gate_norm, w_gate):
    e = p_i.astype(h.dtype) @ w_pe
    gate = jax.nn.sigmoid((rms_norm(h, gate_norm) @ w_gate).astype(jnp.float32)).astype(h.dtype)
    return gate * e


def setup_inputs(seed: int = 0) -> dict:
    key = jax.random.key(seed)
    ks = jax.random.split(key, 26)
    f32 = jnp.float32

    def nrm(k, shape, scale):
        return jax.random.normal(k, shape, f32) * scale

    u = jax.random.uniform(ks[11], (N_EVEN, RNN_WIDTH), f32, 0.9, 0.999)
    a = u ** (1.0 / LRU_C)
    return {
        'x': nrm(ks[0], (BATCH, SEQ, D_MODEL), 1.0),
        'p': nrm(ks[1], (DEPTH, BATCH, SEQ, PE_DIM), 1.0),
        'norm_gain': 1.0 + nrm(ks[2], (DEPTH, D_MODEL), 0.02),
        'rel_bias': nrm(ks[3], (N_BUCKETS, ATT_HEADS), 0.5),
        'even_w_in': nrm(ks[4], (N_EVEN, D_MODEL, EVEN_IN), D_MODEL ** -0.5),
        'conv_w': nrm(ks[5], (N_EVEN, CONV_WIDTH, RNN_WIDTH), CONV_WIDTH ** -0.5),
        'conv_b': nrm(ks[6], (N_EVEN, RNN_WIDTH), 0.01),
        'lru_w_r': nrm(ks[7], (N_EVEN, RNN_BLOCKS, RNN_BLOCK, RNN_BLOCK), RNN_BLOCK ** -0.5),
        'lru_b_r': nrm(ks[8], (N_EVEN, RNN_BLOCKS, RNN_BLOCK), 0.01),
        'lru_w_i': nrm(ks[9], (N_EVEN, RNN_BLOCKS, RNN_BLOCK, RNN_BLOCK), RNN_BLOCK ** -0.5),
        'lru_b_i': nrm(ks[10], (N_EVEN, RNN_BLOCKS, RNN_BLOCK), 0.01),
        'lru_lambda': jnp.log(a) - jnp.log1p(-a),
        'q_norm': 1.0 + nrm(ks[12], (N_EVEN, HEAD_DIM), 0.02),
        'k_norm': 1.0 + nrm(ks[13], (N_EVEN, HEAD_DIM), 0.02),
        'even_w_out': nrm(ks[14], (N_EVEN, RNN_WIDTH + ATT_WIDTH, D_MODEL), (RNN_WIDTH + ATT_WIDTH) ** -0.5),
        'odd_w_in': nrm(ks[15], (N_ODD, D_MODEL, ODD_IN), D_MODEL ** -0.5),
        'sg_ln_g': 1.0 + nrm(ks[16], (N_ODD, SG_WIDTH), 0.02),
        'sg_ln_b': nrm(ks[17], (N_ODD, SG_WIDTH), 0.01),
        'sg_w_s': nrm(ks[18], (N_ODD, SG_GROUPS, SG_CHUNK, SG_CHUNK), SG_CHUNK ** -0.5),
        'sg_b_s': 1.0 + nrm(ks[19], (N_ODD, SG_GROUPS, SG_CHUNK), 0.1),
        'odd_w_out': nrm(ks[20], (N_ODD, SG_WIDTH, D_MODEL), SG_WIDTH ** -0.5),
        'pe_w': nrm(ks[21], (DEPTH, PE_DIM, D_MODEL), PE_DIM ** -0.5),
        'pe_gate_norm': 1.0 + nrm(ks[22], (DEPTH, D_MODEL), 0.02),
        'pe_w_gate': nrm(ks[23], (DEPTH, D_MODEL, D_MODEL), D_MODEL ** -0.5),
    }


def reference(x, p, norm_gain, rel_bias, even_w_in, conv_w, conv_b, lru_w_r, lru_b_r, lru_w_i, lru_b_i,
              lru_lambda, q_norm, k_norm, even_w_out, odd_w_in, sg_ln_g, sg_ln_b, sg_w_s, sg_b_s, odd_w_out,
              pe_w, pe_gate_norm, pe_w_gate):
    topk = min(TOPK_MAX, x.shape[1] // 4)
    h = x
    for layer in range(DEPTH):
        hn = rms_norm(h, norm_gain[layer])
        j = layer // 2
        if layer % 2 == 0:
            mix = even_mixer(hn, even_w_in[j], conv_w[j], conv_b[j], lru_w_r[j], lru_b_r[j], lru_w_i[j],
                             lru_b_i[j], lru_lambda[j], q_norm[j], k_norm[j], rel_bias, even_w_out[j], topk)
        else:
            mix = odd_mixer(hn, odd_w_in[j], sg_ln_g[j], sg_ln_b[j], sg_w_s[j], sg_b_s[j], odd_w_out[j])
        h = h + mix
        h = h + per_layer_embedding(h, p[layer], pe_w[layer], pe_gate_norm[layer], pe_w_gate[layer])
    return h
```

```python
import os
from contextlib import ExitStack
import numpy as np
import concourse.bass as bass
import concourse.mybir as mybir
from concourse.bass_utils import run_bass_kernel_spmd

F32 = mybir.dt.float32
BF16 = mybir.dt.bfloat16
AF = mybir.ActivationFunctionType
ALU = mybir.AluOpType
AX = mybir.AxisListType

T = 4096
D = 2048
NT = T // 128
EPS = 1e-6
NEG = -1.0e30
NIT = 16
NDS = 12
EPOCH_LIMIT = 9000
SAME = {'pe': False, 'act': True, 'dve': True, 'pool': True, 'sp': False}


class H:
    __slots__ = ('w', 'r')

    def __init__(self):
        self.w = None
        self.r = {}


class K:
    def __init__(self, nc, es):
        self.nc = nc
        self.E = {'pe': nc.tensor, 'act': nc.scalar, 'dve': nc.vector, 'pool': nc.gpsimd, 'sp': nc.sync}
        self.sems = [{k: es.enter_context(nc.semaphore(f"e{s}_{k}")) for k in self.E} for s in range(3)]
        self.dsems = [[es.enter_context(nc.semaphore(f"d{s}_{i}")) for i in range(NDS)] for s in range(3)]
        self.ep = 0
        self.cnt = {k: 0 for k in self.E}
        self.dval = [0] * NDS
        self.dnext = 0
        self.seen = {k: {} for k in self.E}
        self.handles = []

    def h(self):
        x = H()
        self.handles.append(x)
        return x

    def hs(self, n):
        return [self.h() for _ in range(n)]

    def _wait(self, e, key, val):
        if key[0] == 'e' and key[1] == e and not SAME[e]:
            return
        if self.seen[e].get(key, 0) >= val:
            return
        s = self.ep % 3
        sem = self.sems[s][key[1]] if key[0] == 'e' else self.dsems[s][key[1]]
        self.E[e].wait_ge(sem, val)
        self.seen[e][key] = val

    def _deps(self, e, rd, wr):
        d = {}
        for h in rd:
            if h.w is not None:
                d[h.w[0]] = max(d.get(h.w[0], 0), h.w[1])
        for h in wr:
            if h.w is not None:
                d[h.w[0]] = max(d.get(h.w[0], 0), h.w[1])
            for kk, v in h.r.items():
                d[kk] = max(d.get(kk, 0), v)
        for kk, v in d.items():
            self._wait(e, kk, v)

    def _mark(self, tok, rd, wr):
        for h in wr:
            h.w = tok
            h.r = {}
        for h in rd:
            if h not in wr:
                h.r[tok[0]] = max(h.r.get(tok[0], 0), tok[1])

    def op(self, e, fn, rd=(), wr=()):
        self._deps(e, rd, wr)
        ins = fn(self.E[e])
        self.cnt[e] += 1
        ins.then_inc(self.sems[self.ep % 3][e], 1)
        self._mark((('e', e), self.cnt[e]), rd, wr)

    def dma(self, q, out, in_, rd=(), wr=()):
        i = self.dnext
        self.dnext = (i + 1) % NDS
        if self.dval[i] > 0:
            self._wait(q, ('d', i), self.dval[i])
        self._deps(q, rd, wr)
        ins = self.E[q].dma_start(out=out, in_=in_)
        self.dval[i] += 16
        ins.then_inc(self.dsems[self.ep % 3][i], 16)
        self._mark((('d', i), self.dval[i]), rd, wr)

    def barrier(self):
        for e in self.E:
            for o in self.E:
                if o != e and self.cnt[o] > 0:
                    self._wait(e, ('e', o), self.cnt[o])
            for i in range(NDS):
                if self.dval[i] > 0:
                    self._wait(e, ('d', i), self.dval[i])
        self.ep += 1
        self.cnt = {k: 0 for k in self.E}
        self.dval = [0] * NDS
        self.seen = {k: {} for k in self.E}
        for h in self.handles:
            h.w = None
            h.r = {}
        if self.ep >= 2:
            o = (self.ep + 1) % 3
            for e in ('pe', 'act', 'dve', 'pool', 'sp'):
                self.op(e, lambda eng, e=e: eng.sem_clear(self.sems[o][e]))
            for i in range(NDS):
                self.op('sp', lambda eng, i=i: eng.sem_clear(self.dsems[o][i]))

    def maybe_barrier(self):
        if max(self.cnt.values()) > EPOCH_LIMIT or max(self.dval) > EPOCH_LIMIT * 16:
            self.barrier()


_UN = [0]


def un(name):
    _UN[0] += 1
    return f"{name}_u{_UN[0]}"


def t5_bucket_np(rel):
    half = 16
    max_exact = 8
    ret = np.where(rel > 0, half, 0)
    n = np.abs(rel)
    nf = np.maximum(n, 1).astype(np.float32)
    large = max_exact + (np.log(nf / max_exact) / np.float32(np.log(128 / max_exact)) * (half - max_exact)).astype(np.int32)
    large = np.minimum(large, half - 1)
    return ret + np.where(n < max_exact, n, large)


def build_program(layer_list):
    nc = bass.Bass("TRN2", target_bir_lowering=False)
    es = ExitStack()
    k = K(nc, es)

    def din(name, shape, dt=F32):
        return nc.dram_tensor(name, list(shape), dt, kind="ExternalInput").ap()

    DBG = bool(os.environ.get("KDEBUG"))
    PH = os.environ.get("KPHASES", "E1,E2,E3,E4").split(",")

    def dscr(name, shape, dt):
        return nc.dram_tensor(name, list(shape), dt, kind=("ExternalOutput" if DBG else "Internal")).ap()

    x_d = din("x", [T, D])
    p_d = din("p", [4, T, 256])
    ng_d = din("norm_gain", [4, D])
    pgn_d = din("pe_gate_norm", [4, D])
    consts_d = din("consts", [128, 128])
    oh_d = din("oh", [32, 2 * 128 * 128])
    rb_d = din("rel_bias", [32, 16])
    ewf_d = din("ewf", [2, 77, 128, 16 * 128])
    ewv_d = din("ewv", [2, 128, 16 * 512])
    ewiw_d = din("ewiw", [2, 128, 16 * 16])
    ewo_d = din("ewo", [2, 4, 128, 32 * 512])
    lwr_d = din("lwr", [2, 128, 16 * 128])
    lwi_d = din("lwi", [2, 128, 16 * 128])
    evec_d = din("evec", [2, 128, 8 * 16])
    qkn_d = din("qkn", [2, 128, 2])
    owf_d = din("owf", [2, 32, 128, 16 * 128])
    owv_d = din("owv", [2, 4, 128, 16 * 512])
    owo_d = din("owo", [2, 4, 128, 16 * 512])
    wst_d = din("wst", [2, 128, 16 * 128])
    ovec_d = din("ovec", [2, 128, 2 * 16])
    bs_d = din("bs", [2, 16, 128])
    pew_d = din("pew", [4, 4, 128, 2 * 512])
    pwg_d = din("pwg", [4, 4, 128, 16 * 512])
    y_d = nc.dram_tensor("y", [T, D], F32, kind="ExternalOutput").ap()

    hbuf_d = dscr("hbuf", [T, D], F32)
    xa_d = dscr("xaT", [2048, T], F32)
    sga_d = dscr("sgaT", [2048, T], BF16)
    sgb_d = dscr("sgbT", [2048, T], BF16)
    q_d = dscr("qT", [2048, T], BF16)
    kk_d = dscr("kT", [512, T], BF16)
    iq_d = dscr("iqT", [1024, T], BF16)
    ik_d = dscr("ikT", [128, T], BF16)
    v_d = dscr("vaug", [T, 4 * 130], BF16)
    iw_d = dscr("iw", [T, 16], F32)
    yT_d = dscr("yT", [4096, T], BF16)

    sb = lambda name, shape, dt: es.enter_context(nc.sbuf_tensor(un(name), list(shape), dt))
    ps = lambda name, shape, dt: es.enter_context(nc.psum_tensor(un(name), list(shape), dt))

    ident_f = sb("ident_f", [128, 128], F32); h_ident = k.h()
    ident_b = sb("ident_b", [128, 128], BF16)
    onesm = sb("onesm", [128, 128], F32)
    ones1 = sb("ones1", [128, 128], F32)
    diagneg = sb("diagneg", [128, 128], F32)
    epsc = sb("epsc", [128, 1], F32)
    onec = sb("onec", [128, 1], F32)
    EB = sb("EB", [128, 2 * 16 * 128], BF16)
    h_c = k.h()

    k.dma('sp', ident_f[:], consts_d, wr=[h_c])
    k.op('dve', lambda e: e.tensor_copy(out=ident_b[:], in_=ident_f[:]), rd=[h_c], wr=[h_c])
    k.op('dve', lambda e: e.memset(onesm[:], 1.0 / 128), wr=[h_c])
    k.op('dve', lambda e: e.memset(ones1[:], 1.0), wr=[h_c])
    k.op('dve', lambda e: e.memset(diagneg[:], 0.0), wr=[h_c])
    k.op('dve', lambda e: e.memset(diagneg[0:64, 64:128], NEG), wr=[h_c])
    k.op('dve', lambda e: e.memset(epsc[:], EPS), wr=[h_c])
    k.op('dve', lambda e: e.memset(onec[:], 1.0), wr=[h_c])

    with nc.sbuf_tensor("oh_s0", [32, 2 * 128 * 128], F32) as oh_s, nc.sbuf_tensor("rb_s0", [32, 16], F32) as rb_s, \
            nc.psum_tensor("bps", [128, 512], F32) as bps:
        h_oh, h_bps = k.h(), k.h()
        k.dma('sp', oh_s[:], oh_d, wr=[h_oh])
        k.dma('sp', rb_s[:], rb_d, wr=[h_oh])
        EBv = EB[:].rearrange("s (k h t) -> s k h t", k=2, h=16)
        ohv = oh_s[:].rearrange("b (k s t) -> b k s t", k=2, s=128)
        for kind in range(2):
            for t0 in range(0, 128, 32):
                for tt in range(32):
                    k.op('pe', lambda e, kind=kind, t=t0 + tt, tt=tt: e.matmul(
                        bps[:, tt * 16:(tt + 1) * 16], lhsT=ohv[:, kind, :, t], rhs=rb_s[:], start=True, stop=True),
                        rd=[h_oh], wr=[h_bps])
                k.op('act', lambda e, kind=kind, t0=t0: e.activation(
                    out=EBv[:, kind, :, t0:t0 + 32].rearrange("s h t -> s t h"),
                    in_=bps[:].rearrange("s (t h) -> s t h", h=16), func=AF.Exp), rd=[h_bps], wr=[h_c])
        k.barrier()

    def rms_to_T(htile, gain_bc, dstT, col0, h_in, h_dst, bufs):
        junk, ss, sd, rs, hp, tp, h_j, h_hp, h_tp = bufs
        k.op('act', lambda e: e.activation(out=junk[:], in_=htile, func=AF.Square, accum_out=ss[:, 0:1]),
             rd=[h_in], wr=[h_j])
        k.op('act', lambda e: e.activation(out=sd[:, 0:1], in_=ss[:, 0:1], func=AF.Sqrt, scale=1.0 / D, bias=epsc[:, 0:1]),
             rd=[h_j, h_c], wr=[h_j])
        k.op('dve', lambda e: e.reciprocal(out=rs[:, 0:1], in_=sd[:, 0:1]), rd=[h_j], wr=[h_j])
        k.op('dve', lambda e: e.scalar_tensor_tensor(out=hp[:], in0=htile, scalar=rs[:, 0:1], in1=gain_bc,
                                                      op0=ALU.mult, op1=ALU.mult), rd=[h_in, h_j, h_c], wr=[h_hp])
        for g4 in range(4):
            for j in range(4):
                dc = g4 * 4 + j
                k.op('pe', lambda e, dc=dc, j=j: e.transpose(out=tp[:, j * 128:(j + 1) * 128],
                                                              in_=hp[:, dc * 128:(dc + 1) * 128], identity=ident_b[:]),
                     rd=[h_hp, h_c], wr=[h_tp])
            k.op('act', lambda e, g4=g4: e.activation(out=dstT[:, g4 * 4:(g4 + 1) * 4, col0:col0 + 128],
                                                       in_=tp[:].rearrange("p (j t) -> p j t", j=4), func=AF.Copy),
                 rd=[h_tp], wr=[h_dst])

    state = {'wi': 0}

    def load_w(src_ap, n, ST, WB, h_st, h_wb, view=None):
        i = state['wi'] % 2
        state['wi'] += 1
        k.dma('sp', ST[i][:, 0:n], src_ap, wr=[h_st[i]])
        k.op('pool', lambda e: e.tensor_copy(out=WB[i][:, 0:n], in_=ST[i][:, 0:n]), rd=[h_st[i]], wr=[h_wb[i]])
        return WB[i], h_wb[i]

    def load_big(src2d, n, dst, h_dst_, ST, h_st):
        for c0 in range(0, n, 2048):
            w = min(2048, n - c0)
            i = state['wi'] % 2
            state['wi'] += 1
            k.dma('sp', ST[i][:, 0:w], src2d[:, c0:c0 + w], wr=[h_st[i]])
            k.op('pool', lambda e, i=i, c0=c0, w=w: e.tensor_copy(out=dst[:, c0:c0 + w], in_=ST[i][:, 0:w]), rd=[h_st[i]], wr=[h_dst_])

    def tail_block(layer, tb, hblk, h_hblk, yT, h_yT, CC, wo_src, tl):
        (ST, WB, h_st, h_wb, WO, h_wo, hgT, h_hgT, normbufs, pa, pb, h_pa, h_pb, pt_s, ptb, pT, h_pt, h_pT,
         sg, ge, h_sg, gbc) = tl
        for dblk in range(4):
            wo = WO[dblk % len(WO)]
            hwo = h_wo[dblk % len(WO)]
            load_big(wo_src[dblk], CC * 512, wo, hwo, ST, h_st)
            for tile in range(4):
                pp, hp_ = (pa, h_pa) if tile % 2 == 0 else (pb, h_pb)
                for cc in range(CC):
                    k.op('pe', lambda e, cc=cc, tile=tile, pp=pp, wo=wo: e.matmul(
                        pp[:], lhsT=yT[:, cc, tile * 128:(tile + 1) * 128], rhs=wo[:, cc * 512:(cc + 1) * 512],
                        start=(cc == 0), stop=(cc == CC - 1)), rd=[h_yT, hwo], wr=[hp_])
                k.op('dve', lambda e, tile=tile, dblk=dblk, pp=pp: e.tensor_tensor(
                    out=hblk[:, tile, dblk * 512:(dblk + 1) * 512], in0=pp[:], in1=hblk[:, tile, dblk * 512:(dblk + 1) * 512],
                    op=ALU.add), rd=[hp_, h_hblk], wr=[h_hblk])
        for tile in range(4):
            rms_to_T(hblk[:, tile, :], gbc[:], hgT, tile * 128, h_hblk, h_hgT, normbufs)
            k.dma('sp', pt_s[:], p_d[layer, tb * 512 + tile * 128: tb * 512 + (tile + 1) * 128, :], wr=[h_pt])
            k.op('pool', lambda e: e.tensor_copy(out=ptb[:], in_=pt_s[:]), rd=[h_pt], wr=[h_pt])
            tp, h_tp = normbufs[5], normbufs[8]
            for c in range(2):
                k.op('pe', lambda e, c=c: e.transpose(out=tp[:, c * 128:(c + 1) * 128], in_=ptb[:, c * 128:(c + 1) * 128],
                                                      identity=ident_b[:]), rd=[h_pt, h_c], wr=[h_tp])
            k.op('act', lambda e, tile=tile: e.activation(out=pT[:, :, tile * 128:(tile + 1) * 128],
                                                          in_=tp[:, 0:256].rearrange("p (j t) -> p j t", j=2), func=AF.Copy),
                 rd=[h_tp], wr=[h_pT])
        for dblk in range(4):
            load_big(pwg_d[layer, dblk], 8192, WB[0], h_wb[0], ST, h_st)
            load_big(pew_d[layer, dblk], 1024, WB[1], h_wb[1], ST, h_st)
            for tile in range(4):
                for dc in range(16):
                    k.op('pe', lambda e, dc=dc, tile=tile: e.matmul(
                        pa[:], lhsT=hgT[:, dc, tile * 128:(tile + 1) * 128], rhs=WB[0][:, dc * 512:(dc + 1) * 512],
                        start=(dc == 0), stop=(dc == 15)), rd=[h_hgT, h_wb[0]], wr=[h_pa])
                for c in range(2):
                    k.op('pe', lambda e, c=c, tile=tile: e.matmul(
                        pb[:], lhsT=pT[:, c, tile * 128:(tile + 1) * 128], rhs=WB[1][:, c * 512:(c + 1) * 512],
                        start=(c == 0), stop=(c == 1)), rd=[h_pT, h_wb[1]], wr=[h_pb])
                k.op('act', lambda e: e.activation(out=sg[:], in_=pa[:], func=AF.Sigmoid), rd=[h_pa], wr=[h_sg])
                k.op('dve', lambda e: e.tensor_tensor(out=ge[:], in0=pb[:], in1=sg[:], op=ALU.mult), rd=[h_pb, h_sg], wr=[h_sg])
                k.op('dve', lambda e, tile=tile, dblk=dblk: e.tensor_tensor(
                    out=hblk[:, tile, dblk * 512:(dblk + 1) * 512], in0=ge[:], in1=hblk[:, tile, dblk * 512:(dblk + 1) * 512],
                    op=ALU.add), rd=[h_sg, h_hblk], wr=[h_hblk])


    def alloc_tail(ls, CC):
        sbl = lambda name, shape, dt: ls.enter_context(nc.sbuf_tensor(un(name), list(shape), dt))
        psl = lambda name, shape, dt: ls.enter_context(nc.psum_tensor(un(name), list(shape), dt))
        ST = [sbl(f"ST{i}", [128, 2048], F32) for i in range(2)]
        WB = [sbl("WB0", [128, 8192], BF16), sbl("WB1", [128, 2048], BF16)]
        WO = [sbl("WO0", [128, CC * 512], BF16)]
        hgT = sbl("hgT", [128, 16, 512], BF16)
        ss = sbl("ss", [128, 1], F32); sd = sbl("sd", [128, 1], F32); rs = sbl("rs", [128, 1], F32)
        hp = sbl("hp", [128, 2048], BF16)
        tp = psl("tp", [128, 512], BF16)
        h_hp_ = k.h()
        normbufs = (hp, ss, sd, rs, hp, tp, h_hp_, h_hp_, k.h())
        pa = psl("pa", [128, 512], F32); pb = psl("pb", [128, 512], F32)
        pt_s = sbl("pt_s", [128, 256], F32); ptb = sbl("ptb", [128, 256], BF16)
        pT = sbl("pT", [128, 2, 512], BF16)
        sg = sbl("sg", [128, 512], F32); ge = sbl("ge", [128, 512], F32)
        gbc = sbl("gbc", [128, 2048], F32)
        return (ST, WB, k.hs(2), k.hs(2), WO, k.hs(1), hgT, k.h(), normbufs, pa, pb, k.h(), k.h(), pt_s, ptb, pT,
                k.h(), k.h(), sg, ge, k.h(), gbc)

    def odd_layer(layer, j, h_src, h_dst):
        with ExitStack() as ls:
            sbl = lambda name, shape, dt: ls.enter_context(nc.sbuf_tensor(un(name), list(shape), dt))
            psl = lambda name, shape, dt: ls.enter_context(nc.psum_tensor(un(name), list(shape), dt))
            tl = alloc_tail(ls, 16)
            ST, WB, h_st, h_wb = tl[0], tl[1], tl[2], tl[3]
            normbufs = tl[8]
            pa, pb, h_pa, h_pb = tl[9], tl[10], tl[11], tl[12]
            gbc = tl[21]
            hblk = sbl("hblk", [128, 4, 2048], F32); h_hblk = k.h()
            hnT = tl[6]; h_hnT = tl[7]
            usg = sbl("usg", [128, 16, 512], BF16); h_usg = k.h()
            tmpb = sbl("tmpb", [128, 512], BF16); h_tmpb = k.h()
            WC = [sbl(f"WC{i}", [128, 2048], BF16) for i in range(2)]; h_wc = k.hs(2)
            gv = sbl("gv", [128, 4, 2048], F32); h_gv = k.h()
            z = normbufs[4]; h_z = normbufs[7]
            yT = usg; h_yT = h_usg
            nbc = sbl("nbc", [128, 2048], F32)
            ovec = sbl("ovec", [128, 32], F32)
            wsT_f = gv[:, 0, :]
            wsT_b = sbl("wsT_b", [128, 16, 128], BF16)
            bsbc = gv[:, 1, :]
            Cg = sbl("Cg", [128, 16, 128], F32)
            st6 = sbl("st6", [128, 4, 6], F32); mv = sbl("mv", [128, 2], F32); h_st6 = k.h()
            lsd = sbl("lsd", [128, 1], F32); lrs = sbl("lrs", [128, 1], F32); nmr = sbl("nmr", [128, 1], F32)
            mx = sbl("mx", [128, 128], F32); h_mx = k.h()
            mps = psl("mps", [128, 512], F32); h_mps = k.h()
            h_lc = k.h()
            k.dma('sp', nbc[:], ng_d[layer, :].partition_broadcast(128), wr=[h_c])
            k.dma('sp', gbc[:], pgn_d[layer, :].partition_broadcast(128), wr=[h_c])
            k.dma('sp', ovec[:], ovec_d[j], wr=[h_lc])
            k.dma('sp', wsT_f, wst_d[j], wr=[h_lc])
            for g in range(16):
                k.dma('sp', bsbc[:, g * 128:(g + 1) * 128], bs_d[j, g, :].partition_broadcast(128), wr=[h_lc])
            wv = wsT_f.rearrange("s (g t) -> s g t", g=16)
            k.op('dve', lambda e: e.memset(wv[64:128, :, 0:64], 0.0), rd=[h_lc], wr=[h_lc])
            k.op('dve', lambda e: e.tensor_copy(out=wsT_b[:], in_=wv), rd=[h_lc], wr=[h_lc])
            for g in range(16):
                k.op('pe', lambda e, g=g: e.matmul(mps[:, 0:128], lhsT=ones1[:], rhs=wv[:, g, :], start=True, stop=True),
                     rd=[h_lc, h_c], wr=[h_mps])
                k.op('dve', lambda e, g=g: e.scalar_tensor_tensor(out=Cg[:, g, :], in0=mps[:, 0:128], scalar=ovec[:, 16 + g:17 + g],
                                                                   in1=bsbc[:, g * 128:(g + 1) * 128], op0=ALU.mult, op1=ALU.add),
                     rd=[h_mps, h_lc], wr=[h_lc])
            k.barrier()
            for tb in range(8):
                k.maybe_barrier()
                t0 = tb * 512
                k.dma('sp', hblk[:], h_src[t0:t0 + 512, :].rearrange("(i p) d -> p i d", p=128), wr=[h_hblk])
                for tile in range(4):
                    rms_to_T(hblk[:, tile, :], nbc[:], hnT, tile * 128, h_hblk, h_hnT, normbufs)
                for ch in range(32):
                    wb, hwb = load_w(owf_d[j, ch], 2048, ST, WC, h_st, h_wc)
                    pp, hp_ = (pa, h_pa) if ch % 2 == 0 else (pb, h_pb)
                    for dc in range(16):
                        k.op('pe', lambda e, dc=dc, wb=wb, pp=pp: e.matmul(pp[:], lhsT=wb[:, dc * 128:(dc + 1) * 128], rhs=hnT[:, dc, :],
                                                                           start=(dc == 0), stop=(dc == 15)), rd=[hwb, h_hnT], wr=[hp_])
                    if ch < 16:
                        k.op('act', lambda e, ch=ch, pp=pp: e.activation(out=usg[:, ch, :], in_=pp[:], func=AF.Gelu_apprx_tanh),
                             rd=[hp_], wr=[h_usg])
                    else:
                        k.op('act', lambda e, pp=pp: e.activation(out=tmpb[:], in_=pp[:], func=AF.Silu), rd=[hp_], wr=[h_tmpb])
                        k.op('dve', lambda e, ch=ch: e.tensor_tensor(out=usg[:, ch - 16, :], in0=usg[:, ch - 16, :], in1=tmpb[:], op=ALU.mult),
                             rd=[h_tmpb, h_usg], wr=[h_usg])
                for vb in range(4):
                    load_big(owv_d[j, vb], 8192, WB[0], h_wb[0], ST, h_st)
                    for tile in range(4):
                        pp, hp_ = (pa, h_pa) if tile % 2 == 0 else (pb, h_pb)
                        for dc in range(16):
                            k.op('pe', lambda e, dc=dc, tile=tile, pp=pp: e.matmul(
                                pp[:], lhsT=hnT[:, dc, tile * 128:(tile + 1) * 128], rhs=WB[0][:, dc * 512:(dc + 1) * 512],
                                start=(dc == 0), stop=(dc == 15)), rd=[h_wb[0], h_hnT], wr=[hp_])
                        k.op('act', lambda e, tile=tile, vb=vb, pp=pp: e.activation(out=gv[:, tile, vb * 512:(vb + 1) * 512], in_=pp[:],
                                                                                   func=AF.Gelu_apprx_tanh), rd=[hp_], wr=[h_gv])
                for tile in range(4):
                    for q4 in range(4):
                        k.op('dve', lambda e, q4=q4, tile=tile: e.bn_stats(out=st6[:, q4, :], in_=gv[:, tile, q4 * 512:(q4 + 1) * 512]),
                             rd=[h_gv], wr=[h_st6])
                    k.op('dve', lambda e: e.bn_aggr(out=mv[:], in_=st6[:].rearrange("p a b -> p (a b)")), rd=[h_st6], wr=[h_st6])
                    k.op('act', lambda e: e.activation(out=lsd[:], in_=mv[:, 1:2], func=AF.Sqrt, bias=epsc[:, 0:1]), rd=[h_st6, h_c], wr=[h_st6])
                    k.op('dve', lambda e: e.reciprocal(out=lrs[:], in_=lsd[:]), rd=[h_st6], wr=[h_st6])
                    k.op('dve', lambda e: e.scalar_tensor_tensor(out=nmr[:], in0=mv[:, 0:1], scalar=-1.0, in1=lrs[:], op0=ALU.mult, op1=ALU.mult),
                         rd=[h_st6], wr=[h_st6])
                    k.op('dve', lambda e, tile=tile: e.tensor_scalar(out=z[:], in0=gv[:, tile, :], scalar1=lrs[:, 0:1], scalar2=nmr[:, 0:1],
                                                                      op0=ALU.mult, op1=ALU.add), rd=[h_gv, h_st6], wr=[h_z])
                    for g in range(16):
                        k.op('pe', lambda e, g=g: e.matmul(mps[:, (g % 4) * 128:(g % 4 + 1) * 128], lhsT=z[:, g * 128:(g + 1) * 128],
                                                            rhs=wsT_b[:, g, :], start=True, stop=True), rd=[h_z, h_lc], wr=[h_mps])
                        k.op('dve', lambda e, g=g: e.scalar_tensor_tensor(out=mx[:], in0=mps[:, (g % 4) * 128:(g % 4 + 1) * 128],
                                                                           scalar=ovec[:, g:g + 1], in1=Cg[:, g, :], op0=ALU.mult, op1=ALU.add),
                             rd=[h_mps, h_lc], wr=[h_mx])
                        k.op('dve', lambda e, g=g, tile=tile: e.tensor_tensor(out=yT[:, g, tile * 128:(tile + 1) * 128], in0=mx[:],
                                                                              in1=usg[:, g, tile * 128:(tile + 1) * 128], op=ALU.mult),
                             rd=[h_mx, h_usg], wr=[h_yT])
                tail_block(layer, tb, hblk, h_hblk, yT, h_yT, 16, owo_d[j], tl)
                k.dma('sp', h_dst[t0:t0 + 512, :].rearrange("(i p) d -> p i d", p=128), hblk[:], rd=[h_hblk])
            k.barrier()


    def even_layer(layer, j, h_src, h_dst):
        CH = ([('xa', i) for i in range(16)] + [('ga', i) for i in range(16)] + [('q', i) for i in range(16)] +
              [('k', i) for i in range(4)] + [('gb', i) for i in range(16)] + [('iq', i) for i in range(8)] + [('ik', 0)])
        with ExitStack() as ls:
          if 'E1' in PH:
              sbl = lambda name, shape, dt: ls.enter_context(nc.sbuf_tensor(un(name), list(shape), dt))
              psl = lambda name, shape, dt: ls.enter_context(nc.psum_tensor(un(name), list(shape), dt))
              ST = [sbl(f"ST{i}", [128, 2048], F32) for i in range(2)]; h_st = k.hs(2)
              WB = [sbl("WB0", [128, 8192], BF16), sbl("WB1", [128, 2048], BF16)]; h_wb = k.hs(2)
              WC = [sbl(f"WC{i}", [128, 2048], BF16) for i in range(2)]; h_wc = k.hs(2)
              hnT = sbl("hnT", [128, 16, 1024], BF16); h_hnT = k.h()
              ht = [sbl(f"ht{i}", [128, 2048], F32) for i in range(2)]; h_ht = k.hs(2)
              ss = sbl("ss", [128, 1], F32); sd = sbl("sd", [128, 1], F32); rs = sbl("rs", [128, 1], F32)
              hp = sbl("hp", [128, 2048], BF16)
              tp = psl("tp", [128, 512], BF16)
              h_hp_ = k.h()
              normbufs = (hp, ss, sd, rs, hp, tp, h_hp_, h_hp_, k.h())
              pa = psl("pa", [128, 512], F32); pb = psl("pb", [128, 512], F32); h_pa, h_pb = k.h(), k.h()
              pm = psl("pm", [128, 512], F32); h_pm = k.h()
              nbc = sbl("nbc", [128, 2048], F32)
              qkn = sbl("qkn", [128, 2], F32); qg = sbl("qg", [128, 2], F32)
              of = [sbl(f"of{i}", [128, 512], F32) for i in range(2)]
              ob = [sbl(f"ob{i}", [128, 512], BF16) for i in range(2)]; h_o = k.hs(2)
              sq = sbl("sq", [128, 512], F32); rsd = sbl("rsd", [128, 512], F32); h_sq = k.h()
              vt = sbl("vt", [128, 4, 130], BF16); h_vt = k.h()
              iwt = sbl("iwt", [128, 16], F32); h_iwt = k.h()
              h_lc = k.h()
              k.dma('sp', nbc[:], ng_d[layer, :].partition_broadcast(128), wr=[h_c])
              k.dma('sp', qkn[:], qkn_d[j], wr=[h_lc])
              k.op('dve', lambda e: e.tensor_scalar(out=qg[:, 0:1], in0=qkn[:, 0:1], scalar1=float(128 ** -0.5), scalar2=None, op0=ALU.mult),
                   rd=[h_lc], wr=[h_lc])
              k.op('dve', lambda e: e.tensor_copy(out=qg[:, 1:2], in_=qkn[:, 1:2]), rd=[h_lc], wr=[h_lc])
              k.op('dve', lambda e: e.memset(vt[:], 1.0), wr=[h_vt])
              dst = {'xa': xa_d, 'ga': sga_d, 'gb': sgb_d, 'q': q_d, 'k': kk_d, 'iq': iq_d, 'ik': ik_d}
              oi = 0
              for blk in range(4):
                  k.maybe_barrier()
                  T0 = blk * 1024
                  for tile in range(8):
                      i = tile % 2
                      k.dma('sp', ht[i][:], h_src[T0 + tile * 128:T0 + (tile + 1) * 128, :], wr=[h_ht[i]])
                      rms_to_T(ht[i][:], nbc[:], hnT, tile * 128, h_ht[i], h_hnT, normbufs)
                  for ci, (kind, idx) in enumerate(CH):
                      wb, hwb = load_w(ewf_d[j, ci], 2048, ST, WC, h_st, h_wc)
                      for half in range(2):
                          pp, hp_ = (pa, h_pa) if half == 0 else (pb, h_pb)
                          for dc in range(16):
                              k.op('pe', lambda e, dc=dc, wb=wb, pp=pp, half=half: e.matmul(
                                  pp[:], lhsT=wb[:, dc * 128:(dc + 1) * 128], rhs=hnT[:, dc, half * 512:(half + 1) * 512],
                                  start=(dc == 0), stop=(dc == 15)), rd=[hwb, h_hnT], wr=[hp_])
                          o = oi % 2; oi += 1
                          rows = slice(idx * 128, (idx + 1) * 128)
                          cols = slice(T0 + half * 512, T0 + (half + 1) * 512)
                          if kind == 'xa':
                              k.op('act', lambda e, pp=pp, o=o: e.activation(out=of[o][:], in_=pp[:], func=AF.Copy), rd=[hp_], wr=[h_o[o]])
                              k.dma('act', xa_d[rows, cols], of[o][:], rd=[h_o[o]])
                          elif kind in ('ga', 'gb'):
                              k.op('act', lambda e, pp=pp, o=o: e.activation(out=ob[o][:], in_=pp[:], func=AF.Silu), rd=[hp_], wr=[h_o[o]])
                              k.dma('act', dst[kind][rows, cols], ob[o][:], rd=[h_o[o]])
                          elif kind in ('iq', 'ik'):
                              k.op('act', lambda e, pp=pp, o=o: e.activation(out=ob[o][:], in_=pp[:], func=AF.Copy), rd=[hp_], wr=[h_o[o]])
                              k.dma('act', dst[kind][rows, cols], ob[o][:], rd=[h_o[o]])
                          else:
                              gi = 0 if kind == 'q' else 1
                              k.op('act', lambda e, pp=pp: e.activation(out=sq[:], in_=pp[:], func=AF.Square), rd=[hp_], wr=[h_sq])
                              k.op('pe', lambda e: e.matmul(pm[:], lhsT=onesm[:], rhs=sq[:], start=True, stop=True), rd=[h_sq, h_c], wr=[h_pm])
                              k.op('act', lambda e: e.activation(out=rsd[:], in_=pm[:], func=AF.Sqrt, bias=epsc[:, 0:1]), rd=[h_pm, h_c], wr=[h_sq])
                              k.op('dve', lambda e: e.reciprocal(out=rsd[:], in_=rsd[:]), rd=[h_sq], wr=[h_sq])
                              k.op('dve', lambda e, pp=pp, o=o, gi=gi: e.scalar_tensor_tensor(
                                  out=ob[o][:], in0=pp[:], scalar=qg[:, gi:gi + 1], in1=rsd[:], op0=ALU.mult, op1=ALU.mult),
                                  rd=[hp_, h_sq, h_lc], wr=[h_o[o]])
                              k.dma('sp', dst[kind][rows, cols], ob[o][:], rd=[h_o[o]])
                  load_big(ewv_d[j], 8192, WB[0], h_wb[0], ST, h_st)
                  load_big(ewiw_d[j], 256, WB[1], h_wb[1], ST, h_st)
                  for tile in range(8):
                      for dc in range(16):
                          k.op('pe', lambda e, dc=dc, tile=tile: e.matmul(pa[:], lhsT=hnT[:, dc, tile * 128:(tile + 1) * 128],
                                                                        rhs=WB[0][:, dc * 512:(dc + 1) * 512], start=(dc == 0), stop=(dc == 15)),
                               rd=[h_wb[0], h_hnT], wr=[h_pa])
                      k.op('act', lambda e: e.activation(out=vt[:, :, 0:128], in_=pa[:].rearrange("p (g d) -> p g d", g=4), func=AF.Copy),
                           rd=[h_pa], wr=[h_vt])
                      k.dma('act', v_d[T0 + tile * 128:T0 + (tile + 1) * 128, :], vt[:].rearrange("p g d -> p (g d)"), rd=[h_vt])
                      for dc in range(16):
                          k.op('pe', lambda e, dc=dc, tile=tile: e.matmul(pb[:, 0:16], lhsT=hnT[:, dc, tile * 128:(tile + 1) * 128],
                                                                        rhs=WB[1][:, dc * 16:(dc + 1) * 16], start=(dc == 0), stop=(dc == 15)),
                               rd=[h_wb[1], h_hnT], wr=[h_pb])
                      k.op('act', lambda e: e.activation(out=iwt[:], in_=pb[:, 0:16], func=AF.Copy), rd=[h_pb], wr=[h_iwt])
                      k.dma('act', iw_d[T0 + tile * 128:T0 + (tile + 1) * 128, :], iwt[:], rd=[h_iwt])
              k.barrier()
        with ExitStack() as ls:
          if 'E2' in PH:
              sbl = lambda name, shape, dt: ls.enter_context(nc.sbuf_tensor(un(name), list(shape), dt))
              psl = lambda name, shape, dt: ls.enter_context(nc.psum_tensor(un(name), list(shape), dt))
              L = 1024
              evec = sbl("evec", [128, 8, 16], F32)
              kap = sbl("kap", [128, 16], F32); kap2 = sbl("kap2", [128, 16], F32); tmpk = sbl("tmpk", [128, 16], F32)
              wr_f = sbl("wr_f", [128, 2048], F32); wi_f = sbl("wi_f", [128, 2048], F32)
              wr_b = sbl("wr_b", [128, 16, 128], BF16); wi_b = sbl("wi_b", [128, 16, 128], BF16)
              h_lc = k.h()
              xin = [sbl(f"xin{i}", [128, 3 + L], F32) for i in range(2)]; h_xin = k.hs(2)
              xc = sbl("xc", [128, L], F32); h_xc = k.h()
              xcb = sbl("xcb", [128, L], BF16); h_xcb = k.h()
              rg = sbl("rg", [128, L], F32); ig = sbl("ig", [128, L], F32); h_rg, h_ig = k.h(), k.h()
              aa = sbl("aa", [128, L], F32); a2 = sbl("a2", [128, L], F32); h_aa, h_a2 = k.h(), k.h()
              bx = sbl("bx", [128, L], F32); h_bx = k.h()
              hsb = [sbl(f"hs{i}", [128, L], F32) for i in range(2)]; h_hs = k.hs(2)
              sga = [sbl(f"sga{i}", [128, L], BF16) for i in range(2)]; h_sga = k.hs(2)
              yb = [sbl(f"yb{i}", [128, L], BF16) for i in range(2)]; h_yb = k.hs(2)
              pa = psl("pa", [128, 512], F32); pb = psl("pb", [128, 512], F32); h_pa, h_pb = k.h(), k.h()
              k.dma('sp', evec[:].rearrange("p a g -> p (a g)"), evec_d[j], wr=[h_lc])
              k.dma('sp', wr_f[:], lwr_d[j], wr=[h_lc])
              k.dma('sp', wi_f[:], lwi_d[j], wr=[h_lc])
              k.op('dve', lambda e: e.tensor_copy(out=wr_b[:].rearrange("p g j -> p (g j)"), in_=wr_f[:]), rd=[h_lc], wr=[h_lc])
              k.op('dve', lambda e: e.tensor_copy(out=wi_b[:].rearrange("p g j -> p (g j)"), in_=wi_f[:]), rd=[h_lc], wr=[h_lc])
              k.op('act', lambda e: e.activation(out=tmpk[:], in_=evec[:, 7, :], func=AF.Exp, scale=-1.0), rd=[h_lc], wr=[h_lc])
              k.op('act', lambda e: e.activation(out=tmpk[:], in_=tmpk[:], func=AF.Ln, bias=onec[:, 0:1]), rd=[h_lc, h_c], wr=[h_lc])
              k.op('dve', lambda e: e.tensor_scalar(out=kap[:], in0=tmpk[:], scalar1=-8.0, scalar2=None, op0=ALU.mult), rd=[h_lc], wr=[h_lc])
              k.op('dve', lambda e: e.tensor_scalar(out=kap2[:], in0=tmpk[:], scalar1=-16.0, scalar2=None, op0=ALU.mult), rd=[h_lc], wr=[h_lc])
              it = 0
              for g in range(16):
                  k.maybe_barrier()
                  rows = slice(g * 128, (g + 1) * 128)
                  for pc in range(T // L):
                      i = it % 2; it += 1
                      t0 = pc * L
                      if pc == 0:
                          k.op('dve', lambda e, i=i: e.memset(xin[i][:, 0:3], 0.0), wr=[h_xin[i]])
                          k.dma('sp', xin[i][:, 3:3 + L], xa_d[rows, t0:t0 + L], wr=[h_xin[i]])
                      else:
                          k.dma('sp', xin[i][:, 0:3 + L], xa_d[rows, t0 - 3:t0 + L], wr=[h_xin[i]])
                      k.dma('sp', sga[i][:], sga_d[rows, t0:t0 + L], wr=[h_sga[i]])
                      k.op('dve', lambda e, i=i, g=g: e.tensor_scalar(out=xc[:], in0=xin[i][:, 3:3 + L], scalar1=evec[:, 3, g:g + 1],
                                                                      scalar2=evec[:, 4, g:g + 1], op0=ALU.mult, op1=ALU.add),
                           rd=[h_xin[i], h_lc], wr=[h_xc])
                      for tap in range(3):
                          k.op('dve', lambda e, i=i, g=g, tap=tap: e.scalar_tensor_tensor(
                              out=xc[:], in0=xin[i][:, tap:tap + L], scalar=evec[:, tap, g:g + 1], in1=xc[:], op0=ALU.mult, op1=ALU.add),
                              rd=[h_xin[i], h_lc, h_xc], wr=[h_xc])
                      k.op('act', lambda e: e.activation(out=xcb[:], in_=xc[:], func=AF.Copy), rd=[h_xc], wr=[h_xcb])
                      for half in range(L // 512):
                          sl = slice(half * 512, (half + 1) * 512)
                          k.op('pe', lambda e, g=g, sl=sl: e.matmul(pa[:], lhsT=wr_b[:, g, :], rhs=xcb[:, sl], start=True, stop=True),
                               rd=[h_xcb, h_lc], wr=[h_pa])
                          k.op('pe', lambda e, g=g, sl=sl: e.matmul(pb[:], lhsT=wi_b[:, g, :], rhs=xcb[:, sl], start=True, stop=True),
                               rd=[h_xcb, h_lc], wr=[h_pb])
                          k.op('act', lambda e, g=g, sl=sl: e.activation(out=rg[:, sl], in_=pa[:], func=AF.Sigmoid, bias=evec[:, 5, g:g + 1]),
                               rd=[h_pa, h_lc], wr=[h_rg])
                          k.op('act', lambda e, g=g, sl=sl: e.activation(out=ig[:, sl], in_=pb[:], func=AF.Sigmoid, bias=evec[:, 6, g:g + 1]),
                               rd=[h_pb, h_lc], wr=[h_ig])
                      k.op('act', lambda e, g=g: e.activation(out=aa[:], in_=rg[:], func=AF.Exp, scale=kap[:, g:g + 1]), rd=[h_rg, h_lc], wr=[h_aa])
                      k.op('act', lambda e, g=g: e.activation(out=a2[:], in_=rg[:], func=AF.Exp, scale=kap2[:, g:g + 1]), rd=[h_rg, h_lc], wr=[h_a2])
                      k.op('act', lambda e: e.activation(out=a2[:], in_=a2[:], func=AF.Sqrt, scale=-1.0, bias=onec[:, 0:1]), rd=[h_a2, h_c], wr=[h_a2])
                      k.op('pool', lambda e: e.tensor_tensor(out=bx[:], in0=a2[:], in1=ig[:], op=ALU.mult), rd=[h_a2, h_ig], wr=[h_bx])
                      k.op('pool', lambda e: e.tensor_tensor(out=bx[:], in0=bx[:], in1=xc[:], op=ALU.mult), rd=[h_bx, h_xc], wr=[h_bx])
                      if pc == 0:
                          k.op('dve', lambda e, i=i: e.tensor_tensor_scan(out=hsb[i][:], data0=aa[:], data1=bx[:], initial=0.0,
                                                                          op0=ALU.mult, op1=ALU.add), rd=[h_aa, h_bx], wr=[h_hs[i]])
                      else:
                          k.op('dve', lambda e, i=i: e.tensor_tensor_scan(out=hsb[i][:], data0=aa[:], data1=bx[:], initial=hsb[1 - i][:, L - 1:L],
                                                                          op0=ALU.mult, op1=ALU.add), rd=[h_aa, h_bx, h_hs[1 - i]], wr=[h_hs[i]])
                      k.op('pool', lambda e, i=i: e.tensor_tensor(out=yb[i][:], in0=hsb[i][:], in1=sga[i][:], op=ALU.mult),
                           rd=[h_hs[i], h_sga[i]], wr=[h_yb[i]])
                      k.dma('sp', yT_d[rows, t0:t0 + L], yb[i][:], rd=[h_yb[i]])
              k.barrier()
        with ExitStack() as ls:
          if 'E3' in PH:
              sbl = lambda name, shape, dt: ls.enter_context(nc.sbuf_tensor(un(name), list(shape), dt))
              psl = lambda name, shape, dt: ls.enter_context(nc.psum_tensor(un(name), list(shape), dt))
              kT = sbl("kT", [128, 4, T], BF16); vA = sbl("vA", [128, 32, 4 * 130], BF16); ik2 = sbl("ik2", [128, T], BF16)
              h_kv = k.h()
              for g in range(4):
                  k.dma('sp', kT[:, g, :], kk_d[g * 128:(g + 1) * 128, :], wr=[h_kv])
              for c in range(4):
                  k.dma('sp', vA[:, c * 8:(c + 1) * 8, :], v_d[c * 1024:(c + 1) * 1024, :].rearrange("(i p) d -> p i d", p=128), wr=[h_kv])
              k.dma('sp', ik2[:], ik_d, wr=[h_kv])
              qt = [sbl(f"qt{i}", [128, 16, 128], BF16) for i in range(2)]
              iqt = [sbl(f"iqt{i}", [128, 8, 128], BF16) for i in range(2)]
              iwt = [sbl(f"iwt{i}", [128, 16], F32) for i in range(2)]
              sgbt = [sbl(f"sgbt{i}", [128, 16, 128], BF16) for i in range(2)]
              h_ld = k.hs(2)
              dg = sbl("dg", [128, 16, 128], BF16); h_dg = k.h()
              Rb = [sbl(f"Rb{i}", [128, 512], BF16) for i in range(2)]; h_R = k.hs(2)
              acc = sbl("acc", [128, T], F32); h_acc = k.h()
              cj = sbl("cj", [128, T], BF16); h_cj = k.h()
              am = sbl("am", [128, 1], F32); lo = sbl("lo", [128, 1], F32); w0 = sbl("w0", [128, 1], F32)
              mid = sbl("mid", [128, 1], F32); cnt = sbl("cnt", [128, 1], F32); t1 = sbl("t1", [128, 1], F32); h_bs = k.h()
              mask = sbl("mask", [128, T], BF16); h_mask = k.h()
              mT4 = sbl("mT4", [128, 32, 4, 128], BF16); h_mT = k.h()
              Mn = sbl("Mn", [128, 2, 16, 128], BF16); h_Mn = k.h()
              Eb = [sbl(f"Eb{i}", [128, 512], BF16) for i in range(2)]; h_E = k.hs(2)
              Pb = [sbl(f"Pb{i}", [128, 4, 128], BF16) for i in range(2)]; h_P = k.hs(2)
              ri = sbl("ri", [128, 4], F32); h_ri = k.h()
              ot = sbl("ot", [128, 16, 128], BF16); h_ot = k.h()
              yt = [sbl(f"yt{i}", [128, 16, 128], BF16) for i in range(2)]; h_yt = k.hs(2)
              pA = psl("pA", [128, 512], F32); pB = psl("pB", [128, 512], F32); pC = psl("pC", [128, 512], F32)
              h_pA, h_pB, h_pC = k.h(), k.h(), k.h()
              mtp = psl("mtp", [128, 512], BF16); h_mtp = k.h()
              oacc = [psl(f"oacc{i}", [128, 512], F32) for i in range(4)]; h_oa = k.hs(4)
              EBv = EB[:].rearrange("s (k h t) -> s k h t", k=2, h=16)
              zi = 0
              for n in range(NT):
                  k.maybe_barrier()
                  b = n % 2
                  S = 128 * (n + 1)
                  c0 = n * 128
                  k.dma('sp', qt[b][:], q_d[:, c0:c0 + 128].rearrange("(h p) t -> p h t", p=128), wr=[h_ld[b]])
                  k.dma('sp', iqt[b][:], iq_d[:, c0:c0 + 128].rearrange("(h p) t -> p h t", p=128), wr=[h_ld[b]])
                  k.dma('sp', iwt[b][:], iw_d[c0:c0 + 128, :], wr=[h_ld[b]])
                  k.dma('sp', sgbt[b][:], sgb_d[:, c0:c0 + 128].rearrange("(h p) t -> p h t", p=128), wr=[h_ld[b]])
                  for h in range(16):
                      k.op('pool', lambda e, h=h, b=b: e.tensor_scalar(out=dg[:, h, :], in0=ident_f[:], scalar1=iwt[b][:, h:h + 1], scalar2=0.0,
                                                                       op0=ALU.mult, op1=ALU.add), rd=[h_ld[b], h_c], wr=[h_dg])
                  nkb = (S + 511) // 512
                  for kb in range(nkb):
                      w = min(512, S - kb * 512)
                      for h in range(16):
                          c = h // 2; p0 = 64 * (h % 2)
                          zp, hz = (pA, h_pA) if zi % 2 == 0 else (pB, h_pB)
                          R, hR = Rb[zi % 2], h_R[zi % 2]
                          zi += 1
                          k.op('pe', lambda e, c=c, p0=p0, zp=zp, kb=kb, w=w, b=b: e.matmul(
                              zp[:, 0:w], lhsT=iqt[b][p0:p0 + 64, c, :], rhs=ik2[p0:p0 + 64, kb * 512:kb * 512 + w], start=True, stop=True),
                              rd=[h_ld[b], h_kv], wr=[hz])
                          k.op('act', lambda e, zp=zp, R=R, w=w: e.activation(out=R[:, 0:w], in_=zp[:, 0:w], func=AF.Relu), rd=[hz], wr=[hR])
                          k.op('pe', lambda e, h=h, R=R, w=w: e.matmul(pC[:, 0:w], lhsT=dg[:, h, :], rhs=R[:, 0:w], start=(h == 0), stop=(h == 15)),
                               rd=[h_dg, hR], wr=[h_pC])
                      k.op('dve', lambda e, kb=kb, w=w: e.tensor_copy(out=acc[:, kb * 512:kb * 512 + w], in_=pC[:, 0:w]), rd=[h_pC], wr=[h_acc])
                  k.op('dve', lambda e, S=S: e.tensor_reduce(out=am[:], in_=acc[:, 0:S], axis=AX.X, op=ALU.max, apply_absolute_value=True),
                       rd=[h_acc], wr=[h_bs])
                  k.op('dve', lambda e: e.tensor_scalar(out=lo[:], in0=am[:], scalar1=-1.001, scalar2=-1e-20, op0=ALU.mult, op1=ALU.add), rd=[h_bs], wr=[h_bs])
                  k.op('dve', lambda e: e.tensor_scalar(out=w0[:], in0=am[:], scalar1=2.002, scalar2=2e-20, op0=ALU.mult, op1=ALU.add), rd=[h_bs], wr=[h_bs])
                  k.op('dve', lambda e, S=S: e.tensor_tensor(out=acc[:, S - 128:S], in0=acc[:, S - 128:S], in1=diagneg[:], op=ALU.add),
                       rd=[h_acc, h_c], wr=[h_acc])
                  if S > 256:
                      for itn in range(1, NIT + 1):
                          f = 2.0 ** (-itn)
                          k.op('dve', lambda e, f=f: e.scalar_tensor_tensor(out=mid[:], in0=w0[:], scalar=f, in1=lo[:], op0=ALU.mult, op1=ALU.add),
                               rd=[h_bs], wr=[h_bs])
                          k.op('dve', lambda e, S=S: e.tensor_scalar(out=cj[:, 0:S], in0=acc[:, 0:S], scalar1=mid[:, 0:1], scalar2=None,
                                                                     op0=ALU.is_ge, op1=ALU.add, accum_out=cnt[:, 0:1]), rd=[h_acc, h_bs], wr=[h_cj, h_bs])
                          k.op('dve', lambda e, f=f: e.tensor_scalar(out=t1[:], in0=cnt[:], scalar1=255.5, scalar2=f, op0=ALU.is_ge, op1=ALU.mult),
                               rd=[h_bs], wr=[h_bs])
                          k.op('dve', lambda e: e.scalar_tensor_tensor(out=lo[:], in0=t1[:], scalar=w0[:, 0:1], in1=lo[:], op0=ALU.mult, op1=ALU.add),
                               rd=[h_bs], wr=[h_bs])
                  k.op('dve', lambda e, S=S: e.tensor_scalar(out=mask[:, 0:S], in0=acc[:, 0:S], scalar1=lo[:, 0:1], scalar2=None, op0=ALU.is_ge),
                       rd=[h_acc, h_bs], wr=[h_mask])
                  for k0 in range(0, n + 1, 4):
                      nk = min(4, n + 1 - k0)
                      for q in range(nk):
                          k.op('pe', lambda e, q=q, k0=k0: e.transpose(out=mtp[:, q * 128:(q + 1) * 128], in_=mask[:, (k0 + q) * 128:(k0 + q + 1) * 128],
                                                                       identity=ident_b[:]), rd=[h_mask, h_c], wr=[h_mtp])
                      for r in range(4):
                          eng = 'act' if r % 2 == 0 else 'dve'
                          if eng == 'act':
                              k.op('act', lambda e, r=r, k0=k0, nk=nk: e.activation(out=mT4[:, k0:k0 + nk, r, :],
                                                                                   in_=mtp[:, 0:nk * 128].rearrange("p (q t) -> p q t", q=nk), func=AF.Copy),
                                   rd=[h_mtp], wr=[h_mT])
                          else:
                              k.op('dve', lambda e, r=r, k0=k0, nk=nk: e.tensor_copy(out=mT4[:, k0:k0 + nk, r, :],
                                                                                    in_=mtp[:, 0:nk * 128].rearrange("p (q t) -> p q t", q=nk)),
                                   rd=[h_mtp], wr=[h_mT])
                  for kind in range(2):
                      if n - kind < 0:
                          continue
                      for g4 in range(4):
                          k.op('pool', lambda e, kind=kind, g4=g4, n=n: e.tensor_tensor(out=Mn[:, kind, g4 * 4:(g4 + 1) * 4, :],
                                                                                        in0=EBv[:, kind, g4 * 4:(g4 + 1) * 4, :],
                                                                                        in1=mT4[:, n - kind, :, :], op=ALU.mult),
                               rd=[h_mT, h_c], wr=[h_Mn])
                  si = 0
                  for g in range(4):
                      for kt in range(n + 1):
                          sp_, hs_ = (pA, h_pA) if si % 2 == 0 else (pB, h_pB)
                          E_, hE = Eb[si % 2], h_E[si % 2]
                          P_, hP = Pb[si % 2], h_P[si % 2]
                          si += 1
                          k.op('pe', lambda e, g=g, kt=kt, sp_=sp_, b=b: e.matmul(sp_[:], lhsT=kT[:, g, kt * 128:(kt + 1) * 128],
                                                                                 rhs=qt[b][:, g * 4:(g + 1) * 4, :], start=True, stop=True),
                               rd=[h_kv, h_ld[b]], wr=[hs_])
                          k.op('act', lambda e, sp_=sp_, E_=E_: e.activation(out=E_[:], in_=sp_[:], func=AF.Exp), rd=[hs_], wr=[hE])
                          kind = n - kt
                          if kind < 2:
                              k.op('dve', lambda e, E_=E_, P_=P_, kind=kind, g=g: e.tensor_tensor(
                                  out=P_[:], in0=E_[:].rearrange("p (r t) -> p r t", r=4), in1=Mn[:, kind, g * 4:(g + 1) * 4, :], op=ALU.mult),
                                  rd=[hE, h_Mn], wr=[hP])
                          else:
                              k.op('dve', lambda e, E_=E_, P_=P_, kt=kt: e.tensor_tensor(
                                  out=P_[:], in0=E_[:].rearrange("p (r t) -> p r t", r=4), in1=mT4[:, kt, :, :], op=ALU.mult),
                                  rd=[hE, h_mT], wr=[hP])
                          for r in range(4):
                              k.op('pe', lambda e, r=r, P_=P_, kt=kt, g=g, n=n: e.matmul(
                                  oacc[r][:, 0:129], lhsT=P_[:, r, :], rhs=vA[:, kt, g * 130:g * 130 + 129], start=(kt == 0), stop=(kt == n)),
                                  rd=[hP, h_kv], wr=[h_oa[r]])
                      for r in range(4):
                          k.op('dve', lambda e, r=r: e.reciprocal(out=ri[:, r:r + 1], in_=oacc[r][:, 128:129]), rd=[h_oa[r]], wr=[h_ri])
                          k.op('dve', lambda e, r=r, g=g: e.tensor_scalar(out=ot[:, g * 4 + r, :], in0=oacc[r][:, 0:128], scalar1=ri[:, r:r + 1],
                                                                          scalar2=None, op0=ALU.mult), rd=[h_oa[r], h_ri], wr=[h_ot])
                  for g4 in range(4):
                      for r in range(4):
                          k.op('pe', lambda e, g4=g4, r=r: e.transpose(out=mtp[:, r * 128:(r + 1) * 128], in_=ot[:, g4 * 4 + r, :], identity=ident_b[:]),
                               rd=[h_ot, h_c], wr=[h_mtp])
                      k.op('dve', lambda e, g4=g4, b=b: e.tensor_tensor(out=yt[b][:, g4 * 4:(g4 + 1) * 4, :], in0=mtp[:].rearrange("p (r t) -> p r t", r=4),
                                                                        in1=sgbt[b][:, g4 * 4:(g4 + 1) * 4, :], op=ALU.mult),
                           rd=[h_mtp, h_ld[b]], wr=[h_yt[b]])
                  k.dma('sp', yT_d[2048:4096, c0:c0 + 128].rearrange("(h p) t -> p h t", p=128), yt[b][:], rd=[h_yt[b]])
              k.barrier()
        with ExitStack() as ls:
          if 'E4' in PH:
              sbl = lambda name, shape, dt: ls.enter_context(nc.sbuf_tensor(un(name), list(shape), dt))
              tl = alloc_tail(ls, 32)
              gbc = tl[21]
              hblk = sbl("hblk", [128, 4, 2048], F32); h_hblk = k.h()
              yT = sbl("yTs", [128, 32, 512], BF16); h_yT = k.h()
              h_lc = k.h()
              k.dma('sp', gbc[:], pgn_d[layer, :].partition_broadcast(128), wr=[h_c])
              for tb in range(8):
                  k.maybe_barrier()
                  t0 = tb * 512
                  k.dma('sp', hblk[:], h_src[t0:t0 + 512, :].rearrange("(i p) d -> p i d", p=128), wr=[h_hblk])
                  for c in range(4):
                      k.dma('sp', yT[:, c * 8:(c + 1) * 8, :], yT_d[c * 1024:(c + 1) * 1024, t0:t0 + 512].rearrange("(c p) t -> p c t", p=128), wr=[h_yT])
                  tail_block(layer, tb, hblk, h_hblk, yT, h_yT, 32, ewo_d[j], tl)
                  k.dma('sp', h_dst[t0:t0 + 512, :].rearrange("(i p) d -> p i d", p=128), hblk[:], rd=[h_hblk])
              k.barrier()

    for layer in layer_list:
        j = layer // 2
        h_src = x_d if layer == layer_list[0] else hbuf_d
        h_dst = y_d if layer == layer_list[-1] else hbuf_d
        if layer % 2 == 0:
            even_layer(layer, j, h_src, h_dst)
        else:
            odd_layer(layer, j, h_src, h_dst)

    k.barrier()
    es.close()
    return nc


def _prep_inputs(inp):
    f = lambda a: np.ascontiguousarray(np.asarray(a, dtype=np.float32))
    W = {kk: f(v) for kk, v in inp.items()}
    rng = np.arange
    shared = {}
    shared["norm_gain"] = W["norm_gain"]
    shared["pe_gate_norm"] = W["pe_gate_norm"]
    shared["consts"] = np.eye(128, dtype=np.float32)
    s_ = rng(128)[:, None]
    t_ = rng(128)[None, :]
    oh = np.zeros((32, 2, 128, 128), np.float32)
    for kind in range(2):
        bk = t5_bucket_np((s_ - t_ - 128 * kind).astype(np.int32))
        for b in range(32):
            oh[b, kind] = (bk == b).astype(np.float32)
    oh[15] -= 1.0
    shared["oh"] = oh.reshape(32, -1)
    shared["rel_bias"] = W["rel_bias"]
    cidx = np.concatenate([rng(0, 2048), rng(2048, 4096), rng(4096, 6144), rng(6144, 6656), rng(7168, 9216),
                           rng(9216, 10240), rng(10240, 10304), rng(10240, 10304)])
    ewf = np.empty((2, 77, 128, 2048), np.float32)
    ewv = np.empty((2, 128, 8192), np.float32)
    ewiw = np.empty((2, 128, 256), np.float32)
    ewo = np.empty((2, 4, 128, 32 * 512), np.float32)
    lwr = np.empty((2, 128, 2048), np.float32)
    lwi = np.empty((2, 128, 2048), np.float32)
    evec = np.empty((2, 128, 8, 16), np.float32)
    qkn = np.empty((2, 128, 2), np.float32)
    owf = np.empty((2, 32, 128, 2048), np.float32)
    owv = np.empty((2, 4, 128, 8192), np.float32)
    owo = np.empty((2, 4, 128, 8192), np.float32)
    wst = np.empty((2, 128, 2048), np.float32)
    ovec = np.empty((2, 128, 32), np.float32)
    for j in range(2):
        Wi = W["even_w_in"][j]
        ewf[j] = Wi[:, cidx].reshape(16, 128, 77, 128).transpose(2, 1, 0, 3).reshape(77, 128, 2048)
        ewv[j] = Wi[:, 6656:7168].reshape(16, 128, 512).transpose(1, 0, 2).reshape(128, 8192)
        ewiw[j] = Wi[:, 10304:10320].reshape(16, 128, 16).transpose(1, 0, 2).reshape(128, 256)
        ewo[j] = W["even_w_out"][j].reshape(32, 128, 4, 512).transpose(2, 1, 0, 3).reshape(4, 128, 32 * 512)
        lwr[j] = W["lru_w_r"][j].transpose(1, 0, 2).reshape(128, 2048)
        lwi[j] = W["lru_w_i"][j].transpose(1, 0, 2).reshape(128, 2048)
        pc = lambda v: v.reshape(16, 128).T
        for a in range(4):
            evec[j, :, a, :] = pc(W["conv_w"][j, a])
        evec[j, :, 4, :] = pc(W["conv_b"][j])
        evec[j, :, 5, :] = pc(W["lru_b_r"][j].reshape(-1))
        evec[j, :, 6, :] = pc(W["lru_b_i"][j].reshape(-1))
        evec[j, :, 7, :] = pc(W["lru_lambda"][j])
        qkn[j, :, 0] = W["q_norm"][j]
        qkn[j, :, 1] = W["k_norm"][j]
        Wo = W["odd_w_in"][j]
        oidx = np.concatenate([rng(0, 2048), rng(4096, 6144)])
        owf[j] = Wo[:, oidx].reshape(16, 128, 32, 128).transpose(2, 1, 0, 3).reshape(32, 128, 2048)
        owv[j] = Wo[:, 2048:4096].reshape(16, 128, 4, 512).transpose(2, 1, 0, 3).reshape(4, 128, 8192)
        owo[j] = W["odd_w_out"][j].reshape(16, 128, 4, 512).transpose(2, 1, 0, 3).reshape(4, 128, 8192)
        wst[j] = W["sg_w_s"][j].transpose(2, 0, 1).reshape(128, 2048)
        ovec[j, :, 0:16] = pc(W["sg_ln_g"][j])
        ovec[j, :, 16:32] = pc(W["sg_ln_b"][j])
    shared.update(ewf=ewf, ewv=ewv, ewiw=ewiw, ewo=ewo, lwr=lwr, lwi=lwi, evec=evec.reshape(2, 128, 128), qkn=qkn,
                  owf=owf, owv=owv, owo=owo, wst=wst, ovec=ovec, bs=W["sg_b_s"])
    shared["pew"] = np.ascontiguousarray(W["pe_w"].reshape(4, 2, 128, 4, 512).transpose(0, 3, 2, 1, 4).reshape(4, 4, 128, 1024))
    shared["pwg"] = np.ascontiguousarray(W["pe_w_gate"].reshape(4, 16, 128, 4, 512).transpose(0, 3, 2, 1, 4).reshape(4, 4, 128, 8192))
    shared = {kk: np.ascontiguousarray(v, dtype=np.float32) for kk, v in shared.items()}
    maps = []
    for b in range(2):
        m = dict(shared)
        m["x"] = np.ascontiguousarray(W["x"][b])
        m["p"] = np.ascontiguousarray(W["p"][:, b])
        maps.append(m)
    return maps


def kernel(**inputs):
    ll = os.environ.get("KLAYERS")
    layer_list = [int(v) for v in ll.split(",")] if ll else [0, 1, 2, 3]
    maps = _prep_inputs(inputs)
    nc = build_program(layer_list)
    res = run_bass_kernel_spmd(nc, maps, core_ids=[0, 1], trace=bool(os.environ.get('KTRACE')))
    if os.environ.get('KTRACE'):
        print('EXEC_TIME_NS', res.exec_time_ns, flush=True)
    out = np.stack([np.asarray(res.results[b]["y"], dtype=np.float32) for b in range(2)], axis=0)
    if os.environ.get("KDEBUG"):
        global LAST_RESULTS
        LAST_RESULTS = res.results
    return out
```

```python
import os
from contextlib import ExitStack
import numpy as np
import concourse.bass as bass
import concourse.mybir as mybir
from concourse.bass_utils import run_bass_kernel_spmd

F32 = mybir.dt.float32
BF16 = mybir.dt.bfloat16
AF = mybir.ActivationFunctionType
ALU = mybir.AluOpType
AX = mybir.AxisListType

T = 4096
D = 2048
NT = T // 128
EPS = 1e-6
NEG = -1.0e30
NIT = 16
NDS = 12
EPOCH_LIMIT = 9000
SAME = {'pe': False, 'act': True, 'dve': True, 'pool': True, 'sp': False}


class H:
    __slots__ = ('w', 'r')

    def __init__(self):
        self.w = None
        self.r = {}


class K:
    def __init__(self, nc, es):
        self.nc = nc
        self.E = {'pe': nc.tensor, 'act': nc.scalar, 'dve': nc.vector, 'pool': nc.gpsimd, 'sp': nc.sync}
        self.sems = [{k: es.enter_context(nc.semaphore(f"e{s}_{k}")) for k in self.E} for s in range(3)]
        self.dsems = [[es.enter_context(nc.semaphore(f"d{s}_{i}")) for i in range(NDS)] for s in range(3)]
        self.ep = 0
        self.cnt = {k: 0 for k in self.E}
        self.dval = [0] * NDS
        self.dnext = 0
        self.seen = {k: {} for k in self.E}
        self.handles = []

    def h(self):
        x = H()
        self.handles.append(x)
        return x

    def hs(self, n):
        return [self.h() for _ in range(n)]

    def _wait(self, e, key, val):
        if key[0] == 'e' and key[1] == e and not SAME[e]:
            return
        if self.seen[e].get(key, 0) >= val:
            return
        s = self.ep % 3
        sem = self.sems[s][key[1]] if key[0] == 'e' else self.dsems[s][key[1]]
        self.E[e].wait_ge(sem, val)
        self.seen[e][key] = val

    def _deps(self, e, rd, wr):
        d = {}
        for h in rd:
            if h.w is not None:
                d[h.w[0]] = max(d.get(h.w[0], 0), h.w[1])
        for h in wr:
            if h.w is not None:
                d[h.w[0]] = max(d.get(h.w[0], 0), h.w[1])
            for kk, v in h.r.items():
                d[kk] = max(d.get(kk, 0), v)
        for kk, v in d.items():
            self._wait(e, kk, v)

    def _mark(self, tok, rd, wr):
        for h in wr:
            h.w = tok
            h.r = {}
        for h in rd:
            if h not in wr:
                h.r[tok[0]] = max(h.r.get(tok[0], 0), tok[1])

    def op(self, e, fn, rd=(), wr=()):
        self._deps(e, rd, wr)
        ins = fn(self.E[e])
        self.cnt[e] += 1
        ins.then_inc(self.sems[self.ep % 3][e], 1)
        self._mark((('e', e), self.cnt[e]), rd, wr)

    def dma(self, q, out, in_, rd=(), wr=()):
        i = self.dnext
        self.dnext = (i + 1) % NDS
        if self.dval[i] > 0:
            self._wait(q, ('d', i), self.dval[i])
        self._deps(q, rd, wr)
        ins = self.E[q].dma_start(out=out, in_=in_)
        self.dval[i] += 16
        ins.then_inc(self.dsems[self.ep % 3][i], 16)
        self._mark((('d', i), self.dval[i]), rd, wr)

    def barrier(self):
        for e in self.E:
            for o in self.E:
                if o != e and self.cnt[o] > 0:
                    self._wait(e, ('e', o), self.cnt[o])
            for i in range(NDS):
                if self.dval[i] > 0:
                    self._wait(e, ('d', i), self.dval[i])
        self.ep += 1
        self.cnt = {k: 0 for k in self.E}
        self.dval = [0] * NDS
        self.seen = {k: {} for k in self.E}
        for h in self.handles:
            h.w = None
            h.r = {}
        if self.ep >= 2:
            o = (self.ep + 1) % 3
            for e in ('pe', 'act', 'dve', 'pool', 'sp'):
                self.op(e, lambda eng, e=e: eng.sem_clear(self.sems[o][e]))
            for i in range(NDS):
                self.op('sp', lambda eng, i=i: eng.sem_clear(self.dsems[o][i]))

    def maybe_barrier(self):
        if max(self.cnt.values()) > EPOCH_LIMIT or max(self.dval) > EPOCH_LIMIT * 16:
            self.barrier()


_UN = [0]


def un(name):
    _UN[0] += 1
    return f"{name}_u{_UN[0]}"


def t5_bucket_np(rel):
    half = 16
    max_exact = 8
    ret = np.where(rel > 0, half, 0)
    n = np.abs(rel)
    nf = np.maximum(n, 1).astype(np.float32)
    large = max_exact + (np.log(nf / max_exact) / np.float32(np.log(128 / max_exact)) * (half - max_exact)).astype(np.int32)
    large = np.minimum(large, half - 1)
    return ret + np.where(n < max_exact, n, large)


def build_program(layer_list):
    nc = bass.Bass("TRN2", target_bir_lowering=False)
    es = ExitStack()
    k = K(nc, es)

    def din(name, shape, dt=F32):
        return nc.dram_tensor(name, list(shape), dt, kind="ExternalInput").ap()

    DBG = bool(os.environ.get("KDEBUG"))
    PH = os.environ.get("KPHASES", "E1,E2,E3,E4").split(",")

    def dscr(name, shape, dt):
        return nc.dram_tensor(name, list(shape), dt, kind=("ExternalOutput" if DBG else "Internal")).ap()

    x_d = din("x", [T, D])
    p_d = din("p", [4, T, 256])
    ng_d = din("norm_gain", [4, D])
    pgn_d = din("pe_gate_norm", [4, D])
    consts_d = din("consts", [128, 128])
    oh_d = din("oh", [32, 2 * 128 * 128])
    rb_d = din("rel_bias", [32, 16])
    ewf_d = din("ewf", [2, 77, 128, 16 * 128])
    ewv_d = din("ewv", [2, 128, 16 * 512])
    ewiw_d = din("ewiw", [2, 128, 16 * 16])
    ewo_d = din("ewo", [2, 4, 128, 32 * 512])
    lwr_d = din("lwr", [2, 128, 16 * 128])
    lwi_d = din("lwi", [2, 128, 16 * 128])
    evec_d = din("evec", [2, 128, 8 * 16])
    qkn_d = din("qkn", [2, 128, 2])
    owf_d = din("owf", [2, 32, 128, 16 * 128])
    owv_d = din("owv", [2, 4, 128, 16 * 512])
    owo_d = din("owo", [2, 4, 128, 16 * 512])
    wst_d = din("wst", [2, 128, 16 * 128])
    ovec_d = din("ovec", [2, 128, 2 * 16])
    bs_d = din("bs", [2, 16, 128])
    pew_d = din("pew", [4, 4, 128, 2 * 512])
    pwg_d = din("pwg", [4, 4, 128, 16 * 512])
    y_d = nc.dram_tensor("y", [T, D], F32, kind="ExternalOutput").ap()

    hbuf_d = dscr("hbuf", [T, D], F32)
    xa_d = dscr("xaT", [2048, T], F32)
    sga_d = dscr("sgaT", [2048, T], BF16)
    sgb_d = dscr("sgbT", [2048, T], BF16)
    q_d = dscr("qT", [2048, T], BF16)
    kk_d = dscr("kT", [512, T], BF16)
    iq_d = dscr("iqT", [1024, T], BF16)
    ik_d = dscr("ikT", [128, T], BF16)
    v_d = dscr("vaug", [T, 4 * 130], BF16)
    iw_d = dscr("iw", [T, 16], F32)
    yT_d = dscr("yT", [4096, T], BF16)

    sb = lambda name, shape, dt: es.enter_context(nc.sbuf_tensor(un(name), list(shape), dt))
    ps = lambda name, shape, dt: es.enter_context(nc.psum_tensor(un(name), list(shape), dt))

    ident_f = sb("ident_f", [128, 128], F32); h_ident = k.h()
    ident_b = sb("ident_b", [128, 128], BF16)
    onesm = sb("onesm", [128, 128], F32)
    ones1 = sb("ones1", [128, 128], F32)
    diagneg = sb("diagneg", [128, 128], F32)
    epsc = sb("epsc", [128, 1], F32)
    onec = sb("onec", [128, 1], F32)
    EB = sb("EB", [128, 2 * 16 * 128], BF16)
    h_c = k.h()

    k.dma('sp', ident_f[:], consts_d, wr=[h_c])
    k.op('dve', lambda e: e.tensor_copy(out=ident_b[:], in_=ident_f[:]), rd=[h_c], wr=[h_c])
    k.op('dve', lambda e: e.memset(onesm[:], 1.0 / 128), wr=[h_c])
    k.op('dve', lambda e: e.memset(ones1[:], 1.0), wr=[h_c])
    k.op('dve', lambda e: e.memset(diagneg[:], 0.0), wr=[h_c])
    k.op('dve', lambda e: e.memset(diagneg[0:64, 64:128], NEG), wr=[h_c])
    k.op('dve', lambda e: e.memset(epsc[:], EPS), wr=[h_c])
    k.op('dve', lambda e: e.memset(onec[:], 1.0), wr=[h_c])

    with nc.sbuf_tensor("oh_s0", [32, 2 * 128 * 128], F32) as oh_s, nc.sbuf_tensor("rb_s0", [32, 16], F32) as rb_s, \
            nc.psum_tensor("bps", [128, 512], F32) as bps:
        h_oh, h_bps = k.h(), k.h()
        k.dma('sp', oh_s[:], oh_d, wr=[h_oh])
        k.dma('sp', rb_s[:], rb_d, wr=[h_oh])
        EBv = EB[:].rearrange("s (k h t) -> s k h t", k=2, h=16)
        ohv = oh_s[:].rearrange("b (k s t) -> b k s t", k=2, s=128)
        for kind in range(2):
            for t0 in range(0, 128, 32):
                for tt in range(32):
                    k.op('pe', lambda e, kind=kind, t=t0 + tt, tt=tt: e.matmul(
                        bps[:, tt * 16:(tt + 1) * 16], lhsT=ohv[:, kind, :, t], rhs=rb_s[:], start=True, stop=True),
                        rd=[h_oh], wr=[h_bps])
                k.op('act', lambda e, kind=kind, t0=t0: e.activation(
                    out=EBv[:, kind, :, t0:t0 + 32].rearrange("s h t -> s t h"),
                    in_=bps[:].rearrange("s (t h) -> s t h", h=16), func=AF.Exp), rd=[h_bps], wr=[h_c])
        k.barrier()

    def rms_to_T(htile, gain_bc, dstT, col0, h_in, h_dst, bufs):
        junk, ss, sd, rs, hp, tp, h_j, h_hp, h_tp = bufs
        k.op('act', lambda e: e.activation(out=junk[:], in_=htile, func=AF.Square, accum_out=ss[:, 0:1]),
             rd=[h_in], wr=[h_j])
        k.op('act', lambda e: e.activation(out=sd[:, 0:1], in_=ss[:, 0:1], func=AF.Sqrt, scale=1.0 / D, bias=epsc[:, 0:1]),
             rd=[h_j, h_c], wr=[h_j])
        k.op('dve', lambda e: e.reciprocal(out=rs[:, 0:1], in_=sd[:, 0:1]), rd=[h_j], wr=[h_j])
        k.op('dve', lambda e: e.scalar_tensor_tensor(out=hp[:], in0=htile, scalar=rs[:, 0:1], in1=gain_bc,
                                                      op0=ALU.mult, op1=ALU.mult), rd=[h_in, h_j, h_c], wr=[h_hp])
        for g4 in range(4):
            for j in range(4):
                dc = g4 * 4 + j
                k.op('pe', lambda e, dc=dc, j=j: e.transpose(out=tp[:, j * 128:(j + 1) * 128],
                                                              in_=hp[:, dc * 128:(dc + 1) * 128], identity=ident_b[:]),
                     rd=[h_hp, h_c], wr=[h_tp])
            k.op('act', lambda e, g4=g4: e.activation(out=dstT[:, g4 * 4:(g4 + 1) * 4, col0:col0 + 128],
                                                       in_=tp[:].rearrange("p (j t) -> p j t", j=4), func=AF.Copy),
                 rd=[h_tp], wr=[h_dst])

    state = {'wi': 0, 'ci': 0}

    def cast(out_ap, in_ap, rd, wr):
        state['ci'] += 1
        if state['ci'] % 2 == 0:
            k.op('act', lambda e: e.activation(out=out_ap, in_=in_ap, func=AF.Copy), rd=rd, wr=wr)
        else:
            k.op('dve', lambda e: e.tensor_copy(out=out_ap, in_=in_ap), rd=rd, wr=wr)

    def load_w(src_ap, n, ST, WB, h_st, h_wb, view=None):
        i = state['wi'] % 2
        state['wi'] += 1
        k.dma('sp', ST[i][:, 0:n], src_ap, wr=[h_st[i]])
        cast(WB[i][:, 0:n], ST[i][:, 0:n], [h_st[i]], [h_wb[i]])
        return WB[i], h_wb[i]

    def load_big(src2d, n, dst, h_dst_, ST, h_st):
        for c0 in range(0, n, 2048):
            w = min(2048, n - c0)
            i = state['wi'] % 2
            state['wi'] += 1
            k.dma('sp', ST[i][:, 0:w], src2d[:, c0:c0 + w], wr=[h_st[i]])
            cast(dst[:, c0:c0 + w], ST[i][:, 0:w], [h_st[i]], [h_dst_])

    def tail_block(layer, tb, hblk, h_hblk, yT, h_yT, CC, wo_src, tl):
        (ST, WB, h_st, h_wb, WO, h_wo, hgT, h_hgT, normbufs, pa, pb, h_pa, h_pb, pt_s, ptb, pT, h_pt, h_pT,
         sg, ge, h_sg, gbc) = tl
        for dblk in range(4):
            wo = WO[dblk % len(WO)]
            hwo = h_wo[dblk % len(WO)]
            load_big(wo_src[dblk], CC * 512, wo, hwo, ST, h_st)
            for tile in range(4):
                pp, hp_ = (pa, h_pa) if tile % 2 == 0 else (pb, h_pb)
                for cc in range(CC):
                    k.op('pe', lambda e, cc=cc, tile=tile, pp=pp, wo=wo: e.matmul(
                        pp[:], lhsT=yT[:, cc, tile * 128:(tile + 1) * 128], rhs=wo[:, cc * 512:(cc + 1) * 512],
                        start=(cc == 0), stop=(cc == CC - 1)), rd=[h_yT, hwo], wr=[hp_])
                k.op('dve', lambda e, tile=tile, dblk=dblk, pp=pp: e.tensor_tensor(
                    out=hblk[:, tile, dblk * 512:(dblk + 1) * 512], in0=pp[:], in1=hblk[:, tile, dblk * 512:(dblk + 1) * 512],
                    op=ALU.add), rd=[hp_, h_hblk], wr=[h_hblk])
        for tile in range(4):
            rms_to_T(hblk[:, tile, :], gbc[:], hgT, tile * 128, h_hblk, h_hgT, normbufs)
            k.dma('sp', pt_s[:], p_d[layer, tb * 512 + tile * 128: tb * 512 + (tile + 1) * 128, :], wr=[h_pt])
            k.op('pool', lambda e: e.tensor_copy(out=ptb[:], in_=pt_s[:]), rd=[h_pt], wr=[h_pt])
            tp, h_tp = normbufs[5], normbufs[8]
            for c in range(2):
                k.op('pe', lambda e, c=c: e.transpose(out=tp[:, c * 128:(c + 1) * 128], in_=ptb[:, c * 128:(c + 1) * 128],
                                                      identity=ident_b[:]), rd=[h_pt, h_c], wr=[h_tp])
            k.op('act', lambda e, tile=tile: e.activation(out=pT[:, :, tile * 128:(tile + 1) * 128],
                                                          in_=tp[:, 0:256].rearrange("p (j t) -> p j t", j=2), func=AF.Copy),
                 rd=[h_tp], wr=[h_pT])
        for dblk in range(4):
            load_big(pwg_d[layer, dblk], 8192, WB[0], h_wb[0], ST, h_st)
            load_big(pew_d[layer, dblk], 1024, WB[1], h_wb[1], ST, h_st)
            for tile in range(4):
                for dc in range(16):
                    k.op('pe', lambda e, dc=dc, tile=tile: e.matmul(
                        pa[:], lhsT=hgT[:, dc, tile * 128:(tile + 1) * 128], rhs=WB[0][:, dc * 512:(dc + 1) * 512],
                        start=(dc == 0), stop=(dc == 15)), rd=[h_hgT, h_wb[0]], wr=[h_pa])
                for c in range(2):
                    k.op('pe', lambda e, c=c, tile=tile: e.matmul(
                        pb[:], lhsT=pT[:, c, tile * 128:(tile + 1) * 128], rhs=WB[1][:, c * 512:(c + 1) * 512],
                        start=(c == 0), stop=(c == 1)), rd=[h_pT, h_wb[1]], wr=[h_pb])
                k.op('act', lambda e: e.activation(out=sg[:], in_=pa[:], func=AF.Sigmoid), rd=[h_pa], wr=[h_sg])
                k.op('dve', lambda e: e.tensor_tensor(out=ge[:], in0=pb[:], in1=sg[:], op=ALU.mult), rd=[h_pb, h_sg], wr=[h_sg])
                k.op('dve', lambda e, tile=tile, dblk=dblk: e.tensor_tensor(
                    out=hblk[:, tile, dblk * 512:(dblk + 1) * 512], in0=ge[:], in1=hblk[:, tile, dblk * 512:(dblk + 1) * 512],
                    op=ALU.add), rd=[h_sg, h_hblk], wr=[h_hblk])


    def alloc_tail(ls, CC):
        sbl = lambda name, shape, dt: ls.enter_context(nc.sbuf_tensor(un(name), list(shape), dt))
        psl = lambda name, shape, dt: ls.enter_context(nc.psum_tensor(un(name), list(shape), dt))
        ST = [sbl(f"ST{i}", [128, 2048], F32) for i in range(2)]
        WB = [sbl("WB0", [128, 8192], BF16), sbl("WB1", [128, 2048], BF16)]
        WO = [sbl("WO0", [128, CC * 512], BF16)]
        hgT = sbl("hgT", [128, 16, 512], BF16)
        ss = sbl("ss", [128, 1], F32); sd = sbl("sd", [128, 1], F32); rs = sbl("rs", [128, 1], F32)
        hp = sbl("hp", [128, 2048], BF16)
        tp = psl("tp", [128, 512], BF16)
        h_hp_ = k.h()
        normbufs = (hp, ss, sd, rs, hp, tp, h_hp_, h_hp_, k.h())
        pa = psl("pa", [128, 512], F32); pb = psl("pb", [128, 512], F32)
        pt_s = sbl("pt_s", [128, 256], F32); ptb = sbl("ptb", [128, 256], BF16)
        pT = sbl("pT", [128, 2, 512], BF16)
        sg = sbl("sg", [128, 512], F32); ge = sbl("ge", [128, 512], F32)
        gbc = sbl("gbc", [128, 2048], F32)
        return (ST, WB, k.hs(2), k.hs(2), WO, k.hs(1), hgT, k.h(), normbufs, pa, pb, k.h(), k.h(), pt_s, ptb, pT,
                k.h(), k.h(), sg, ge, k.h(), gbc)

    def odd_layer(layer, j, h_src, h_dst):
        with ExitStack() as ls:
            sbl = lambda name, shape, dt: ls.enter_context(nc.sbuf_tensor(un(name), list(shape), dt))
            psl = lambda name, shape, dt: ls.enter_context(nc.psum_tensor(un(name), list(shape), dt))
            tl = alloc_tail(ls, 16)
            ST, WB, h_st, h_wb = tl[0], tl[1], tl[2], tl[3]
            normbufs = tl[8]
            pa, pb, h_pa, h_pb = tl[9], tl[10], tl[11], tl[12]
            gbc = tl[21]
            hblk = sbl("hblk", [128, 4, 2048], F32); h_hblk = k.h()
            hnT = tl[6]; h_hnT = tl[7]
            usg = sbl("usg", [128, 16, 512], BF16); h_usg = k.h()
            tmpb = sbl("tmpb", [128, 512], BF16); h_tmpb = k.h()
            WC = [sbl(f"WC{i}", [128, 2048], BF16) for i in range(2)]; h_wc = k.hs(2)
            gv = sbl("gv", [128, 4, 2048], F32); h_gv = k.h()
            z = normbufs[4]; h_z = normbufs[7]
            yT = usg; h_yT = h_usg
            nbc = sbl("nbc", [128, 2048], F32)
            ovec = sbl("ovec", [128, 32], F32)
            wsT_f = gv[:, 0, :]
            wsT_b = sbl("wsT_b", [128, 16, 128], BF16)
            bsbc = gv[:, 1, :]
            Cg = sbl("Cg", [128, 16, 128], F32)
            st6 = sbl("st6", [128, 4, 6], F32); mv = sbl("mv", [128, 2], F32); h_st6 = k.h()
            lsd = sbl("lsd", [128, 1], F32); lrs = sbl("lrs", [128, 1], F32); nmr = sbl("nmr", [128, 1], F32)
            mx = sbl("mx", [128, 128], F32); h_mx = k.h()
            mps = psl("mps", [128, 512], F32); h_mps = k.h()
            h_lc = k.h()
            k.dma('sp', nbc[:], ng_d[layer, :].partition_broadcast(128), wr=[h_c])
            k.dma('sp', gbc[:], pgn_d[layer, :].partition_broadcast(128), wr=[h_c])
            k.dma('sp', ovec[:], ovec_d[j], wr=[h_lc])
            k.dma('sp', wsT_f, wst_d[j], wr=[h_lc])
            for g in range(16):
                k.dma('sp', bsbc[:, g * 128:(g + 1) * 128], bs_d[j, g, :].partition_broadcast(128), wr=[h_lc])
            wv = wsT_f.rearrange("s (g t) -> s g t", g=16)
            k.op('dve', lambda e: e.memset(wv[64:128, :, 0:64], 0.0), rd=[h_lc], wr=[h_lc])
            k.op('dve', lambda e: e.tensor_copy(out=wsT_b[:], in_=wv), rd=[h_lc], wr=[h_lc])
            for g in range(16):
                k.op('pe', lambda e, g=g: e.matmul(mps[:, 0:128], lhsT=ones1[:], rhs=wv[:, g, :], start=True, stop=True),
                     rd=[h_lc, h_c], wr=[h_mps])
                k.op('dve', lambda e, g=g: e.scalar_tensor_tensor(out=Cg[:, g, :], in0=mps[:, 0:128], scalar=ovec[:, 16 + g:17 + g],
                                                                   in1=bsbc[:, g * 128:(g + 1) * 128], op0=ALU.mult, op1=ALU.add),
                     rd=[h_mps, h_lc], wr=[h_lc])
            k.barrier()
            for tb in range(8):
                k.maybe_barrier()
                t0 = tb * 512
                k.dma('sp', hblk[:], h_src[t0:t0 + 512, :].rearrange("(i p) d -> p i d", p=128), wr=[h_hblk])
                for tile in range(4):
                    rms_to_T(hblk[:, tile, :], nbc[:], hnT, tile * 128, h_hblk, h_hnT, normbufs)
                for ch in range(32):
                    wb, hwb = load_w(owf_d[j, ch], 2048, ST, WC, h_st, h_wc)
                    pp, hp_ = (pa, h_pa) if ch % 2 == 0 else (pb, h_pb)
                    for dc in range(16):
                        k.op('pe', lambda e, dc=dc, wb=wb, pp=pp: e.matmul(pp[:], lhsT=wb[:, dc * 128:(dc + 1) * 128], rhs=hnT[:, dc, :],
                                                                           start=(dc == 0), stop=(dc == 15)), rd=[hwb, h_hnT], wr=[hp_])
                    if ch < 16:
                        k.op('act', lambda e, ch=ch, pp=pp: e.activation(out=usg[:, ch, :], in_=pp[:], func=AF.Gelu_apprx_tanh),
                             rd=[hp_], wr=[h_usg])
                    else:
                        k.op('act', lambda e, pp=pp: e.activation(out=tmpb[:], in_=pp[:], func=AF.Silu), rd=[hp_], wr=[h_tmpb])
                        k.op('dve', lambda e, ch=ch: e.tensor_tensor(out=usg[:, ch - 16, :], in0=usg[:, ch - 16, :], in1=tmpb[:], op=ALU.mult),
                             rd=[h_tmpb, h_usg], wr=[h_usg])
                for vb in range(4):
                    load_big(owv_d[j, vb], 8192, WB[0], h_wb[0], ST, h_st)
                    for tile in range(4):
                        pp, hp_ = (pa, h_pa) if tile % 2 == 0 else (pb, h_pb)
                        for dc in range(16):
                            k.op('pe', lambda e, dc=dc, tile=tile, pp=pp: e.matmul(
                                pp[:], lhsT=hnT[:, dc, tile * 128:(tile + 1) * 128], rhs=WB[0][:, dc * 512:(dc + 1) * 512],
                                start=(dc == 0), stop=(dc == 15)), rd=[h_wb[0], h_hnT], wr=[hp_])
                        k.op('act', lambda e, tile=tile, vb=vb, pp=pp: e.activation(out=gv[:, tile, vb * 512:(vb + 1) * 512], in_=pp[:],
                                                                                   func=AF.Gelu_apprx_tanh), rd=[hp_], wr=[h_gv])
                for tile in range(4):
                    for q4 in range(4):
                        k.op('dve', lambda e, q4=q4, tile=tile: e.bn_stats(out=st6[:, q4, :], in_=gv[:, tile, q4 * 512:(q4 + 1) * 512]),
                             rd=[h_gv], wr=[h_st6])
                    k.op('dve', lambda e: e.bn_aggr(out=mv[:], in_=st6[:].rearrange("p a b -> p (a b)")), rd=[h_st6], wr=[h_st6])
                    k.op('act', lambda e: e.activation(out=lsd[:], in_=mv[:, 1:2], func=AF.Sqrt, bias=epsc[:, 0:1]), rd=[h_st6, h_c], wr=[h_st6])
                    k.op('dve', lambda e: e.reciprocal(out=lrs[:], in_=lsd[:]), rd=[h_st6], wr=[h_st6])
                    k.op('dve', lambda e: e.scalar_tensor_tensor(out=nmr[:], in0=mv[:, 0:1], scalar=-1.0, in1=lrs[:], op0=ALU.mult, op1=ALU.mult),
                         rd=[h_st6], wr=[h_st6])
                    k.op('dve', lambda e, tile=tile: e.tensor_scalar(out=z[:], in0=gv[:, tile, :], scalar1=lrs[:, 0:1], scalar2=nmr[:, 0:1],
                                                                      op0=ALU.mult, op1=ALU.add), rd=[h_gv, h_st6], wr=[h_z])
                    for g in range(16):
                        k.op('pe', lambda e, g=g: e.matmul(mps[:, (g % 4) * 128:(g % 4 + 1) * 128], lhsT=z[:, g * 128:(g + 1) * 128],
                                                            rhs=wsT_b[:, g, :], start=True, stop=True), rd=[h_z, h_lc], wr=[h_mps])
                        k.op('dve', lambda e, g=g: e.scalar_tensor_tensor(out=mx[:], in0=mps[:, (g % 4) * 128:(g % 4 + 1) * 128],
                                                                           scalar=ovec[:, g:g + 1], in1=Cg[:, g, :], op0=ALU.mult, op1=ALU.add),
                             rd=[h_mps, h_lc], wr=[h_mx])
                        k.op('dve', lambda e, g=g, tile=tile: e.tensor_tensor(out=yT[:, g, tile * 128:(tile + 1) * 128], in0=mx[:],
                                                                              in1=usg[:, g, tile * 128:(tile + 1) * 128], op=ALU.mult),
                             rd=[h_mx, h_usg], wr=[h_yT])
                tail_block(layer, tb, hblk, h_hblk, yT, h_yT, 16, owo_d[j], tl)
                k.dma('sp', h_dst[t0:t0 + 512, :].rearrange("(i p) d -> p i d", p=128), hblk[:], rd=[h_hblk])
            k.barrier()


    def even_layer(layer, j, h_src, h_dst):
        CH = ([('xa', i) for i in range(16)] + [('ga', i) for i in range(16)] + [('q', i) for i in range(16)] +
              [('k', i) for i in range(4)] + [('gb', i) for i in range(16)] + [('iq', i) for i in range(8)] + [('ik', 0)])
        with ExitStack() as ls:
          if 'E1' in PH:
              sbl = lambda name, shape, dt: ls.enter_context(nc.sbuf_tensor(un(name), list(shape), dt))
              psl = lambda name, shape, dt: ls.enter_context(nc.psum_tensor(un(name), list(shape), dt))
              ST = [sbl(f"ST{i}", [128, 2048], F32) for i in range(2)]; h_st = k.hs(2)
              WB = [sbl("WB0", [128, 8192], BF16), sbl("WB1", [128, 2048], BF16)]; h_wb = k.hs(2)
              WC = [sbl(f"WC{i}", [128, 2048], BF16) for i in range(2)]; h_wc = k.hs(2)
              hnT = sbl("hnT", [128, 16, 1024], BF16); h_hnT = k.h()
              ht = [sbl(f"ht{i}", [128, 2048], F32) for i in range(2)]; h_ht = k.hs(2)
              ss = sbl("ss", [128, 1], F32); sd = sbl("sd", [128, 1], F32); rs = sbl("rs", [128, 1], F32)
              hp = sbl("hp", [128, 2048], BF16)
              tp = psl("tp", [128, 512], BF16)
              h_hp_ = k.h()
              normbufs = (hp, ss, sd, rs, hp, tp, h_hp_, h_hp_, k.h())
              pa = psl("pa", [128, 512], F32); pb = psl("pb", [128, 512], F32); h_pa, h_pb = k.h(), k.h()
              pm = psl("pm", [128, 512], F32); h_pm = k.h()
              nbc = sbl("nbc", [128, 2048], F32)
              qkn = sbl("qkn", [128, 2], F32); qg = sbl("qg", [128, 2], F32)
              of = [sbl(f"of{i}", [128, 512], F32) for i in range(2)]
              ob = [sbl(f"ob{i}", [128, 512], BF16) for i in range(2)]; h_o = k.hs(2)
              sq = sbl("sq", [128, 512], F32); rsd = sbl("rsd", [128, 512], F32); h_sq = k.h()
              vt = sbl("vt", [128, 4, 130], BF16); h_vt = k.h()
              iwt = sbl("iwt", [128, 16], F32); h_iwt = k.h()
              h_lc = k.h()
              k.dma('sp', nbc[:], ng_d[layer, :].partition_broadcast(128), wr=[h_c])
              k.dma('sp', qkn[:], qkn_d[j], wr=[h_lc])
              k.op('dve', lambda e: e.tensor_scalar(out=qg[:, 0:1], in0=qkn[:, 0:1], scalar1=float(128 ** -0.5), scalar2=None, op0=ALU.mult),
                   rd=[h_lc], wr=[h_lc])
              k.op('dve', lambda e: e.tensor_copy(out=qg[:, 1:2], in_=qkn[:, 1:2]), rd=[h_lc], wr=[h_lc])
              k.op('dve', lambda e: e.memset(vt[:], 1.0), wr=[h_vt])
              dst = {'xa': xa_d, 'ga': sga_d, 'gb': sgb_d, 'q': q_d, 'k': kk_d, 'iq': iq_d, 'ik': ik_d}
              oi = 0
              for blk in range(4):
                  k.maybe_barrier()
                  T0 = blk * 1024
                  for tile in range(8):
                      i = tile % 2
                      k.dma('sp', ht[i][:], h_src[T0 + tile * 128:T0 + (tile + 1) * 128, :], wr=[h_ht[i]])
                      rms_to_T(ht[i][:], nbc[:], hnT, tile * 128, h_ht[i], h_hnT, normbufs)
                  for ci, (kind, idx) in enumerate(CH):
                      wb, hwb = load_w(ewf_d[j, ci], 2048, ST, WC, h_st, h_wc)
                      for half in range(2):
                          pp, hp_ = (pa, h_pa) if half == 0 else (pb, h_pb)
                          for dc in range(16):
                              k.op('pe', lambda e, dc=dc, wb=wb, pp=pp, half=half: e.matmul(
                                  pp[:], lhsT=wb[:, dc * 128:(dc + 1) * 128], rhs=hnT[:, dc, half * 512:(half + 1) * 512],
                                  start=(dc == 0), stop=(dc == 15)), rd=[hwb, h_hnT], wr=[hp_])
                          o = oi % 2; oi += 1
                          rows = slice(idx * 128, (idx + 1) * 128)
                          cols = slice(T0 + half * 512, T0 + (half + 1) * 512)
                          if kind == 'xa':
                              k.op('act', lambda e, pp=pp, o=o: e.activation(out=of[o][:], in_=pp[:], func=AF.Copy), rd=[hp_], wr=[h_o[o]])
                              k.dma('act', xa_d[rows, cols], of[o][:], rd=[h_o[o]])
                          elif kind in ('ga', 'gb'):
                              k.op('act', lambda e, pp=pp, o=o: e.activation(out=ob[o][:], in_=pp[:], func=AF.Silu), rd=[hp_], wr=[h_o[o]])
                              k.dma('act', dst[kind][rows, cols], ob[o][:], rd=[h_o[o]])
                          elif kind in ('iq', 'ik'):
                              k.op('act', lambda e, pp=pp, o=o: e.activation(out=ob[o][:], in_=pp[:], func=AF.Copy), rd=[hp_], wr=[h_o[o]])
                              k.dma('act', dst[kind][rows, cols], ob[o][:], rd=[h_o[o]])
                          else:
                              gi = 0 if kind == 'q' else 1
                              k.op('act', lambda e, pp=pp: e.activation(out=sq[:], in_=pp[:], func=AF.Square), rd=[hp_], wr=[h_sq])
                              k.op('pe', lambda e: e.matmul(pm[:], lhsT=onesm[:], rhs=sq[:], start=True, stop=True), rd=[h_sq, h_c], wr=[h_pm])
                              k.op('act', lambda e: e.activation(out=rsd[:], in_=pm[:], func=AF.Sqrt, bias=epsc[:, 0:1]), rd=[h_pm, h_c], wr=[h_sq])
                              k.op('dve', lambda e: e.reciprocal(out=rsd[:], in_=rsd[:]), rd=[h_sq], wr=[h_sq])
                              k.op('dve', lambda e, pp=pp, o=o, gi=gi: e.scalar_tensor_tensor(
                                  out=ob[o][:], in0=pp[:], scalar=qg[:, gi:gi + 1], in1=rsd[:], op0=ALU.mult, op1=ALU.mult),
                                  rd=[hp_, h_sq, h_lc], wr=[h_o[o]])
                              k.dma('sp', dst[kind][rows, cols], ob[o][:], rd=[h_o[o]])
                  load_big(ewv_d[j], 8192, WB[0], h_wb[0], ST, h_st)
                  load_big(ewiw_d[j], 256, WB[1], h_wb[1], ST, h_st)
                  for tile in range(8):
                      for dc in range(16):
                          k.op('pe', lambda e, dc=dc, tile=tile: e.matmul(pa[:], lhsT=hnT[:, dc, tile * 128:(tile + 1) * 128],
                                                                        rhs=WB[0][:, dc * 512:(dc + 1) * 512], start=(dc == 0), stop=(dc == 15)),
                               rd=[h_wb[0], h_hnT], wr=[h_pa])
                      k.op('act', lambda e: e.activation(out=vt[:, :, 0:128], in_=pa[:].rearrange("p (g d) -> p g d", g=4), func=AF.Copy),
                           rd=[h_pa], wr=[h_vt])
                      k.dma('act', v_d[T0 + tile * 128:T0 + (tile + 1) * 128, :], vt[:].rearrange("p g d -> p (g d)"), rd=[h_vt])
                      for dc in range(16):
                          k.op('pe', lambda e, dc=dc, tile=tile: e.matmul(pb[:, 0:16], lhsT=hnT[:, dc, tile * 128:(tile + 1) * 128],
                                                                        rhs=WB[1][:, dc * 16:(dc + 1) * 16], start=(dc == 0), stop=(dc == 15)),
                               rd=[h_wb[1], h_hnT], wr=[h_pb])
                      k.op('act', lambda e: e.activation(out=iwt[:], in_=pb[:, 0:16], func=AF.Copy), rd=[h_pb], wr=[h_iwt])
                      k.dma('act', iw_d[T0 + tile * 128:T0 + (tile + 1) * 128, :], iwt[:], rd=[h_iwt])
              k.barrier()
        with ExitStack() as ls:
          if 'E2' in PH:
              sbl = lambda name, shape, dt: ls.enter_context(nc.sbuf_tensor(un(name), list(shape), dt))
              psl = lambda name, shape, dt: ls.enter_context(nc.psum_tensor(un(name), list(shape), dt))
              L = 1024
              evec = sbl("evec", [128, 8, 16], F32)
              kap = sbl("kap", [128, 16], F32); kap2 = sbl("kap2", [128, 16], F32); tmpk = sbl("tmpk", [128, 16], F32)
              wr_f = sbl("wr_f", [128, 2048], F32); wi_f = sbl("wi_f", [128, 2048], F32)
              wr_b = sbl("wr_b", [128, 16, 128], BF16); wi_b = sbl("wi_b", [128, 16, 128], BF16)
              h_lc = k.h()
              xin = [sbl(f"xin{i}", [128, 3 + L], F32) for i in range(2)]; h_xin = k.hs(2)
              xc = sbl("xc", [128, L], F32); h_xc = k.h()
              xcb = sbl("xcb", [128, L], BF16); h_xcb = k.h()
              rg = sbl("rg", [128, L], F32); ig = sbl("ig", [128, L], F32); h_rg, h_ig = k.h(), k.h()
              aa = sbl("aa", [128, L], F32); a2 = sbl("a2", [128, L], F32); h_aa, h_a2 = k.h(), k.h()
              bx = sbl("bx", [128, L], F32); h_bx = k.h()
              hsb = [sbl(f"hs{i}", [128, L], F32) for i in range(2)]; h_hs = k.hs(2)
              sga = [sbl(f"sga{i}", [128, L], BF16) for i in range(2)]; h_sga = k.hs(2)
              yb = [sbl(f"yb{i}", [128, L], BF16) for i in range(2)]; h_yb = k.hs(2)
              pa = psl("pa", [128, 512], F32); pb = psl("pb", [128, 512], F32); h_pa, h_pb = k.h(), k.h()
              k.dma('sp', evec[:].rearrange("p a g -> p (a g)"), evec_d[j], wr=[h_lc])
              k.dma('sp', wr_f[:], lwr_d[j], wr=[h_lc])
              k.dma('sp', wi_f[:], lwi_d[j], wr=[h_lc])
              k.op('dve', lambda e: e.tensor_copy(out=wr_b[:].rearrange("p g j -> p (g j)"), in_=wr_f[:]), rd=[h_lc], wr=[h_lc])
              k.op('dve', lambda e: e.tensor_copy(out=wi_b[:].rearrange("p g j -> p (g j)"), in_=wi_f[:]), rd=[h_lc], wr=[h_lc])
              k.op('act', lambda e: e.activation(out=tmpk[:], in_=evec[:, 7, :], func=AF.Exp, scale=-1.0), rd=[h_lc], wr=[h_lc])
              k.op('act', lambda e: e.activation(out=tmpk[:], in_=tmpk[:], func=AF.Ln, bias=onec[:, 0:1]), rd=[h_lc, h_c], wr=[h_lc])
              k.op('dve', lambda e: e.tensor_scalar(out=kap[:], in0=tmpk[:], scalar1=-8.0, scalar2=None, op0=ALU.mult), rd=[h_lc], wr=[h_lc])
              k.op('dve', lambda e: e.tensor_scalar(out=kap2[:], in0=tmpk[:], scalar1=-16.0, scalar2=None, op0=ALU.mult), rd=[h_lc], wr=[h_lc])
              it = 0
              for g in range(16):
                  k.maybe_barrier()
                  rows = slice(g * 128, (g + 1) * 128)
                  for pc in range(T // L):
                      i = it % 2; it += 1
                      t0 = pc * L
                      if pc == 0:
                          k.op('dve', lambda e, i=i: e.memset(xin[i][:, 0:3], 0.0), wr=[h_xin[i]])
                          k.dma('sp', xin[i][:, 3:3 + L], xa_d[rows, t0:t0 + L], wr=[h_xin[i]])
                      else:
                          k.dma('sp', xin[i][:, 0:3 + L], xa_d[rows, t0 - 3:t0 + L], wr=[h_xin[i]])
                      k.dma('sp', sga[i][:], sga_d[rows, t0:t0 + L], wr=[h_sga[i]])
                      k.op('dve', lambda e, i=i, g=g: e.tensor_scalar(out=xc[:], in0=xin[i][:, 3:3 + L], scalar1=evec[:, 3, g:g + 1],
                                                                      scalar2=evec[:, 4, g:g + 1], op0=ALU.mult, op1=ALU.add),
                           rd=[h_xin[i], h_lc], wr=[h_xc])
                      for tap in range(3):
                          k.op('dve', lambda e, i=i, g=g, tap=tap: e.scalar_tensor_tensor(
                              out=xc[:], in0=xin[i][:, tap:tap + L], scalar=evec[:, tap, g:g + 1], in1=xc[:], op0=ALU.mult, op1=ALU.add),
                              rd=[h_xin[i], h_lc, h_xc], wr=[h_xc])
                      k.op('act', lambda e: e.activation(out=xcb[:], in_=xc[:], func=AF.Copy), rd=[h_xc], wr=[h_xcb])
                      for half in range(L // 512):
                          sl = slice(half * 512, (half + 1) * 512)
                          k.op('pe', lambda e, g=g, sl=sl: e.matmul(pa[:], lhsT=wr_b[:, g, :], rhs=xcb[:, sl], start=True, stop=True),
                               rd=[h_xcb, h_lc], wr=[h_pa])
                          k.op('pe', lambda e, g=g, sl=sl: e.matmul(pb[:], lhsT=wi_b[:, g, :], rhs=xcb[:, sl], start=True, stop=True),
                               rd=[h_xcb, h_lc], wr=[h_pb])
                          k.op('act', lambda e, g=g, sl=sl: e.activation(out=rg[:, sl], in_=pa[:], func=AF.Sigmoid, bias=evec[:, 5, g:g + 1]),
                               rd=[h_pa, h_lc], wr=[h_rg])
                          k.op('act', lambda e, g=g, sl=sl: e.activation(out=ig[:, sl], in_=pb[:], func=AF.Sigmoid, bias=evec[:, 6, g:g + 1]),
                               rd=[h_pb, h_lc], wr=[h_ig])
                      k.op('act', lambda e, g=g: e.activation(out=aa[:], in_=rg[:], func=AF.Exp, scale=kap[:, g:g + 1]), rd=[h_rg, h_lc], wr=[h_aa])
                      k.op('act', lambda e, g=g: e.activation(out=a2[:], in_=rg[:], func=AF.Exp, scale=kap2[:, g:g + 1]), rd=[h_rg, h_lc], wr=[h_a2])
                      k.op('act', lambda e: e.activation(out=a2[:], in_=a2[:], func=AF.Sqrt, scale=-1.0, bias=onec[:, 0:1]), rd=[h_a2, h_c], wr=[h_a2])
                      k.op('pool', lambda e: e.tensor_tensor(out=bx[:], in0=a2[:], in1=ig[:], op=ALU.mult), rd=[h_a2, h_ig], wr=[h_bx])
                      k.op('pool', lambda e: e.tensor_tensor(out=bx[:], in0=bx[:], in1=xc[:], op=ALU.mult), rd=[h_bx, h_xc], wr=[h_bx])
                      if pc == 0:
                          k.op('dve', lambda e, i=i: e.tensor_tensor_scan(out=hsb[i][:], data0=aa[:], data1=bx[:], initial=0.0,
                                                                          op0=ALU.mult, op1=ALU.add), rd=[h_aa, h_bx], wr=[h_hs[i]])
                      else:
                          k.op('dve', lambda e, i=i: e.tensor_tensor_scan(out=hsb[i][:], data0=aa[:], data1=bx[:], initial=hsb[1 - i][:, L - 1:L],
                                                                          op0=ALU.mult, op1=ALU.add), rd=[h_aa, h_bx, h_hs[1 - i]], wr=[h_hs[i]])
                      k.op('pool', lambda e, i=i: e.tensor_tensor(out=yb[i][:], in0=hsb[i][:], in1=sga[i][:], op=ALU.mult),
                           rd=[h_hs[i], h_sga[i]], wr=[h_yb[i]])
                      k.dma('sp', yT_d[rows, t0:t0 + L], yb[i][:], rd=[h_yb[i]])
              k.barrier()
        with ExitStack() as ls:
          if 'E3' in PH:
              sbl = lambda name, shape, dt: ls.enter_context(nc.sbuf_tensor(un(name), list(shape), dt))
              psl = lambda name, shape, dt: ls.enter_context(nc.psum_tensor(un(name), list(shape), dt))
              kT = sbl("kT", [128, 4, T], BF16); vA = sbl("vA", [128, 32, 4 * 130], BF16); ik2 = sbl("ik2", [128, T], BF16)
              h_kv = k.h()
              for g in range(4):
                  k.dma('sp', kT[:, g, :], kk_d[g * 128:(g + 1) * 128, :], wr=[h_kv])
              for c in range(4):
                  k.dma('sp', vA[:, c * 8:(c + 1) * 8, :], v_d[c * 1024:(c + 1) * 1024, :].rearrange("(i p) d -> p i d", p=128), wr=[h_kv])
              k.dma('sp', ik2[:], ik_d, wr=[h_kv])
              qt = [sbl(f"qt{i}", [128, 16, 128], BF16) for i in range(2)]
              iqt = [sbl(f"iqt{i}", [128, 8, 128], BF16) for i in range(2)]
              iwt = [sbl(f"iwt{i}", [128, 16], F32) for i in range(2)]
              sgbt = [sbl(f"sgbt{i}", [128, 16, 128], BF16) for i in range(2)]
              h_ld = k.hs(2)
              dg = sbl("dg", [128, 16, 128], BF16); h_dg = k.h()
              Rb = [sbl(f"Rb{i}", [128, 512], BF16) for i in range(2)]; h_R = k.hs(2)
              acc = sbl("acc", [128, T], F32); h_acc = k.h()
              cj = sbl("cj", [128, T], BF16); h_cj = k.h()
              am = sbl("am", [128, 1], F32); lo = sbl("lo", [128, 1], F32); w0 = sbl("w0", [128, 1], F32)
              mid = sbl("mid", [128, 1], F32); cnt = sbl("cnt", [128, 1], F32); t1 = sbl("t1", [128, 1], F32); h_bs = k.h()
              mask = sbl("mask", [128, T], BF16); h_mask = k.h()
              mT4 = sbl("mT4", [128, 32, 4, 128], BF16); h_mT = k.h()
              Mn = sbl("Mn", [128, 2, 16, 128], BF16); h_Mn = k.h()
              Eb = [sbl(f"Eb{i}", [128, 512], BF16) for i in range(2)]; h_E = k.hs(2)
              Pb = [sbl(f"Pb{i}", [128, 4, 128], BF16) for i in range(2)]; h_P = k.hs(2)
              ri = sbl("ri", [128, 4], F32); h_ri = k.h()
              ot = sbl("ot", [128, 16, 128], BF16); h_ot = k.h()
              yt = [sbl(f"yt{i}", [128, 16, 128], BF16) for i in range(2)]; h_yt = k.hs(2)
              pA = psl("pA", [128, 512], F32); pB = psl("pB", [128, 512], F32); pC = psl("pC", [128, 512], F32)
              h_pA, h_pB, h_pC = k.h(), k.h(), k.h()
              mtp = psl("mtp", [128, 512], BF16); h_mtp = k.h()
              oacc = [psl(f"oacc{i}", [128, 512], F32) for i in range(4)]; h_oa = k.hs(4)
              EBv = EB[:].rearrange("s (k h t) -> s k h t", k=2, h=16)
              zi = 0
              for n in range(NT):
                  k.maybe_barrier()
                  b = n % 2
                  S = 128 * (n + 1)
                  c0 = n * 128
                  k.dma('sp', qt[b][:], q_d[:, c0:c0 + 128].rearrange("(h p) t -> p h t", p=128), wr=[h_ld[b]])
                  k.dma('sp', iqt[b][:], iq_d[:, c0:c0 + 128].rearrange("(h p) t -> p h t", p=128), wr=[h_ld[b]])
                  k.dma('sp', iwt[b][:], iw_d[c0:c0 + 128, :], wr=[h_ld[b]])
                  k.dma('sp', sgbt[b][:], sgb_d[:, c0:c0 + 128].rearrange("(h p) t -> p h t", p=128), wr=[h_ld[b]])
                  for h in range(16):
                      k.op('pool', lambda e, h=h, b=b: e.tensor_scalar(out=dg[:, h, :], in0=ident_f[:], scalar1=iwt[b][:, h:h + 1], scalar2=0.0,
                                                                       op0=ALU.mult, op1=ALU.add), rd=[h_ld[b], h_c], wr=[h_dg])
                  nkb = (S + 511) // 512
                  for kb in range(nkb):
                      w = min(512, S - kb * 512)
                      for h in range(16):
                          c = h // 2; p0 = 64 * (h % 2)
                          zp, hz = (pA, h_pA) if zi % 2 == 0 else (pB, h_pB)
                          R, hR = Rb[zi % 2], h_R[zi % 2]
                          zi += 1
                          k.op('pe', lambda e, c=c, p0=p0, zp=zp, kb=kb, w=w, b=b: e.matmul(
                              zp[:, 0:w], lhsT=iqt[b][p0:p0 + 64, c, :], rhs=ik2[p0:p0 + 64, kb * 512:kb * 512 + w], start=True, stop=True),
                              rd=[h_ld[b], h_kv], wr=[hz])
                          k.op('act', lambda e, zp=zp, R=R, w=w: e.activation(out=R[:, 0:w], in_=zp[:, 0:w], func=AF.Relu), rd=[hz], wr=[hR])
                          k.op('pe', lambda e, h=h, R=R, w=w: e.matmul(pC[:, 0:w], lhsT=dg[:, h, :], rhs=R[:, 0:w], start=(h == 0), stop=(h == 15)),
                               rd=[h_dg, hR], wr=[h_pC])
                      k.op('dve', lambda e, kb=kb, w=w: e.tensor_copy(out=acc[:, kb * 512:kb * 512 + w], in_=pC[:, 0:w]), rd=[h_pC], wr=[h_acc])
                  k.op('dve', lambda e, S=S: e.tensor_reduce(out=am[:], in_=acc[:, 0:S], axis=AX.X, op=ALU.max, apply_absolute_value=True),
                       rd=[h_acc], wr=[h_bs])
                  k.op('dve', lambda e: e.tensor_scalar(out=lo[:], in0=am[:], scalar1=-1.001, scalar2=-1e-20, op0=ALU.mult, op1=ALU.add), rd=[h_bs], wr=[h_bs])
                  k.op('dve', lambda e: e.tensor_scalar(out=w0[:], in0=am[:], scalar1=2.002, scalar2=2e-20, op0=ALU.mult, op1=ALU.add), rd=[h_bs], wr=[h_bs])
                  k.op('dve', lambda e, S=S: e.tensor_tensor(out=acc[:, S - 128:S], in0=acc[:, S - 128:S], in1=diagneg[:], op=ALU.add),
                       rd=[h_acc, h_c], wr=[h_acc])
                  if S > 256:
                      for itn in range(1, NIT + 1):
                          f = 2.0 ** (-itn)
                          k.op('dve', lambda e, f=f: e.scalar_tensor_tensor(out=mid[:], in0=w0[:], scalar=f, in1=lo[:], op0=ALU.mult, op1=ALU.add),
                               rd=[h_bs], wr=[h_bs])
                          k.op('dve', lambda e, S=S: e.tensor_scalar(out=cj[:, 0:S], in0=acc[:, 0:S], scalar1=mid[:, 0:1], scalar2=None,
                                                                     op0=ALU.is_ge, op1=ALU.add, accum_out=cnt[:, 0:1]), rd=[h_acc, h_bs], wr=[h_cj, h_bs])
                          k.op('dve', lambda e, f=f: e.tensor_scalar(out=t1[:], in0=cnt[:], scalar1=255.5, scalar2=f, op0=ALU.is_ge, op1=ALU.mult),
                               rd=[h_bs], wr=[h_bs])
                          k.op('dve', lambda e: e.scalar_tensor_tensor(out=lo[:], in0=t1[:], scalar=w0[:, 0:1], in1=lo[:], op0=ALU.mult, op1=ALU.add),
                               rd=[h_bs], wr=[h_bs])
                  k.op('dve', lambda e, S=S: e.tensor_scalar(out=mask[:, 0:S], in0=acc[:, 0:S], scalar1=lo[:, 0:1], scalar2=None, op0=ALU.is_ge),
                       rd=[h_acc, h_bs], wr=[h_mask])
                  for k0 in range(0, n + 1, 4):
                      nk = min(4, n + 1 - k0)
                      for q in range(nk):
                          k.op('pe', lambda e, q=q, k0=k0: e.transpose(out=mtp[:, q * 128:(q + 1) * 128], in_=mask[:, (k0 + q) * 128:(k0 + q + 1) * 128],
                                                                       identity=ident_b[:]), rd=[h_mask, h_c], wr=[h_mtp])
                      for r in range(4):
                          eng = 'act' if r % 2 == 0 else 'dve'
                          if eng == 'act':
                              k.op('act', lambda e, r=r, k0=k0, nk=nk: e.activation(out=mT4[:, k0:k0 + nk, r, :],
                                                                                   in_=mtp[:, 0:nk * 128].rearrange("p (q t) -> p q t", q=nk), func=AF.Copy),
                                   rd=[h_mtp], wr=[h_mT])
                          else:
                              k.op('dve', lambda e, r=r, k0=k0, nk=nk: e.tensor_copy(out=mT4[:, k0:k0 + nk, r, :],
                                                                                    in_=mtp[:, 0:nk * 128].rearrange("p (q t) -> p q t", q=nk)),
                                   rd=[h_mtp], wr=[h_mT])
                  for kind in range(2):
                      if n - kind < 0:
                          continue
                      for g4 in range(4):
                          k.op('pool', lambda e, kind=kind, g4=g4, n=n: e.tensor_tensor(out=Mn[:, kind, g4 * 4:(g4 + 1) * 4, :],
                                                                                        in0=EBv[:, kind, g4 * 4:(g4 + 1) * 4, :],
                                                                                        in1=mT4[:, n - kind, :, :], op=ALU.mult),
                               rd=[h_mT, h_c], wr=[h_Mn])
                  si = 0
                  for g in range(4):
                      for kt in range(n + 1):
                          sp_, hs_ = (pA, h_pA) if si % 2 == 0 else (pB, h_pB)
                          E_, hE = Eb[si % 2], h_E[si % 2]
                          P_, hP = Pb[si % 2], h_P[si % 2]
                          si += 1
                          k.op('pe', lambda e, g=g, kt=kt, sp_=sp_, b=b: e.matmul(sp_[:], lhsT=kT[:, g, kt * 128:(kt + 1) * 128],
                                                                                 rhs=qt[b][:, g * 4:(g + 1) * 4, :], start=True, stop=True),
                               rd=[h_kv, h_ld[b]], wr=[hs_])
                          k.op('act', lambda e, sp_=sp_, E_=E_: e.activation(out=E_[:], in_=sp_[:], func=AF.Exp), rd=[hs_], wr=[hE])
                          kind = n - kt
                          if kind < 2:
                              k.op('dve', lambda e, E_=E_, P_=P_, kind=kind, g=g: e.tensor_tensor(
                                  out=P_[:], in0=E_[:].rearrange("p (r t) -> p r t", r=4), in1=Mn[:, kind, g * 4:(g + 1) * 4, :], op=ALU.mult),
                                  rd=[hE, h_Mn], wr=[hP])
                          else:
                              k.op('dve', lambda e, E_=E_, P_=P_, kt=kt: e.tensor_tensor(
                                  out=P_[:], in0=E_[:].rearrange("p (r t) -> p r t", r=4), in1=mT4[:, kt, :, :], op=ALU.mult),
                                  rd=[hE, h_mT], wr=[hP])
                          for r in range(4):
                              k.op('pe', lambda e, r=r, P_=P_, kt=kt, g=g, n=n: e.matmul(
                                  oacc[r][:, 0:129], lhsT=P_[:, r, :], rhs=vA[:, kt, g * 130:g * 130 + 129], start=(kt == 0), stop=(kt == n)),
                                  rd=[hP, h_kv], wr=[h_oa[r]])
                      for r in range(4):
                          k.op('dve', lambda e, r=r: e.reciprocal(out=ri[:, r:r + 1], in_=oacc[r][:, 128:129]), rd=[h_oa[r]], wr=[h_ri])
                          k.op('dve', lambda e, r=r, g=g: e.tensor_scalar(out=ot[:, g * 4 + r, :], in0=oacc[r][:, 0:128], scalar1=ri[:, r:r + 1],
                                                                          scalar2=None, op0=ALU.mult), rd=[h_oa[r], h_ri], wr=[h_ot])
                  for g4 in range(4):
                      for r in range(4):
                          k.op('pe', lambda e, g4=g4, r=r: e.transpose(out=mtp[:, r * 128:(r + 1) * 128], in_=ot[:, g4 * 4 + r, :], identity=ident_b[:]),
                               rd=[h_ot, h_c], wr=[h_mtp])
                      k.op('dve', lambda e, g4=g4, b=b: e.tensor_tensor(out=yt[b][:, g4 * 4:(g4 + 1) * 4, :], in0=mtp[:].rearrange("p (r t) -> p r t", r=4),
                                                                        in1=sgbt[b][:, g4 * 4:(g4 + 1) * 4, :], op=ALU.mult),
                           rd=[h_mtp, h_ld[b]], wr=[h_yt[b]])
                  k.dma('sp', yT_d[2048:4096, c0:c0 + 128].rearrange("(h p) t -> p h t", p=128), yt[b][:], rd=[h_yt[b]])
              k.barrier()
        with ExitStack() as ls:
          if 'E4' in PH:
              sbl = lambda name, shape, dt: ls.enter_context(nc.sbuf_tensor(un(name), list(shape), dt))
              tl = alloc_tail(ls, 32)
              gbc = tl[21]
              hblk = sbl("hblk", [128, 4, 2048], F32); h_hblk = k.h()
              yT = sbl("yTs", [128, 32, 512], BF16); h_yT = k.h()
              h_lc = k.h()
              k.dma('sp', gbc[:], pgn_d[layer, :].partition_broadcast(128), wr=[h_c])
              for tb in range(8):
                  k.maybe_barrier()
                  t0 = tb * 512
                  k.dma('sp', hblk[:], h_src[t0:t0 + 512, :].rearrange("(i p) d -> p i d", p=128), wr=[h_hblk])
                  for c in range(4):
                      k.dma('sp', yT[:, c * 8:(c + 1) * 8, :], yT_d[c * 1024:(c + 1) * 1024, t0:t0 + 512].rearrange("(c p) t -> p c t", p=128), wr=[h_yT])
                  tail_block(layer, tb, hblk, h_hblk, yT, h_yT, 32, ewo_d[j], tl)
                  k.dma('sp', h_dst[t0:t0 + 512, :].rearrange("(i p) d -> p i d", p=128), hblk[:], rd=[h_hblk])
              k.barrier()

    for layer in layer_list:
        j = layer // 2
        h_src = x_d if layer == layer_list[0] else hbuf_d
        h_dst = y_d if layer == layer_list[-1] else hbuf_d
        if layer % 2 == 0:
            even_layer(layer, j, h_src, h_dst)
        else:
            odd_layer(layer, j, h_src, h_dst)

    k.barrier()
    es.close()
    return nc


def _prep_inputs(inp):
    f = lambda a: np.ascontiguousarray(np.asarray(a, dtype=np.float32))
    W = {kk: f(v) for kk, v in inp.items()}
    rng = np.arange
    shared = {}
    shared["norm_gain"] = W["norm_gain"]
    shared["pe_gate_norm"] = W["pe_gate_norm"]
    shared["consts"] = np.eye(128, dtype=np.float32)
    s_ = rng(128)[:, None]
    t_ = rng(128)[None, :]
    oh = np.zeros((32, 2, 128, 128), np.float32)
    for kind in range(2):
        bk = t5_bucket_np((s_ - t_ - 128 * kind).astype(np.int32))
        for b in range(32):
            oh[b, kind] = (bk == b).astype(np.float32)
    oh[15] -= 1.0
    shared["oh"] = oh.reshape(32, -1)
    shared["rel_bias"] = W["rel_bias"]
    cidx = np.concatenate([rng(0, 2048), rng(2048, 4096), rng(4096, 6144), rng(6144, 6656), rng(7168, 9216),
                           rng(9216, 10240), rng(10240, 10304), rng(10240, 10304)])
    ewf = np.empty((2, 77, 128, 2048), np.float32)
    ewv = np.empty((2, 128, 8192), np.float32)
    ewiw = np.empty((2, 128, 256), np.float32)
    ewo = np.empty((2, 4, 128, 32 * 512), np.float32)
    lwr = np.empty((2, 128, 2048), np.float32)
    lwi = np.empty((2, 128, 2048), np.float32)
    evec = np.empty((2, 128, 8, 16), np.float32)
    qkn = np.empty((2, 128, 2), np.float32)
    owf = np.empty((2, 32, 128, 2048), np.float32)
    owv = np.empty((2, 4, 128, 8192), np.float32)
    owo = np.empty((2, 4, 128, 8192), np.float32)
    wst = np.empty((2, 128, 2048), np.float32)
    ovec = np.empty((2, 128, 32), np.float32)
    for j in range(2):
        Wi = W["even_w_in"][j]
        ewf[j] = Wi[:, cidx].reshape(16, 128, 77, 128).transpose(2, 1, 0, 3).reshape(77, 128, 2048)
        ewv[j] = Wi[:, 6656:7168].reshape(16, 128, 512).transpose(1, 0, 2).reshape(128, 8192)
        ewiw[j] = Wi[:, 10304:10320].reshape(16, 128, 16).transpose(1, 0, 2).reshape(128, 256)
        ewo[j] = W["even_w_out"][j].reshape(32, 128, 4, 512).transpose(2, 1, 0, 3).reshape(4, 128, 32 * 512)
        lwr[j] = W["lru_w_r"][j].transpose(1, 0, 2).reshape(128, 2048)
        lwi[j] = W["lru_w_i"][j].transpose(1, 0, 2).reshape(128, 2048)
        pc = lambda v: v.reshape(16, 128).T
        for a in range(4):
            evec[j, :, a, :] = pc(W["conv_w"][j, a])
        evec[j, :, 4, :] = pc(W["conv_b"][j])
        evec[j, :, 5, :] = pc(W["lru_b_r"][j].reshape(-1))
        evec[j, :, 6, :] = pc(W["lru_b_i"][j].reshape(-1))
        evec[j, :, 7, :] = pc(W["lru_lambda"][j])
        qkn[j, :, 0] = W["q_norm"][j]
        qkn[j, :, 1] = W["k_norm"][j]
        Wo = W["odd_w_in"][j]
        oidx = np.concatenate([rng(0, 2048), rng(4096, 6144)])
        owf[j] = Wo[:, oidx].reshape(16, 128, 32, 128).transpose(2, 1, 0, 3).reshape(32, 128, 2048)
        owv[j] = Wo[:, 2048:4096].reshape(16, 128, 4, 512).transpose(2, 1, 0, 3).reshape(4, 128, 8192)
        owo[j] = W["odd_w_out"][j].reshape(16, 128, 4, 512).transpose(2, 1, 0, 3).reshape(4, 128, 8192)
        wst[j] = W["sg_w_s"][j].transpose(2, 0, 1).reshape(128, 2048)
        ovec[j, :, 0:16] = pc(W["sg_ln_g"][j])
        ovec[j, :, 16:32] = pc(W["sg_ln_b"][j])
    shared.update(ewf=ewf, ewv=ewv, ewiw=ewiw, ewo=ewo, lwr=lwr, lwi=lwi, evec=evec.reshape(2, 128, 128), qkn=qkn,
                  owf=owf, owv=owv, owo=owo, wst=wst, ovec=ovec, bs=W["sg_b_s"])
    shared["pew"] = np.ascontiguousarray(W["pe_w"].reshape(4, 2, 128, 4, 512).transpose(0, 3, 2, 1, 4).reshape(4, 4, 128, 1024))
    shared["pwg"] = np.ascontiguousarray(W["pe_w_gate"].reshape(4, 16, 128, 4, 512).transpose(0, 3, 2, 1, 4).reshape(4, 4, 128, 8192))
    shared = {kk: np.ascontiguousarray(v, dtype=np.float32) for kk, v in shared.items()}
    maps = []
    for b in range(2):
        m = dict(shared)
        m["x"] = np.ascontiguousarray(W["x"][b])
        m["p"] = np.ascontiguousarray(W["p"][:, b])
        maps.append(m)
    return maps


def kernel(**inputs):
    ll = os.environ.get("KLAYERS")
    layer_list = [int(v) for v in ll.split(",")] if ll else [0, 1, 2, 3]
    maps = _prep_inputs(inputs)
    nc = build_program(layer_list)
    res = run_bass_kernel_spmd(nc, maps, core_ids=[0, 1], trace=bool(os.environ.get('KTRACE')))
    if os.environ.get('KTRACE'):
        print('EXEC_TIME_NS', res.exec_time_ns, flush=True)
    out = np.stack([np.asarray(res.results[b]["y"], dtype=np.float32) for b in range(2)], axis=0)
    if os.environ.get("KDEBUG"):
        global LAST_RESULTS
        LAST_RESULTS = res.results
    return out
```

```python
import os
from contextlib import ExitStack
import numpy as np
import concourse.bass as bass
import concourse.mybir as mybir
from concourse.bass_utils import run_bass_kernel_spmd

F32 = mybir.dt.float32
BF16 = mybir.dt.bfloat16
AF = mybir.ActivationFunctionType
ALU = mybir.AluOpType
AX = mybir.AxisListType

T = 4096
D = 2048
NT = T // 128
EPS = 1e-6
NEG = -1.0e30
NIT = 16
NDS = 12
EPOCH_LIMIT = 9000
SAME = {'pe': False, 'act': True, 'dve': True, 'pool': True, 'sp': False}


class H:
    __slots__ = ('w', 'r')

    def __init__(self):
        self.w = None
        self.r = {}


class K:
    def __init__(self, nc, es):
        self.nc = nc
        self.E = {'pe': nc.tensor, 'act': nc.scalar, 'dve': nc.vector, 'pool': nc.gpsimd, 'sp': nc.sync}
        self.sems = [{k: es.enter_context(nc.semaphore(f"e{s}_{k}")) for k in self.E} for s in range(3)]
        self.dsems = [[es.enter_context(nc.semaphore(f"d{s}_{i}")) for i in range(NDS)] for s in range(3)]
        self.ep = 0
        self.cnt = {k: 0 for k in self.E}
        self.dval = [0] * NDS
        self.dnext = 0
        self.seen = {k: {} for k in self.E}
        self.handles = []

    def h(self):
        x = H()
        self.handles.append(x)
        return x

    def hs(self, n):
        return [self.h() for _ in range(n)]

    def _wait(self, e, key, val):
        if key[0] == 'e' and key[1] == e and not SAME[e]:
            return
        if self.seen[e].get(key, 0) >= val:
            return
        s = self.ep % 3
        sem = self.sems[s][key[1]] if key[0] == 'e' else self.dsems[s][key[1]]
        self.E[e].wait_ge(sem, val)
        self.seen[e][key] = val

    def _deps(self, e, rd, wr):
        d = {}
        for h in rd:
            if h.w is not None:
                d[h.w[0]] = max(d.get(h.w[0], 0), h.w[1])
        for h in wr:
            if h.w is not None:
                d[h.w[0]] = max(d.get(h.w[0], 0), h.w[1])
            for kk, v in h.r.items():
                d[kk] = max(d.get(kk, 0), v)
        for kk, v in d.items():
            self._wait(e, kk, v)

    def _mark(self, tok, rd, wr):
        for h in wr:
            h.w = tok
            h.r = {}
        for h in rd:
            if h not in wr:
                h.r[tok[0]] = max(h.r.get(tok[0], 0), tok[1])

    def op(self, e, fn, rd=(), wr=()):
        self._deps(e, rd, wr)
        ins = fn(self.E[e])
        self.cnt[e] += 1
        ins.then_inc(self.sems[self.ep % 3][e], 1)
        self._mark((('e', e), self.cnt[e]), rd, wr)

    def dma(self, q, out, in_, rd=(), wr=()):
        i = self.dnext
        self.dnext = (i + 1) % NDS
        if self.dval[i] > 0:
            self._wait(q, ('d', i), self.dval[i])
        self._deps(q, rd, wr)
        ins = self.E[q].dma_start(out=out, in_=in_)
        self.dval[i] += 16
        ins.then_inc(self.dsems[self.ep % 3][i], 16)
        self._mark((('d', i), self.dval[i]), rd, wr)

    def barrier(self):
        for e in self.E:
            for o in self.E:
                if o != e and self.cnt[o] > 0:
                    self._wait(e, ('e', o), self.cnt[o])
            for i in range(NDS):
                if self.dval[i] > 0:
                    self._wait(e, ('d', i), self.dval[i])
        self.ep += 1
        self.cnt = {k: 0 for k in self.E}
        self.dval = [0] * NDS
        self.seen = {k: {} for k in self.E}
        for h in self.handles:
            h.w = None
            h.r = {}
        if self.ep >= 2:
            o = (self.ep + 1) % 3
            for e in ('pe', 'act', 'dve', 'pool', 'sp'):
                self.op(e, lambda eng, e=e: eng.sem_clear(self.sems[o][e]))
            for i in range(NDS):
                self.op('sp', lambda eng, i=i: eng.sem_clear(self.dsems[o][i]))

    def maybe_barrier(self):
        if max(self.cnt.values()) > EPOCH_LIMIT or max(self.dval) > EPOCH_LIMIT * 16:
            self.barrier()


_UN = [0]


def un(name):
    _UN[0] += 1
    return f"{name}_u{_UN[0]}"


def t5_bucket_np(rel):
    half = 16
    max_exact = 8
    ret = np.where(rel > 0, half, 0)
    n = np.abs(rel)
    nf = np.maximum(n, 1).astype(np.float32)
    large = max_exact + (np.log(nf / max_exact) / np.float32(np.log(128 / max_exact)) * (half - max_exact)).astype(np.int32)
    large = np.minimum(large, half - 1)
    return ret + np.where(n < max_exact, n, large)


def build_program(layer_list):
    nc = bass.Bass("TRN2", target_bir_lowering=False)
    es = ExitStack()
    k = K(nc, es)

    def din(name, shape, dt=F32):
        return nc.dram_tensor(name, list(shape), dt, kind="ExternalInput").ap()

    DBG = bool(os.environ.get("KDEBUG"))
    PH = os.environ.get("KPHASES", "E1,E2,E3,E4").split(",")

    def dscr(name, shape, dt):
        return nc.dram_tensor(name, list(shape), dt, kind=("ExternalOutput" if DBG else "Internal")).ap()

    x_d = din("x", [T, D])
    p_d = din("p", [4, T, 256])
    ng_d = din("norm_gain", [4, D])
    pgn_d = din("pe_gate_norm", [4, D])
    consts_d = din("consts", [128, 128])
    oh_d = din("oh", [32, 2 * 128 * 128])
    rb_d = din("rel_bias", [32, 16])
    ewf_d = din("ewf", [2, 77, 128, 16 * 128])
    ewv_d = din("ewv", [2, 128, 16 * 512])
    ewiw_d = din("ewiw", [2, 128, 16 * 16])
    ewo_d = din("ewo", [2, 4, 128, 32 * 512])
    lwr_d = din("lwr", [2, 128, 16 * 128])
    lwi_d = din("lwi", [2, 128, 16 * 128])
    evec_d = din("evec", [2, 128, 8 * 16])
    qkn_d = din("qkn", [2, 128, 2])
    owf_d = din("owf", [2, 32, 128, 16 * 128])
    owv_d = din("owv", [2, 4, 128, 16 * 512])
    owo_d = din("owo", [2, 4, 128, 16 * 512])
    wst_d = din("wst", [2, 128, 16 * 128])
    ovec_d = din("ovec", [2, 128, 2 * 16])
    bs_d = din("bs", [2, 16, 128])
    pew_d = din("pew", [4, 4, 128, 2 * 512])
    pwg_d = din("pwg", [4, 4, 128, 16 * 512])
    y_d = nc.dram_tensor("y", [T, D], F32, kind="ExternalOutput").ap()

    hbuf_d = dscr("hbuf", [T, D], F32)
    xa_d = dscr("xaT", [2048, T], F32)
    sga_d = dscr("sgaT", [2048, T], BF16)
    sgb_d = dscr("sgbT", [2048, T], BF16)
    q_d = dscr("qT", [2048, T], BF16)
    kk_d = dscr("kT", [512, T], BF16)
    iq_d = dscr("iqT", [1024, T], BF16)
    ik_d = dscr("ikT", [128, T], BF16)
    v_d = dscr("vaug", [T, 4 * 130], BF16)
    iw_d = dscr("iw", [T, 16], F32)
    yT_d = dscr("yT", [4096, T], BF16)

    sb = lambda name, shape, dt: es.enter_context(nc.sbuf_tensor(un(name), list(shape), dt))
    ps = lambda name, shape, dt: es.enter_context(nc.psum_tensor(un(name), list(shape), dt))

    ident_f = sb("ident_f", [128, 128], F32); h_ident = k.h()
    ident_b = sb("ident_b", [128, 128], BF16)
    onesm = sb("onesm", [128, 128], F32)
    ones1 = sb("ones1", [128, 128], F32)
    diagneg = sb("diagneg", [128, 128], F32)
    epsc = sb("epsc", [128, 1], F32)
    onec = sb("onec", [128, 1], F32)
    EB = sb("EB", [128, 2 * 16 * 128], BF16)
    h_c = k.h()

    k.dma('sp', ident_f[:], consts_d, wr=[h_c])
    k.op('dve', lambda e: e.tensor_copy(out=ident_b[:], in_=ident_f[:]), rd=[h_c], wr=[h_c])
    k.op('dve', lambda e: e.memset(onesm[:], 1.0 / 128), wr=[h_c])
    k.op('dve', lambda e: e.memset(ones1[:], 1.0), wr=[h_c])
    k.op('dve', lambda e: e.memset(diagneg[:], 0.0), wr=[h_c])
    k.op('dve', lambda e: e.memset(diagneg[0:64, 64:128], NEG), wr=[h_c])
    k.op('dve', lambda e: e.memset(epsc[:], EPS), wr=[h_c])
    k.op('dve', lambda e: e.memset(onec[:], 1.0), wr=[h_c])

    with nc.sbuf_tensor("oh_s0", [32, 2 * 128 * 128], F32) as oh_s, nc.sbuf_tensor("rb_s0", [32, 16], F32) as rb_s, \
            nc.psum_tensor("bps", [128, 512], F32) as bps:
        h_oh, h_bps = k.h(), k.h()
        k.dma('sp', oh_s[:], oh_d, wr=[h_oh])
        k.dma('sp', rb_s[:], rb_d, wr=[h_oh])
        EBv = EB[:].rearrange("s (k h t) -> s k h t", k=2, h=16)
        ohv = oh_s[:].rearrange("b (k s t) -> b k s t", k=2, s=128)
        for kind in range(2):
            for t0 in range(0, 128, 32):
                for tt in range(32):
                    k.op('pe', lambda e, kind=kind, t=t0 + tt, tt=tt: e.matmul(
                        bps[:, tt * 16:(tt + 1) * 16], lhsT=ohv[:, kind, :, t], rhs=rb_s[:], start=True, stop=True),
                        rd=[h_oh], wr=[h_bps])
                k.op('act', lambda e, kind=kind, t0=t0: e.activation(
                    out=EBv[:, kind, :, t0:t0 + 32].rearrange("s h t -> s t h"),
                    in_=bps[:].rearrange("s (t h) -> s t h", h=16), func=AF.Exp), rd=[h_bps], wr=[h_c])
        k.barrier()

    def rms_to_T(htile, gain_bc, dstT, col0, h_in, h_dst, bufs):
        junk, ss, sd, rs, hp, tp, h_j, h_hp, h_tp = bufs
        k.op('act', lambda e: e.activation(out=junk[:], in_=htile, func=AF.Square, accum_out=ss[:, 0:1]),
             rd=[h_in], wr=[h_j])
        k.op('act', lambda e: e.activation(out=sd[:, 0:1], in_=ss[:, 0:1], func=AF.Sqrt, scale=1.0 / D, bias=epsc[:, 0:1]),
             rd=[h_j, h_c], wr=[h_j])
        k.op('dve', lambda e: e.reciprocal(out=rs[:, 0:1], in_=sd[:, 0:1]), rd=[h_j], wr=[h_j])
        k.op('dve', lambda e: e.scalar_tensor_tensor(out=hp[:], in0=htile, scalar=rs[:, 0:1], in1=gain_bc,
                                                      op0=ALU.mult, op1=ALU.mult), rd=[h_in, h_j, h_c], wr=[h_hp])
        for g4 in range(4):
            for j in range(4):
                dc = g4 * 4 + j
                k.op('pe', lambda e, dc=dc, j=j: e.transpose(out=tp[:, j * 128:(j + 1) * 128],
                                                              in_=hp[:, dc * 128:(dc + 1) * 128], identity=ident_b[:]),
                     rd=[h_hp, h_c], wr=[h_tp])
            k.op('act', lambda e, g4=g4: e.activation(out=dstT[:, g4 * 4:(g4 + 1) * 4, col0:col0 + 128],
                                                       in_=tp[:].rearrange("p (j t) -> p j t", j=4), func=AF.Copy),
                 rd=[h_tp], wr=[h_dst])

    state = {'wi': 0, 'ci': 0}

    def cast(out_ap, in_ap, rd, wr):
        state['ci'] += 1
        if state['ci'] % 2 == 0:
            k.op('act', lambda e: e.activation(out=out_ap, in_=in_ap, func=AF.Copy), rd=rd, wr=wr)
        else:
            k.op('dve', lambda e: e.tensor_copy(out=out_ap, in_=in_ap), rd=rd, wr=wr)

    def load_w(src_ap, n, ST, WB, h_st, h_wb, view=None):
        i = state['wi'] % 2
        state['wi'] += 1
        k.dma('sp', ST[i][:, 0:n], src_ap, wr=[h_st[i]])
        cast(WB[i][:, 0:n], ST[i][:, 0:n], [h_st[i]], [h_wb[i]])
        return WB[i], h_wb[i]

    def load_big(src2d, n, dst, h_dst_, ST, h_st):
        for c0 in range(0, n, 2048):
            w = min(2048, n - c0)
            i = state['wi'] % 2
            state['wi'] += 1
            k.dma('sp', ST[i][:, 0:w], src2d[:, c0:c0 + w], wr=[h_st[i]])
            cast(dst[:, c0:c0 + w], ST[i][:, 0:w], [h_st[i]], [h_dst_])

    def tail_block(layer, tb, hblk, h_hblk, yT, h_yT, CC, wo_src, tl):
        (ST, WB, h_st, h_wb, WO, h_wo, hgT, h_hgT, normbufs, pa, pb, h_pa, h_pb, pt_s, ptb, pT, h_pt, h_pT,
         sg, ge, h_sg, gbc) = tl
        for dblk in range(4):
            wo = WO[dblk % len(WO)]
            hwo = h_wo[dblk % len(WO)]
            load_big(wo_src[dblk], CC * 512, wo, hwo, ST, h_st)
            for tile in range(4):
                pp, hp_ = (pa, h_pa) if tile % 2 == 0 else (pb, h_pb)
                for cc in range(CC):
                    k.op('pe', lambda e, cc=cc, tile=tile, pp=pp, wo=wo: e.matmul(
                        pp[:], lhsT=yT[:, cc, tile * 128:(tile + 1) * 128], rhs=wo[:, cc * 512:(cc + 1) * 512],
                        start=(cc == 0), stop=(cc == CC - 1)), rd=[h_yT, hwo], wr=[hp_])
                k.op('dve', lambda e, tile=tile, dblk=dblk, pp=pp: e.tensor_tensor(
                    out=hblk[:, tile, dblk * 512:(dblk + 1) * 512], in0=pp[:], in1=hblk[:, tile, dblk * 512:(dblk + 1) * 512],
                    op=ALU.add), rd=[hp_, h_hblk], wr=[h_hblk])
        for tile in range(4):
            rms_to_T(hblk[:, tile, :], gbc[:], hgT, tile * 128, h_hblk, h_hgT, normbufs)
            k.dma('sp', pt_s[:], p_d[layer, tb * 512 + tile * 128: tb * 512 + (tile + 1) * 128, :], wr=[h_pt])
            k.op('pool', lambda e: e.tensor_copy(out=ptb[:], in_=pt_s[:]), rd=[h_pt], wr=[h_pt])
            tp, h_tp = normbufs[5], normbufs[8]
            for c in range(2):
                k.op('pe', lambda e, c=c: e.transpose(out=tp[:, c * 128:(c + 1) * 128], in_=ptb[:, c * 128:(c + 1) * 128],
                                                      identity=ident_b[:]), rd=[h_pt, h_c], wr=[h_tp])
            k.op('act', lambda e, tile=tile: e.activation(out=pT[:, :, tile * 128:(tile + 1) * 128],
                                                          in_=tp[:, 0:256].rearrange("p (j t) -> p j t", j=2), func=AF.Copy),
                 rd=[h_tp], wr=[h_pT])
        for dblk in range(4):
            load_big(pwg_d[layer, dblk], 8192, WB[0], h_wb[0], ST, h_st)
            load_big(pew_d[layer, dblk], 1024, WB[1], h_wb[1], ST, h_st)
            for tile in range(4):
                for dc in range(16):
                    k.op('pe', lambda e, dc=dc, tile=tile: e.matmul(
                        pa[:], lhsT=hgT[:, dc, tile * 128:(tile + 1) * 128], rhs=WB[0][:, dc * 512:(dc + 1) * 512],
                        start=(dc == 0), stop=(dc == 15)), rd=[h_hgT, h_wb[0]], wr=[h_pa])
                for c in range(2):
                    k.op('pe', lambda e, c=c, tile=tile: e.matmul(
                        pb[:], lhsT=pT[:, c, tile * 128:(tile + 1) * 128], rhs=WB[1][:, c * 512:(c + 1) * 512],
                        start=(c == 0), stop=(c == 1)), rd=[h_pT, h_wb[1]], wr=[h_pb])
                k.op('act', lambda e: e.activation(out=sg[:], in_=pa[:], func=AF.Sigmoid), rd=[h_pa], wr=[h_sg])
                k.op('dve', lambda e: e.tensor_tensor(out=ge[:], in0=pb[:], in1=sg[:], op=ALU.mult), rd=[h_pb, h_sg], wr=[h_sg])
                k.op('dve', lambda e, tile=tile, dblk=dblk: e.tensor_tensor(
                    out=hblk[:, tile, dblk * 512:(dblk + 1) * 512], in0=ge[:], in1=hblk[:, tile, dblk * 512:(dblk + 1) * 512],
                    op=ALU.add), rd=[h_sg, h_hblk], wr=[h_hblk])


    def alloc_tail(ls, CC):
        sbl = lambda name, shape, dt: ls.enter_context(nc.sbuf_tensor(un(name), list(shape), dt))
        psl = lambda name, shape, dt: ls.enter_context(nc.psum_tensor(un(name), list(shape), dt))
        ST = [sbl(f"ST{i}", [128, 2048], F32) for i in range(2)]
        WB = [sbl("WB0", [128, 8192], BF16), sbl("WB1", [128, 2048], BF16)]
        WO = [sbl("WO0", [128, CC * 512], BF16)]
        hgT = sbl("hgT", [128, 16, 512], BF16)
        ss = sbl("ss", [128, 1], F32); sd = sbl("sd", [128, 1], F32); rs = sbl("rs", [128, 1], F32)
        hp = sbl("hp", [128, 2048], BF16)
        tp = psl("tp", [128, 512], BF16)
        h_hp_ = k.h()
        normbufs = (hp, ss, sd, rs, hp, tp, h_hp_, h_hp_, k.h())
        pa = psl("pa", [128, 512], F32); pb = psl("pb", [128, 512], F32)
        pt_s = sbl("pt_s", [128, 256], F32); ptb = sbl("ptb", [128, 256], BF16)
        pT = sbl("pT", [128, 2, 512], BF16)
        sg = sbl("sg", [128, 512], F32); ge = sbl("ge", [128, 512], F32)
        gbc = sbl("gbc", [128, 2048], F32)
        return (ST, WB, k.hs(2), k.hs(2), WO, k.hs(1), hgT, k.h(), normbufs, pa, pb, k.h(), k.h(), pt_s, ptb, pT,
                k.h(), k.h(), sg, ge, k.h(), gbc)

    def odd_layer(layer, j, h_src, h_dst):
        with ExitStack() as ls:
            sbl = lambda name, shape, dt: ls.enter_context(nc.sbuf_tensor(un(name), list(shape), dt))
            psl = lambda name, shape, dt: ls.enter_context(nc.psum_tensor(un(name), list(shape), dt))
            tl = alloc_tail(ls, 16)
            ST, WB, h_st, h_wb = tl[0], tl[1], tl[2], tl[3]
            normbufs = tl[8]
            pa, pb, h_pa, h_pb = tl[9], tl[10], tl[11], tl[12]
            gbc = tl[21]
            hblk = sbl("hblk", [128, 4, 2048], F32); h_hblk = k.h()
            hnT = tl[6]; h_hnT = tl[7]
            usg = sbl("usg", [128, 16, 512], BF16); h_usg = k.h()
            tmpb = sbl("tmpb", [128, 512], BF16); h_tmpb = k.h()
            WC = [sbl(f"WC{i}", [128, 2048], BF16) for i in range(2)]; h_wc = k.hs(2)
            gv = sbl("gv", [128, 4, 2048], F32); h_gv = k.h()
            z = normbufs[4]; h_z = normbufs[7]
            yT = usg; h_yT = h_usg
            nbc = sbl("nbc", [128, 2048], F32)
            ovec = sbl("ovec", [128, 32], F32)
            wsT_f = gv[:, 0, :]
            wsT_b = sbl("wsT_b", [128, 16, 128], BF16)
            bsbc = gv[:, 1, :]
            Cg = sbl("Cg", [128, 16, 128], F32)
            st6 = sbl("st6", [128, 4, 6], F32); mv = sbl("mv", [128, 2], F32); h_st6 = k.h()
            lsd = sbl("lsd", [128, 1], F32); lrs = sbl("lrs", [128, 1], F32); nmr = sbl("nmr", [128, 1], F32)
            mx = sbl("mx", [128, 128], F32); h_mx = k.h()
            mps = psl("mps", [128, 512], F32); h_mps = k.h()
            h_lc = k.h()
            k.dma('sp', nbc[:], ng_d[layer, :].partition_broadcast(128), wr=[h_c])
            k.dma('sp', gbc[:], pgn_d[layer, :].partition_broadcast(128), wr=[h_c])
            k.dma('sp', ovec[:], ovec_d[j], wr=[h_lc])
            k.dma('sp', wsT_f, wst_d[j], wr=[h_lc])
            for g in range(16):
                k.dma('sp', bsbc[:, g * 128:(g + 1) * 128], bs_d[j, g, :].partition_broadcast(128), wr=[h_lc])
            wv = wsT_f.rearrange("s (g t) -> s g t", g=16)
            k.op('dve', lambda e: e.memset(wv[64:128, :, 0:64], 0.0), rd=[h_lc], wr=[h_lc])
            k.op('dve', lambda e: e.tensor_copy(out=wsT_b[:], in_=wv), rd=[h_lc], wr=[h_lc])
            for g in range(16):
                k.op('pe', lambda e, g=g: e.matmul(mps[:, 0:128], lhsT=ones1[:], rhs=wv[:, g, :], start=True, stop=True),
                     rd=[h_lc, h_c], wr=[h_mps])
                k.op('dve', lambda e, g=g: e.scalar_tensor_tensor(out=Cg[:, g, :], in0=mps[:, 0:128], scalar=ovec[:, 16 + g:17 + g],
                                                                   in1=bsbc[:, g * 128:(g + 1) * 128], op0=ALU.mult, op1=ALU.add),
                     rd=[h_mps, h_lc], wr=[h_lc])
            k.barrier()
            for tb in range(8):
                k.maybe_barrier()
                t0 = tb * 512
                k.dma('sp', hblk[:], h_src[t0:t0 + 512, :].rearrange("(i p) d -> p i d", p=128), wr=[h_hblk])
                for tile in range(4):
                    rms_to_T(hblk[:, tile, :], nbc[:], hnT, tile * 128, h_hblk, h_hnT, normbufs)
                for ch in range(32):
                    wb, hwb = load_w(owf_d[j, ch], 2048, ST, WC, h_st, h_wc)
                    pp, hp_ = (pa, h_pa) if ch % 2 == 0 else (pb, h_pb)
                    for dc in range(16):
                        k.op('pe', lambda e, dc=dc, wb=wb, pp=pp: e.matmul(pp[:], lhsT=wb[:, dc * 128:(dc + 1) * 128], rhs=hnT[:, dc, :],
                                                                           start=(dc == 0), stop=(dc == 15)), rd=[hwb, h_hnT], wr=[hp_])
                    if ch < 16:
                        k.op('act', lambda e, ch=ch, pp=pp: e.activation(out=usg[:, ch, :], in_=pp[:], func=AF.Gelu_apprx_tanh),
                             rd=[hp_], wr=[h_usg])
                    else:
                        k.op('act', lambda e, pp=pp: e.activation(out=tmpb[:], in_=pp[:], func=AF.Silu), rd=[hp_], wr=[h_tmpb])
                        k.op('dve', lambda e, ch=ch: e.tensor_tensor(out=usg[:, ch - 16, :], in0=usg[:, ch - 16, :], in1=tmpb[:], op=ALU.mult),
                             rd=[h_tmpb, h_usg], wr=[h_usg])
                for vb in range(4):
                    load_big(owv_d[j, vb], 8192, WB[0], h_wb[0], ST, h_st)
                    for tile in range(4):
                        pp, hp_ = (pa, h_pa) if tile % 2 == 0 else (pb, h_pb)
                        for dc in range(16):
                            k.op('pe', lambda e, dc=dc, tile=tile, pp=pp: e.matmul(
                                pp[:], lhsT=hnT[:, dc, tile * 128:(tile + 1) * 128], rhs=WB[0][:, dc * 512:(dc + 1) * 512],
                                start=(dc == 0), stop=(dc == 15)), rd=[h_wb[0], h_hnT], wr=[hp_])
                        k.op('act', lambda e, tile=tile, vb=vb, pp=pp: e.activation(out=gv[:, tile, vb * 512:(vb + 1) * 512], in_=pp[:],
                                                                                   func=AF.Gelu_apprx_tanh), rd=[hp_], wr=[h_gv])
                for tile in range(4):
                    for q4 in range(4):
                        k.op('dve', lambda e, q4=q4, tile=tile: e.bn_stats(out=st6[:, q4, :], in_=gv[:, tile, q4 * 512:(q4 + 1) * 512]),
                             rd=[h_gv], wr=[h_st6])
                    k.op('dve', lambda e: e.bn_aggr(out=mv[:], in_=st6[:].rearrange("p a b -> p (a b)")), rd=[h_st6], wr=[h_st6])
                    k.op('act', lambda e: e.activation(out=lsd[:], in_=mv[:, 1:2], func=AF.Sqrt, bias=epsc[:, 0:1]), rd=[h_st6, h_c], wr=[h_st6])
                    k.op('dve', lambda e: e.reciprocal(out=lrs[:], in_=lsd[:]), rd=[h_st6], wr=[h_st6])
                    k.op('dve', lambda e: e.scalar_tensor_tensor(out=nmr[:], in0=mv[:, 0:1], scalar=-1.0, in1=lrs[:], op0=ALU.mult, op1=ALU.mult),
                         rd=[h_st6], wr=[h_st6])
                    k.op('dve', lambda e, tile=tile: e.tensor_scalar(out=z[:], in0=gv[:, tile, :], scalar1=lrs[:, 0:1], scalar2=nmr[:, 0:1],
                                                                      op0=ALU.mult, op1=ALU.add), rd=[h_gv, h_st6], wr=[h_z])
                    for g in range(16):
                        k.op('pe', lambda e, g=g: e.matmul(mps[:, (g % 4) * 128:(g % 4 + 1) * 128], lhsT=z[:, g * 128:(g + 1) * 128],
                                                            rhs=wsT_b[:, g, :], start=True, stop=True), rd=[h_z, h_lc], wr=[h_mps])
                        k.op('dve', lambda e, g=g: e.scalar_tensor_tensor(out=mx[:], in0=mps[:, (g % 4) * 128:(g % 4 + 1) * 128],
                                                                           scalar=ovec[:, g:g + 1], in1=Cg[:, g, :], op0=ALU.mult, op1=ALU.add),
                             rd=[h_mps, h_lc], wr=[h_mx])
                        k.op('dve', lambda e, g=g, tile=tile: e.tensor_tensor(out=yT[:, g, tile * 128:(tile + 1) * 128], in0=mx[:],
                                                                              in1=usg[:, g, tile * 128:(tile + 1) * 128], op=ALU.mult),
                             rd=[h_mx, h_usg], wr=[h_yT])
                tail_block(layer, tb, hblk, h_hblk, yT, h_yT, 16, owo_d[j], tl)
                k.dma('sp', h_dst[t0:t0 + 512, :].rearrange("(i p) d -> p i d", p=128), hblk[:], rd=[h_hblk])
            k.barrier()


    def even_layer(layer, j, h_src, h_dst):
        CH = ([('xa', i) for i in range(16)] + [('ga', i) for i in range(16)] + [('q', i) for i in range(16)] +
              [('k', i) for i in range(4)] + [('gb', i) for i in range(16)] + [('iq', i) for i in range(8)] + [('ik', 0)])
        with ExitStack() as ls:
          if 'E1' in PH:
              sbl = lambda name, shape, dt: ls.enter_context(nc.sbuf_tensor(un(name), list(shape), dt))
              psl = lambda name, shape, dt: ls.enter_context(nc.psum_tensor(un(name), list(shape), dt))
              ST = [sbl(f"ST{i}", [128, 2048], F32) for i in range(2)]; h_st = k.hs(2)
              WB = [sbl("WB0", [128, 8192], BF16), sbl("WB1", [128, 2048], BF16)]; h_wb = k.hs(2)
              WC = [sbl(f"WC{i}", [128, 2048], BF16) for i in range(2)]; h_wc = k.hs(2)
              hnT = sbl("hnT", [128, 16, 1024], BF16); h_hnT = k.h()
              ht = [sbl(f"ht{i}", [128, 2048], F32) for i in range(2)]; h_ht = k.hs(2)
              ss = sbl("ss", [128, 1], F32); sd = sbl("sd", [128, 1], F32); rs = sbl("rs", [128, 1], F32)
              hp = sbl("hp", [128, 2048], BF16)
              tp = psl("tp", [128, 512], BF16)
              h_hp_ = k.h()
              normbufs = (hp, ss, sd, rs, hp, tp, h_hp_, h_hp_, k.h())
              pa = psl("pa", [128, 512], F32); pb = psl("pb", [128, 512], F32); h_pa, h_pb = k.h(), k.h()
              pm = psl("pm", [128, 512], F32); h_pm = k.h()
              nbc = sbl("nbc", [128, 2048], F32)
              qkn = sbl("qkn", [128, 2], F32); qg = sbl("qg", [128, 2], F32)
              of = [sbl(f"of{i}", [128, 512], F32) for i in range(2)]
              ob = [sbl(f"ob{i}", [128, 512], BF16) for i in range(2)]; h_o = k.hs(2)
              sq = sbl("sq", [128, 512], F32); rsd = sbl("rsd", [128, 512], F32); h_sq = k.h()
              vt = sbl("vt", [128, 4, 130], BF16); h_vt = k.h()
              iwt = sbl("iwt", [128, 16], F32); h_iwt = k.h()
              h_lc = k.h()
              k.dma('sp', nbc[:], ng_d[layer, :].partition_broadcast(128), wr=[h_c])
              k.dma('sp', qkn[:], qkn_d[j], wr=[h_lc])
              k.op('dve', lambda e: e.tensor_scalar(out=qg[:, 0:1], in0=qkn[:, 0:1], scalar1=float(128 ** -0.5), scalar2=None, op0=ALU.mult),
                   rd=[h_lc], wr=[h_lc])
              k.op('dve', lambda e: e.tensor_copy(out=qg[:, 1:2], in_=qkn[:, 1:2]), rd=[h_lc], wr=[h_lc])
              k.op('dve', lambda e: e.memset(vt[:], 1.0), wr=[h_vt])
              dst = {'xa': xa_d, 'ga': sga_d, 'gb': sgb_d, 'q': q_d, 'k': kk_d, 'iq': iq_d, 'ik': ik_d}
              oi = 0
              for blk in range(4):
                  k.maybe_barrier()
                  T0 = blk * 1024
                  for tile in range(8):
                      i = tile % 2
                      k.dma('sp', ht[i][:], h_src[T0 + tile * 128:T0 + (tile + 1) * 128, :], wr=[h_ht[i]])
                      rms_to_T(ht[i][:], nbc[:], hnT, tile * 128, h_ht[i], h_hnT, normbufs)
                  for ci, (kind, idx) in enumerate(CH):
                      wb, hwb = load_w(ewf_d[j, ci], 2048, ST, WC, h_st, h_wc)
                      for half in range(2):
                          pp, hp_ = (pa, h_pa) if half == 0 else (pb, h_pb)
                          for dc in range(16):
                              k.op('pe', lambda e, dc=dc, wb=wb, pp=pp, half=half: e.matmul(
                                  pp[:], lhsT=wb[:, dc * 128:(dc + 1) * 128], rhs=hnT[:, dc, half * 512:(half + 1) * 512],
                                  start=(dc == 0), stop=(dc == 15)), rd=[hwb, h_hnT], wr=[hp_])
                          o = oi % 2; oi += 1
                          rows = slice(idx * 128, (idx + 1) * 128)
                          cols = slice(T0 + half * 512, T0 + (half + 1) * 512)
                          if kind == 'xa':
                              k.op('act', lambda e, pp=pp, o=o: e.activation(out=of[o][:], in_=pp[:], func=AF.Copy), rd=[hp_], wr=[h_o[o]])
                              k.dma('act', xa_d[rows, cols], of[o][:], rd=[h_o[o]])
                          elif kind in ('ga', 'gb'):
                              k.op('act', lambda e, pp=pp, o=o: e.activation(out=ob[o][:], in_=pp[:], func=AF.Silu), rd=[hp_], wr=[h_o[o]])
                              k.dma('act', dst[kind][rows, cols], ob[o][:], rd=[h_o[o]])
                          elif kind in ('iq', 'ik'):
                              k.op('act', lambda e, pp=pp, o=o: e.activation(out=ob[o][:], in_=pp[:], func=AF.Copy), rd=[hp_], wr=[h_o[o]])
                              k.dma('act', dst[kind][rows, cols], ob[o][:], rd=[h_o[o]])
                          else:
                              gi = 0 if kind == 'q' else 1
                              k.op('act', lambda e, pp=pp: e.activation(out=sq[:], in_=pp[:], func=AF.Square), rd=[hp_], wr=[h_sq])
                              k.op('pe', lambda e: e.matmul(pm[:], lhsT=onesm[:], rhs=sq[:], start=True, stop=True), rd=[h_sq, h_c], wr=[h_pm])
                              k.op('act', lambda e: e.activation(out=rsd[:], in_=pm[:], func=AF.Sqrt, bias=epsc[:, 0:1]), rd=[h_pm, h_c], wr=[h_sq])
                              k.op('dve', lambda e: e.reciprocal(out=rsd[:], in_=rsd[:]), rd=[h_sq], wr=[h_sq])
                              k.op('dve', lambda e, pp=pp, o=o, gi=gi: e.scalar_tensor_tensor(
                                  out=ob[o][:], in0=pp[:], scalar=qg[:, gi:gi + 1], in1=rsd[:], op0=ALU.mult, op1=ALU.mult),
                                  rd=[hp_, h_sq, h_lc], wr=[h_o[o]])
                              k.dma('sp', dst[kind][rows, cols], ob[o][:], rd=[h_o[o]])
                  load_big(ewv_d[j], 8192, WB[0], h_wb[0], ST, h_st)
                  load_big(ewiw_d[j], 256, WB[1], h_wb[1], ST, h_st)
                  for tile in range(8):
                      for dc in range(16):
                          k.op('pe', lambda e, dc=dc, tile=tile: e.matmul(pa[:], lhsT=hnT[:, dc, tile * 128:(tile + 1) * 128],
                                                                        rhs=WB[0][:, dc * 512:(dc + 1) * 512], start=(dc == 0), stop=(dc == 15)),
                               rd=[h_wb[0], h_hnT], wr=[h_pa])
                      k.op('act', lambda e: e.activation(out=vt[:, :, 0:128], in_=pa[:].rearrange("p (g d) -> p g d", g=4), func=AF.Copy),
                           rd=[h_pa], wr=[h_vt])
                      k.dma('act', v_d[T0 + tile * 128:T0 + (tile + 1) * 128, :], vt[:].rearrange("p g d -> p (g d)"), rd=[h_vt])
                      for dc in range(16):
                          k.op('pe', lambda e, dc=dc, tile=tile: e.matmul(pb[:, 0:16], lhsT=hnT[:, dc, tile * 128:(tile + 1) * 128],
                                                                        rhs=WB[1][:, dc * 16:(dc + 1) * 16], start=(dc == 0), stop=(dc == 15)),
                               rd=[h_wb[1], h_hnT], wr=[h_pb])
                      k.op('act', lambda e: e.activation(out=iwt[:], in_=pb[:, 0:16], func=AF.Copy), rd=[h_pb], wr=[h_iwt])
                      k.dma('act', iw_d[T0 + tile * 128:T0 + (tile + 1) * 128, :], iwt[:], rd=[h_iwt])
              k.barrier()
        with ExitStack() as ls:
          if 'E2' in PH:
              sbl = lambda name, shape, dt: ls.enter_context(nc.sbuf_tensor(un(name), list(shape), dt))
              psl = lambda name, shape, dt: ls.enter_context(nc.psum_tensor(un(name), list(shape), dt))
              L = 1024
              evec = sbl("evec", [128, 8, 16], F32)
              kap = sbl("kap", [128, 16], F32); kap2 = sbl("kap2", [128, 16], F32); tmpk = sbl("tmpk", [128, 16], F32)
              wr_f = sbl("wr_f", [128, 2048], F32); wi_f = sbl("wi_f", [128, 2048], F32)
              wr_b = sbl("wr_b", [128, 16, 128], BF16); wi_b = sbl("wi_b", [128, 16, 128], BF16)
              h_lc = k.h()
              xin = [sbl(f"xin{i}", [128, 3 + L], F32) for i in range(2)]; h_xin = k.hs(2)
              xc = sbl("xc", [128, L], F32); h_xc = k.h()
              xcb = sbl("xcb", [128, L], BF16); h_xcb = k.h()
              rg = sbl("rg", [128, L], F32); ig = sbl("ig", [128, L], F32); h_rg, h_ig = k.h(), k.h()
              aa = sbl("aa", [128, L], F32); a2 = sbl("a2", [128, L], F32); h_aa, h_a2 = k.h(), k.h()
              bx = sbl("bx", [128, L], F32); h_bx = k.h()
              hsb = [sbl(f"hs{i}", [128, L], F32) for i in range(2)]; h_hs = k.hs(2)
              sga = [sbl(f"sga{i}", [128, L], BF16) for i in range(2)]; h_sga = k.hs(2)
              yb = [sbl(f"yb{i}", [128, L], BF16) for i in range(2)]; h_yb = k.hs(2)
              pa = psl("pa", [128, 512], F32); pb = psl("pb", [128, 512], F32); h_pa, h_pb = k.h(), k.h()
              k.dma('sp', evec[:].rearrange("p a g -> p (a g)"), evec_d[j], wr=[h_lc])
              k.dma('sp', wr_f[:], lwr_d[j], wr=[h_lc])
              k.dma('sp', wi_f[:], lwi_d[j], wr=[h_lc])
              k.op('dve', lambda e: e.tensor_copy(out=wr_b[:].rearrange("p g j -> p (g j)"), in_=wr_f[:]), rd=[h_lc], wr=[h_lc])
              k.op('dve', lambda e: e.tensor_copy(out=wi_b[:].rearrange("p g j -> p (g j)"), in_=wi_f[:]), rd=[h_lc], wr=[h_lc])
              k.op('act', lambda e: e.activation(out=tmpk[:], in_=evec[:, 7, :], func=AF.Exp, scale=-1.0), rd=[h_lc], wr=[h_lc])
              k.op('act', lambda e: e.activation(out=tmpk[:], in_=tmpk[:], func=AF.Ln, bias=onec[:, 0:1]), rd=[h_lc, h_c], wr=[h_lc])
              k.op('dve', lambda e: e.tensor_scalar(out=kap[:], in0=tmpk[:], scalar1=-8.0, scalar2=None, op0=ALU.mult), rd=[h_lc], wr=[h_lc])
              k.op('dve', lambda e: e.tensor_scalar(out=kap2[:], in0=tmpk[:], scalar1=-16.0, scalar2=None, op0=ALU.mult), rd=[h_lc], wr=[h_lc])
              it = 0
              for g in range(16):
                  k.maybe_barrier()
                  rows = slice(g * 128, (g + 1) * 128)
                  for pc in range(T // L):
                      i = it % 2; it += 1
                      t0 = pc * L
                      if pc == 0:
                          k.op('dve', lambda e, i=i: e.memset(xin[i][:, 0:3], 0.0), wr=[h_xin[i]])
                          k.dma('sp', xin[i][:, 3:3 + L], xa_d[rows, t0:t0 + L], wr=[h_xin[i]])
                      else:
                          k.dma('sp', xin[i][:, 0:3 + L], xa_d[rows, t0 - 3:t0 + L], wr=[h_xin[i]])
                      k.dma('sp', sga[i][:], sga_d[rows, t0:t0 + L], wr=[h_sga[i]])
                      k.op('dve', lambda e, i=i, g=g: e.tensor_scalar(out=xc[:], in0=xin[i][:, 3:3 + L], scalar1=evec[:, 3, g:g + 1],
                                                                      scalar2=evec[:, 4, g:g + 1], op0=ALU.mult, op1=ALU.add),
                           rd=[h_xin[i], h_lc], wr=[h_xc])
                      for tap in range(3):
                          k.op('dve', lambda e, i=i, g=g, tap=tap: e.scalar_tensor_tensor(
                              out=xc[:], in0=xin[i][:, tap:tap + L], scalar=evec[:, tap, g:g + 1], in1=xc[:], op0=ALU.mult, op1=ALU.add),
                              rd=[h_xin[i], h_lc, h_xc], wr=[h_xc])
                      k.op('act', lambda e: e.activation(out=xcb[:], in_=xc[:], func=AF.Copy), rd=[h_xc], wr=[h_xcb])
                      for half in range(L // 512):
                          sl = slice(half * 512, (half + 1) * 512)
                          k.op('pe', lambda e, g=g, sl=sl: e.matmul(pa[:], lhsT=wr_b[:, g, :], rhs=xcb[:, sl], start=True, stop=True),
                               rd=[h_xcb, h_lc], wr=[h_pa])
                          k.op('pe', lambda e, g=g, sl=sl: e.matmul(pb[:], lhsT=wi_b[:, g, :], rhs=xcb[:, sl], start=True, stop=True),
                               rd=[h_xcb, h_lc], wr=[h_pb])
                          k.op('act', lambda e, g=g, sl=sl: e.activation(out=rg[:, sl], in_=pa[:], func=AF.Sigmoid, bias=evec[:, 5, g:g + 1]),
                               rd=[h_pa, h_lc], wr=[h_rg])
                          k.op('act', lambda e, g=g, sl=sl: e.activation(out=ig[:, sl], in_=pb[:], func=AF.Sigmoid, bias=evec[:, 6, g:g + 1]),
                               rd=[h_pb, h_lc], wr=[h_ig])
                      k.op('act', lambda e, g=g: e.activation(out=aa[:], in_=rg[:], func=AF.Exp, scale=kap[:, g:g + 1]), rd=[h_rg, h_lc], wr=[h_aa])
                      k.op('act', lambda e, g=g: e.activation(out=a2[:], in_=rg[:], func=AF.Exp, scale=kap2[:, g:g + 1]), rd=[h_rg, h_lc], wr=[h_a2])
                      k.op('act', lambda e: e.activation(out=a2[:], in_=a2[:], func=AF.Sqrt, scale=-1.0, bias=onec[:, 0:1]), rd=[h_a2, h_c], wr=[h_a2])
                      k.op('pool', lambda e: e.tensor_tensor(out=bx[:], in0=a2[:], in1=ig[:], op=ALU.mult), rd=[h_a2, h_ig], wr=[h_bx])
                      k.op('pool', lambda e: e.tensor_tensor(out=bx[:], in0=bx[:], in1=xc[:], op=ALU.mult), rd=[h_bx, h_xc], wr=[h_bx])
                      if pc == 0:
                          k.op('dve', lambda e, i=i: e.tensor_tensor_scan(out=hsb[i][:], data0=aa[:], data1=bx[:], initial=0.0,
                                                                          op0=ALU.mult, op1=ALU.add), rd=[h_aa, h_bx], wr=[h_hs[i]])
                      else:
                          k.op('dve', lambda e, i=i: e.tensor_tensor_scan(out=hsb[i][:], data0=aa[:], data1=bx[:], initial=hsb[1 - i][:, L - 1:L],
                                                                          op0=ALU.mult, op1=ALU.add), rd=[h_aa, h_bx, h_hs[1 - i]], wr=[h_hs[i]])
                      k.op('pool', lambda e, i=i: e.tensor_tensor(out=yb[i][:], in0=hsb[i][:], in1=sga[i][:], op=ALU.mult),
                           rd=[h_hs[i], h_sga[i]], wr=[h_yb[i]])
                      k.dma('sp', yT_d[rows, t0:t0 + L], yb[i][:], rd=[h_yb[i]])
              k.barrier()
        with ExitStack() as ls:
          if 'E3' in PH:
              sbl = lambda name, shape, dt: ls.enter_context(nc.sbuf_tensor(un(name), list(shape), dt))
              psl = lambda name, shape, dt: ls.enter_context(nc.psum_tensor(un(name), list(shape), dt))
              kT = sbl("kT", [128, 4, T], BF16); vA = sbl("vA", [128, 32, 4 * 130], BF16); ik2 = sbl("ik2", [128, T], BF16)
              h_kv = k.h()
              for g in range(4):
                  k.dma('sp', kT[:, g, :], kk_d[g * 128:(g + 1) * 128, :], wr=[h_kv])
              for c in range(4):
                  k.dma('sp', vA[:, c * 8:(c + 1) * 8, :], v_d[c * 1024:(c + 1) * 1024, :].rearrange("(i p) d -> p i d", p=128), wr=[h_kv])
              k.dma('sp', ik2[:], ik_d, wr=[h_kv])
              qt = [sbl(f"qt{i}", [128, 16, 128], BF16) for i in range(2)]
              iqt = [sbl(f"iqt{i}", [128, 8, 128], BF16) for i in range(2)]
              iwt = [sbl(f"iwt{i}", [128, 16], F32) for i in range(2)]
              sgbt = [sbl(f"sgbt{i}", [128, 16, 128], BF16) for i in range(2)]
              h_ld = k.hs(2)
              dg = sbl("dg", [128, 16, 128], BF16); h_dg = k.h()
              Rb = [sbl(f"Rb{i}", [128, 512], BF16) for i in range(2)]; h_R = k.hs(2)
              acc = sbl("acc", [128, T], F32); h_acc = k.h()
              cj = sbl("cj", [128, T], BF16); h_cj = k.h()
              am = sbl("am", [128, 1], F32); lo = sbl("lo", [128, 1], F32); w0 = sbl("w0", [128, 1], F32)
              mid = sbl("mid", [128, 1], F32); cnt = sbl("cnt", [128, 1], F32); t1 = sbl("t1", [128, 1], F32); h_bs = k.h()
              mask = sbl("mask", [128, T], BF16); h_mask = k.h()
              mT4 = sbl("mT4", [128, 32, 4, 128], BF16); h_mT = k.h()
              Mn = sbl("Mn", [128, 2, 16, 128], BF16); h_Mn = k.h()
              Eb = [sbl(f"Eb{i}", [128, 512], BF16) for i in range(2)]; h_E = k.hs(2)
              Pb = [sbl(f"Pb{i}", [128, 4, 128], BF16) for i in range(2)]; h_P = k.hs(2)
              ri = sbl("ri", [128, 4], F32); h_ri = k.h()
              ot = sbl("ot", [128, 16, 128], BF16); h_ot = k.h()
              yt = [sbl(f"yt{i}", [128, 16, 128], BF16) for i in range(2)]; h_yt = k.hs(2)
              pA = psl("pA", [128, 512], F32); pB = psl("pB", [128, 512], F32); pC = psl("pC", [128, 512], F32)
              h_pA, h_pB, h_pC = k.h(), k.h(), k.h()
              mtp = psl("mtp", [128, 512], BF16); h_mtp = k.h()
              oacc = [psl(f"oacc{i}", [128, 512], F32) for i in range(4)]; h_oa = k.hs(4)
              EBv = EB[:].rearrange("s (k h t) -> s k h t", k=2, h=16)
              zi = 0
              for n in range(NT):
                  k.maybe_barrier()
                  b = n % 2
                  S = 128 * (n + 1)
                  c0 = n * 128
                  k.dma('sp', qt[b][:], q_d[:, c0:c0 + 128].rearrange("(h p) t -> p h t", p=128), wr=[h_ld[b]])
                  k.dma('sp', iqt[b][:], iq_d[:, c0:c0 + 128].rearrange("(h p) t -> p h t", p=128), wr=[h_ld[b]])
                  k.dma('sp', iwt[b][:], iw_d[c0:c0 + 128, :], wr=[h_ld[b]])
                  k.dma('sp', sgbt[b][:], sgb_d[:, c0:c0 + 128].rearrange("(h p) t -> p h t", p=128), wr=[h_ld[b]])
                  for h in range(16):
                      k.op('pool', lambda e, h=h, b=b: e.tensor_scalar(out=dg[:, h, :], in0=ident_f[:], scalar1=iwt[b][:, h:h + 1], scalar2=0.0,
                                                                       op0=ALU.mult, op1=ALU.add), rd=[h_ld[b], h_c], wr=[h_dg])
                  nkb = (S + 511) // 512
                  steps = [(kb, h) for kb in range(nkb) for h in range(16)]

                  def idxA(jj):
                      kb, h = steps[jj]
                      w = min(512, S - kb * 512)
                      c = h // 2; p0 = 64 * (h % 2)
                      zp, hz = (pA, h_pA) if jj % 2 == 0 else (pB, h_pB)
                      R, hR = Rb[jj % 2], h_R[jj % 2]
                      k.op('pe', lambda e: e.matmul(
                          zp[:, 0:w], lhsT=iqt[b][p0:p0 + 64, c, :], rhs=ik2[p0:p0 + 64, kb * 512:kb * 512 + w], start=True, stop=True),
                          rd=[h_ld[b], h_kv], wr=[hz])
                      k.op('act', lambda e: e.activation(out=R[:, 0:w], in_=zp[:, 0:w], func=AF.Relu), rd=[hz], wr=[hR])

                  def idxB(jj):
                      kb, h = steps[jj]
                      w = min(512, S - kb * 512)
                      R, hR = Rb[jj % 2], h_R[jj % 2]
                      k.op('pe', lambda e: e.matmul(pC[:, 0:w], lhsT=dg[:, h, :], rhs=R[:, 0:w], start=(h == 0), stop=(h == 15)),
                           rd=[h_dg, hR], wr=[h_pC])
                      if h == 15:
                          k.op('dve', lambda e: e.tensor_copy(out=acc[:, kb * 512:kb * 512 + w], in_=pC[:, 0:w]), rd=[h_pC], wr=[h_acc])

                  idxA(0)
                  for jj in range(len(steps)):
                      if jj + 1 < len(steps):
                          idxA(jj + 1)
                      idxB(jj)
                  k.op('dve', lambda e, S=S: e.tensor_reduce(out=am[:], in_=acc[:, 0:S], axis=AX.X, op=ALU.max, apply_absolute_value=True),
                       rd=[h_acc], wr=[h_bs])
                  k.op('dve', lambda e: e.tensor_scalar(out=lo[:], in0=am[:], scalar1=-1.001, scalar2=-1e-20, op0=ALU.mult, op1=ALU.add), rd=[h_bs], wr=[h_bs])
                  k.op('dve', lambda e: e.tensor_scalar(out=w0[:], in0=am[:], scalar1=2.002, scalar2=2e-20, op0=ALU.mult, op1=ALU.add), rd=[h_bs], wr=[h_bs])
                  k.op('dve', lambda e, S=S: e.tensor_tensor(out=acc[:, S - 128:S], in0=acc[:, S - 128:S], in1=diagneg[:], op=ALU.add),
                       rd=[h_acc, h_c], wr=[h_acc])
                  if S > 256:
                      for itn in range(1, NIT + 1):
                          f = 2.0 ** (-itn)
                          k.op('dve', lambda e, f=f: e.scalar_tensor_tensor(out=mid[:], in0=w0[:], scalar=f, in1=lo[:], op0=ALU.mult, op1=ALU.add),
                               rd=[h_bs], wr=[h_bs])
                          k.op('dve', lambda e, S=S: e.tensor_scalar(out=cj[:, 0:S], in0=acc[:, 0:S], scalar1=mid[:, 0:1], scalar2=None,
                                                                     op0=ALU.is_ge, op1=ALU.add, accum_out=cnt[:, 0:1]), rd=[h_acc, h_bs], wr=[h_cj, h_bs])
                          k.op('dve', lambda e, f=f: e.tensor_scalar(out=t1[:], in0=cnt[:], scalar1=255.5, scalar2=f, op0=ALU.is_ge, op1=ALU.mult),
                               rd=[h_bs], wr=[h_bs])
                          k.op('dve', lambda e: e.scalar_tensor_tensor(out=lo[:], in0=t1[:], scalar=w0[:, 0:1], in1=lo[:], op0=ALU.mult, op1=ALU.add),
                               rd=[h_bs], wr=[h_bs])
                  k.op('dve', lambda e, S=S: e.tensor_scalar(out=mask[:, 0:S], in0=acc[:, 0:S], scalar1=lo[:, 0:1], scalar2=None, op0=ALU.is_ge),
                       rd=[h_acc, h_bs], wr=[h_mask])
                  for k0 in range(0, n + 1, 4):
                      nk = min(4, n + 1 - k0)
                      for q in range(nk):
                          k.op('pe', lambda e, q=q, k0=k0: e.transpose(out=mtp[:, q * 128:(q + 1) * 128], in_=mask[:, (k0 + q) * 128:(k0 + q + 1) * 128],
                                                                       identity=ident_b[:]), rd=[h_mask, h_c], wr=[h_mtp])
                      for r in range(4):
                          eng = 'act' if r % 2 == 0 else 'dve'
                          if eng == 'act':
                              k.op('act', lambda e, r=r, k0=k0, nk=nk: e.activation(out=mT4[:, k0:k0 + nk, r, :],
                                                                                   in_=mtp[:, 0:nk * 128].rearrange("p (q t) -> p q t", q=nk), func=AF.Copy),
                                   rd=[h_mtp], wr=[h_mT])
                          else:
                              k.op('dve', lambda e, r=r, k0=k0, nk=nk: e.tensor_copy(out=mT4[:, k0:k0 + nk, r, :],
                                                                                    in_=mtp[:, 0:nk * 128].rearrange("p (q t) -> p q t", q=nk)),
                                   rd=[h_mtp], wr=[h_mT])
                  for kind in range(2):
                      if n - kind < 0:
                          continue
                      for g4 in range(4):
                          k.op('pool', lambda e, kind=kind, g4=g4, n=n: e.tensor_tensor(out=Mn[:, kind, g4 * 4:(g4 + 1) * 4, :],
                                                                                        in0=EBv[:, kind, g4 * 4:(g4 + 1) * 4, :],
                                                                                        in1=mT4[:, n - kind, :, :], op=ALU.mult),
                               rd=[h_mT, h_c], wr=[h_Mn])
                  pairs = [(g, kt) for g in range(4) for kt in range(n + 1)]

                  def attA(ii):
                      g, kt = pairs[ii]
                      sp_, hs_ = (pA, h_pA) if ii % 2 == 0 else (pB, h_pB)
                      E_, hE = Eb[ii % 2], h_E[ii % 2]
                      P_, hP = Pb[ii % 2], h_P[ii % 2]
                      k.op('pe', lambda e: e.matmul(sp_[:], lhsT=kT[:, g, kt * 128:(kt + 1) * 128],
                                                    rhs=qt[b][:, g * 4:(g + 1) * 4, :], start=True, stop=True),
                           rd=[h_kv, h_ld[b]], wr=[hs_])
                      k.op('act', lambda e: e.activation(out=E_[:], in_=sp_[:], func=AF.Exp), rd=[hs_], wr=[hE])
                      kind = n - kt
                      if kind < 2:
                          k.op('dve', lambda e: e.tensor_tensor(
                              out=P_[:], in0=E_[:].rearrange("p (r t) -> p r t", r=4), in1=Mn[:, kind, g * 4:(g + 1) * 4, :], op=ALU.mult),
                              rd=[hE, h_Mn], wr=[hP])
                      else:
                          k.op('dve', lambda e: e.tensor_tensor(
                              out=P_[:], in0=E_[:].rearrange("p (r t) -> p r t", r=4), in1=mT4[:, kt, :, :], op=ALU.mult),
                              rd=[hE, h_mT], wr=[hP])

                  def attB(ii):
                      g, kt = pairs[ii]
                      P_, hP = Pb[ii % 2], h_P[ii % 2]
                      for r in range(4):
                          k.op('pe', lambda e, r=r: e.matmul(
                              oacc[r][:, 0:129], lhsT=P_[:, r, :], rhs=vA[:, kt, g * 130:g * 130 + 129], start=(kt == 0), stop=(kt == n)),
                              rd=[hP, h_kv], wr=[h_oa[r]])
                      if kt == n:
                          for r in range(4):
                              k.op('dve', lambda e, r=r: e.reciprocal(out=ri[:, r:r + 1], in_=oacc[r][:, 128:129]), rd=[h_oa[r]], wr=[h_ri])
                              k.op('dve', lambda e, r=r: e.tensor_scalar(out=ot[:, g * 4 + r, :], in0=oacc[r][:, 0:128], scalar1=ri[:, r:r + 1],
                                                                         scalar2=None, op0=ALU.mult), rd=[h_oa[r], h_ri], wr=[h_ot])

                  attA(0)
                  for ii in range(len(pairs)):
                      if ii + 1 < len(pairs):
                          attA(ii + 1)
                      attB(ii)
                  for g4 in range(4):
                      for r in range(4):
                          k.op('pe', lambda e, g4=g4, r=r: e.transpose(out=mtp[:, r * 128:(r + 1) * 128], in_=ot[:, g4 * 4 + r, :], identity=ident_b[:]),
                               rd=[h_ot, h_c], wr=[h_mtp])
                      k.op('dve', lambda e, g4=g4, b=b: e.tensor_tensor(out=yt[b][:, g4 * 4:(g4 + 1) * 4, :], in0=mtp[:].rearrange("p (r t) -> p r t", r=4),
                                                                        in1=sgbt[b][:, g4 * 4:(g4 + 1) * 4, :], op=ALU.mult),
                           rd=[h_mtp, h_ld[b]], wr=[h_yt[b]])
                  k.dma('sp', yT_d[2048:4096, c0:c0 + 128].rearrange("(h p) t -> p h t", p=128), yt[b][:], rd=[h_yt[b]])
              k.barrier()
        with ExitStack() as ls:
          if 'E4' in PH:
              sbl = lambda name, shape, dt: ls.enter_context(nc.sbuf_tensor(un(name), list(shape), dt))
              tl = alloc_tail(ls, 32)
              gbc = tl[21]
              hblk = sbl("hblk", [128, 4, 2048], F32); h_hblk = k.h()
              yT = sbl("yTs", [128, 32, 512], BF16); h_yT = k.h()
              h_lc = k.h()
              k.dma('sp', gbc[:], pgn_d[layer, :].partition_broadcast(128), wr=[h_c])
              for tb in range(8):
                  k.maybe_barrier()
                  t0 = tb * 512
                  k.dma('sp', hblk[:], h_src[t0:t0 + 512, :].rearrange("(i p) d -> p i d", p=128), wr=[h_hblk])
                  for c in range(4):
                      k.dma('sp', yT[:, c * 8:(c + 1) * 8, :], yT_d[c * 1024:(c + 1) * 1024, t0:t0 + 512].rearrange("(c p) t -> p c t", p=128), wr=[h_yT])
                  tail_block(layer, tb, hblk, h_hblk, yT, h_yT, 32, ewo_d[j], tl)
                  k.dma('sp', h_dst[t0:t0 + 512, :].rearrange("(i p) d -> p i d", p=128), hblk[:], rd=[h_hblk])
              k.barrier()

    for layer in layer_list:
        j = layer // 2
        h_src = x_d if layer == layer_list[0] else hbuf_d
        h_dst = y_d if layer == layer_list[-1] else hbuf_d
        if layer % 2 == 0:
            even_layer(layer, j, h_src, h_dst)
        else:
            odd_layer(layer, j, h_src, h_dst)

    k.barrier()
    es.close()
    return nc


def _prep_inputs(inp):
    f = lambda a: np.ascontiguousarray(np.asarray(a, dtype=np.float32))
    W = {kk: f(v) for kk, v in inp.items()}
    rng = np.arange
    shared = {}
    shared["norm_gain"] = W["norm_gain"]
    shared["pe_gate_norm"] = W["pe_gate_norm"]
    shared["consts"] = np.eye(128, dtype=np.float32)
    s_ = rng(128)[:, None]
    t_ = rng(128)[None, :]
    oh = np.zeros((32, 2, 128, 128), np.float32)
    for kind in range(2):
        bk = t5_bucket_np((s_ - t_ - 128 * kind).astype(np.int32))
        for b in range(32):
            oh[b, kind] = (bk == b).astype(np.float32)
    oh[15] -= 1.0
    shared["oh"] = oh.reshape(32, -1)
    shared["rel_bias"] = W["rel_bias"]
    cidx = np.concatenate([rng(0, 2048), rng(2048, 4096), rng(4096, 6144), rng(6144, 6656), rng(7168, 9216),
                           rng(9216, 10240), rng(10240, 10304), rng(10240, 10304)])
    ewf = np.empty((2, 77, 128, 2048), np.float32)
    ewv = np.empty((2, 128, 8192), np.float32)
    ewiw = np.empty((2, 128, 256), np.float32)
    ewo = np.empty((2, 4, 128, 32 * 512), np.float32)
    lwr = np.empty((2, 128, 2048), np.float32)
    lwi = np.empty((2, 128, 2048), np.float32)
    evec = np.empty((2, 128, 8, 16), np.float32)
    qkn = np.empty((2, 128, 2), np.float32)
    owf = np.empty((2, 32, 128, 2048), np.float32)
    owv = np.empty((2, 4, 128, 8192), np.float32)
    owo = np.empty((2, 4, 128, 8192), np.float32)
    wst = np.empty((2, 128, 2048), np.float32)
    ovec = np.empty((2, 128, 32), np.float32)
    for j in range(2):
        Wi = W["even_w_in"][j]
        ewf[j] = Wi[:, cidx].reshape(16, 128, 77, 128).transpose(2, 1, 0, 3).reshape(77, 128, 2048)
        ewv[j] = Wi[:, 6656:7168].reshape(16, 128, 512).transpose(1, 0, 2).reshape(128, 8192)
        ewiw[j] = Wi[:, 10304:10320].reshape(16, 128, 16).transpose(1, 0, 2).reshape(128, 256)
        ewo[j] = W["even_w_out"][j].reshape(32, 128, 4, 512).transpose(2, 1, 0, 3).reshape(4, 128, 32 * 512)
        lwr[j] = W["lru_w_r"][j].transpose(1, 0, 2).reshape(128, 2048)
        lwi[j] = W["lru_w_i"][j].transpose(1, 0, 2).reshape(128, 2048)
        pc = lambda v: v.reshape(16, 128).T
        for a in range(4):
            evec[j, :, a, :] = pc(W["conv_w"][j, a])
        evec[j, :, 4, :] = pc(W["conv_b"][j])
        evec[j, :, 5, :] = pc(W["lru_b_r"][j].reshape(-1))
        evec[j, :, 6, :] = pc(W["lru_b_i"][j].reshape(-1))
        evec[j, :, 7, :] = pc(W["lru_lambda"][j])
        qkn[j, :, 0] = W["q_norm"][j]
        qkn[j, :, 1] = W["k_norm"][j]
        Wo = W["odd_w_in"][j]
        oidx = np.concatenate([rng(0, 2048), rng(4096, 6144)])
        owf[j] = Wo[:, oidx].reshape(16, 128, 32, 128).transpose(2, 1, 0, 3).reshape(32, 128, 2048)
        owv[j] = Wo[:, 2048:4096].reshape(16, 128, 4, 512).transpose(2, 1, 0, 3).reshape(4, 128, 8192)
        owo[j] = W["odd_w_out"][j].reshape(16, 128, 4, 512).transpose(2, 1, 0, 3).reshape(4, 128, 8192)
        wst[j] = W["sg_w_s"][j].transpose(2, 0, 1).reshape(128, 2048)
        ovec[j, :, 0:16] = pc(W["sg_ln_g"][j])
        ovec[j, :, 16:32] = pc(W["sg_ln_b"][j])
    shared.update(ewf=ewf, ewv=ewv, ewiw=ewiw, ewo=ewo, lwr=lwr, lwi=lwi, evec=evec.reshape(2, 128, 128), qkn=qkn,
                  owf=owf, owv=owv, owo=owo, wst=wst, ovec=ovec, bs=W["sg_b_s"])
    shared["pew"] = np.ascontiguousarray(W["pe_w"].reshape(4, 2, 128, 4, 512).transpose(0, 3, 2, 1, 4).reshape(4, 4, 128, 1024))
    shared["pwg"] = np.ascontiguousarray(W["pe_w_gate"].reshape(4, 16, 128, 4, 512).transpose(0, 3, 2, 1, 4).reshape(4, 4, 128, 8192))
    shared = {kk: np.ascontiguousarray(v, dtype=np.float32) for kk, v in shared.items()}
    maps = []
    for b in range(2):
        m = dict(shared)
        m["x"] = np.ascontiguousarray(W["x"][b])
        m["p"] = np.ascontiguousarray(W["p"][:, b])
        maps.append(m)
    return maps


def kernel(**inputs):
    ll = os.environ.get("KLAYERS")
    layer_list = [int(v) for v in ll.split(",")] if ll else [0, 1, 2, 3]
    maps = _prep_inputs(inputs)
    nc = build_program(layer_list)
    res = run_bass_kernel_spmd(nc, maps, core_ids=[0, 1], trace=bool(os.environ.get('KTRACE')))
    if os.environ.get('KTRACE'):
        print('EXEC_TIME_NS', res.exec_time_ns, flush=True)
    out = np.stack([np.asarray(res.results[b]["y"], dtype=np.float32) for b in range(2)], axis=0)
    if os.environ.get("KDEBUG"):
        global LAST_RESULTS
        LAST_RESULTS = res.results
    return out
```

```python
import os
from contextlib import ExitStack
import numpy as np
import concourse.bass as bass
import concourse.mybir as mybir
from concourse.bass_utils import run_bass_kernel_spmd

F32 = mybir.dt.float32
BF16 = mybir.dt.bfloat16
AF = mybir.ActivationFunctionType
ALU = mybir.AluOpType
AX = mybir.AxisListType

T = 4096
D = 2048
NT = T // 128
EPS = 1e-6
NEG = -1.0e30
NIT = 16
NDS = 12
EPOCH_LIMIT = 9000
SAME = {'pe': False, 'act': True, 'dve': True, 'pool': True, 'sp': False}


class H:
    __slots__ = ('w', 'r')

    def __init__(self):
        self.w = None
        self.r = {}


class K:
    def __init__(self, nc, es):
        self.nc = nc
        self.E = {'pe': nc.tensor, 'act': nc.scalar, 'dve': nc.vector, 'pool': nc.gpsimd, 'sp': nc.sync}
        self.sems = [{k: es.enter_context(nc.semaphore(f"e{s}_{k}")) for k in self.E} for s in range(3)]
        self.dsems = [[es.enter_context(nc.semaphore(f"d{s}_{i}")) for i in range(NDS)] for s in range(3)]
        self.ep = 0
        self.cnt = {k: 0 for k in self.E}
        self.dval = [0] * NDS
        self.dnext = 0
        self.seen = {k: {} for k in self.E}
        self.handles = []

    def h(self):
        x = H()
        self.handles.append(x)
        return x

    def hs(self, n):
        return [self.h() for _ in range(n)]

    def _wait(self, e, key, val):
        if key[0] == 'e' and key[1] == e and not SAME[e]:
            return
        if self.seen[e].get(key, 0) >= val:
            return
        s = self.ep % 3
        sem = self.sems[s][key[1]] if key[0] == 'e' else self.dsems[s][key[1]]
        self.E[e].wait_ge(sem, val)
        self.seen[e][key] = val

    def _deps(self, e, rd, wr):
        d = {}
        for h in rd:
            if h.w is not None:
                d[h.w[0]] = max(d.get(h.w[0], 0), h.w[1])
        for h in wr:
            if h.w is not None:
                d[h.w[0]] = max(d.get(h.w[0], 0), h.w[1])
            for kk, v in h.r.items():
                d[kk] = max(d.get(kk, 0), v)
        for kk, v in d.items():
            self._wait(e, kk, v)

    def _mark(self, tok, rd, wr):
        for h in wr:
            h.w = tok
            h.r = {}
        for h in rd:
            if h not in wr:
                h.r[tok[0]] = max(h.r.get(tok[0], 0), tok[1])

    def op(self, e, fn, rd=(), wr=()):
        self._deps(e, rd, wr)
        ins = fn(self.E[e])
        self.cnt[e] += 1
        ins.then_inc(self.sems[self.ep % 3][e], 1)
        self._mark((('e', e), self.cnt[e]), rd, wr)

    def dma(self, q, out, in_, rd=(), wr=()):
        i = self.dnext
        self.dnext = (i + 1) % NDS
        if self.dval[i] > 0:
            self._wait(q, ('d', i), self.dval[i])
        self._deps(q, rd, wr)
        ins = self.E[q].dma_start(out=out, in_=in_)
        self.dval[i] += 16
        ins.then_inc(self.dsems[self.ep % 3][i], 16)
        self._mark((('d', i), self.dval[i]), rd, wr)

    def barrier(self):
        for e in self.E:
            for o in self.E:
                if o != e and self.cnt[o] > 0:
                    self._wait(e, ('e', o), self.cnt[o])
            for i in range(NDS):
                if self.dval[i] > 0:
                    self._wait(e, ('d', i), self.dval[i])
        self.ep += 1
        self.cnt = {k: 0 for k in self.E}
        self.dval = [0] * NDS
        self.seen = {k: {} for k in self.E}
        for h in self.handles:
            h.w = None
            h.r = {}
        if self.ep >= 2:
            o = (self.ep + 1) % 3
            for e in ('pe', 'act', 'dve', 'pool', 'sp'):
                self.op(e, lambda eng, e=e: eng.sem_clear(self.sems[o][e]))
            for i in range(NDS):
                self.op('sp', lambda eng, i=i: eng.sem_clear(self.dsems[o][i]))

    def maybe_barrier(self):
        if max(self.cnt.values()) > EPOCH_LIMIT or max(self.dval) > EPOCH_LIMIT * 16:
            self.barrier()


_UN = [0]


def un(name):
    _UN[0] += 1
    return f"{name}_u{_UN[0]}"


def t5_bucket_np(rel):
    half = 16
    max_exact = 8
    ret = np.where(rel > 0, half, 0)
    n = np.abs(rel)
    nf = np.maximum(n, 1).astype(np.float32)
    large = max_exact + (np.log(nf / max_exact) / np.float32(np.log(128 / max_exact)) * (half - max_exact)).astype(np.int32)
    large = np.minimum(large, half - 1)
    return ret + np.where(n < max_exact, n, large)


def build_program(layer_list):
    nc = bass.Bass("TRN2", target_bir_lowering=False)
    es = ExitStack()
    k = K(nc, es)

    def din(name, shape, dt=F32):
        return nc.dram_tensor(name, list(shape), dt, kind="ExternalInput").ap()

    DBG = bool(os.environ.get("KDEBUG"))
    PH = os.environ.get("KPHASES", "E1,E2,E3,E4").split(",")

    def dscr(name, shape, dt):
        return nc.dram_tensor(name, list(shape), dt, kind=("ExternalOutput" if DBG else "Internal")).ap()

    x_d = din("x", [T, D])
    p_d = din("p", [4, T, 256])
    ng_d = din("norm_gain", [4, D])
    pgn_d = din("pe_gate_norm", [4, D])
    consts_d = din("consts", [128, 128])
    oh_d = din("oh", [32, 2 * 128 * 128])
    rb_d = din("rel_bias", [32, 16])
    ewf_d = din("ewf", [2, 77, 128, 16 * 128])
    ewv_d = din("ewv", [2, 128, 16 * 512])
    ewiw_d = din("ewiw", [2, 128, 16 * 16])
    ewo_d = din("ewo", [2, 4, 128, 32 * 512])
    lwr_d = din("lwr", [2, 128, 16 * 128])
    lwi_d = din("lwi", [2, 128, 16 * 128])
    evec_d = din("evec", [2, 128, 8 * 16])
    qkn_d = din("qkn", [2, 128, 2])
    owf_d = din("owf", [2, 32, 128, 16 * 128])
    owv_d = din("owv", [2, 4, 128, 16 * 512])
    owo_d = din("owo", [2, 4, 128, 16 * 512])
    wst_d = din("wst", [2, 128, 16 * 128])
    ovec_d = din("ovec", [2, 128, 2 * 16])
    bs_d = din("bs", [2, 16, 128])
    pew_d = din("pew", [4, 4, 128, 2 * 512])
    pwg_d = din("pwg", [4, 4, 128, 16 * 512])
    y_d = nc.dram_tensor("y", [T, D], F32, kind="ExternalOutput").ap()

    hbuf_d = dscr("hbuf", [T, D], F32)
    xa_d = dscr("xaT", [2048, T], F32)
    sga_d = dscr("sgaT", [2048, T], BF16)
    sgb_d = dscr("sgbT", [2048, T], BF16)
    q_d = dscr("qT", [2048, T], BF16)
    kk_d = dscr("kT", [512, T], BF16)
    iq_d = dscr("iqT", [1024, T], BF16)
    ik_d = dscr("ikT", [128, T], BF16)
    v_d = dscr("vaug", [T, 4 * 130], BF16)
    iw_d = dscr("iw", [T, 16], F32)
    yT_d = dscr("yT", [4096, T], BF16)

    sb = lambda name, shape, dt: es.enter_context(nc.sbuf_tensor(un(name), list(shape), dt))
    ps = lambda name, shape, dt: es.enter_context(nc.psum_tensor(un(name), list(shape), dt))

    ident_f = sb("ident_f", [128, 128], F32); h_ident = k.h()
    ident_b = sb("ident_b", [128, 128], BF16)
    onesm = sb("onesm", [128, 128], F32)
    ones1 = sb("ones1", [128, 128], F32)
    diagneg = sb("diagneg", [128, 128], F32)
    epsc = sb("epsc", [128, 1], F32)
    onec = sb("onec", [128, 1], F32)
    EB = sb("EB", [128, 2 * 16 * 128], BF16)
    h_c = k.h()

    k.dma('sp', ident_f[:], consts_d, wr=[h_c])
    k.op('dve', lambda e: e.tensor_copy(out=ident_b[:], in_=ident_f[:]), rd=[h_c], wr=[h_c])
    k.op('dve', lambda e: e.memset(onesm[:], 1.0 / 128), wr=[h_c])
    k.op('dve', lambda e: e.memset(ones1[:], 1.0), wr=[h_c])
    k.op('dve', lambda e: e.memset(diagneg[:], 0.0), wr=[h_c])
    k.op('dve', lambda e: e.memset(diagneg[0:64, 64:128], NEG), wr=[h_c])
    k.op('dve', lambda e: e.memset(epsc[:], EPS), wr=[h_c])
    k.op('dve', lambda e: e.memset(onec[:], 1.0), wr=[h_c])

    with nc.sbuf_tensor("oh_s0", [32, 2 * 128 * 128], F32) as oh_s, nc.sbuf_tensor("rb_s0", [32, 16], F32) as rb_s, \
            nc.psum_tensor("bps", [128, 512], F32) as bps:
        h_oh, h_bps = k.h(), k.h()
        k.dma('sp', oh_s[:], oh_d, wr=[h_oh])
        k.dma('sp', rb_s[:], rb_d, wr=[h_oh])
        EBv = EB[:].rearrange("s (k h t) -> s k h t", k=2, h=16)
        ohv = oh_s[:].rearrange("b (k s t) -> b k s t", k=2, s=128)
        for kind in range(2):
            for t0 in range(0, 128, 32):
                for tt in range(32):
                    k.op('pe', lambda e, kind=kind, t=t0 + tt, tt=tt: e.matmul(
                        bps[:, tt * 16:(tt + 1) * 16], lhsT=ohv[:, kind, :, t], rhs=rb_s[:], start=True, stop=True),
                        rd=[h_oh], wr=[h_bps])
                k.op('act', lambda e, kind=kind, t0=t0: e.activation(
                    out=EBv[:, kind, :, t0:t0 + 32].rearrange("s h t -> s t h"),
                    in_=bps[:].rearrange("s (t h) -> s t h", h=16), func=AF.Exp), rd=[h_bps], wr=[h_c])
        k.barrier()

    def rms_to_T(htile, gain_bc, dstT, col0, h_in, h_dst, bufs):
        junk, ss, sd, rs, hp, tp, h_j, h_hp, h_tp = bufs
        k.op('act', lambda e: e.activation(out=junk[:], in_=htile, func=AF.Square, accum_out=ss[:, 0:1]),
             rd=[h_in], wr=[h_j])
        k.op('act', lambda e: e.activation(out=sd[:, 0:1], in_=ss[:, 0:1], func=AF.Sqrt, scale=1.0 / D, bias=epsc[:, 0:1]),
             rd=[h_j, h_c], wr=[h_j])
        k.op('dve', lambda e: e.reciprocal(out=rs[:, 0:1], in_=sd[:, 0:1]), rd=[h_j], wr=[h_j])
        k.op('dve', lambda e: e.scalar_tensor_tensor(out=hp[:], in0=htile, scalar=rs[:, 0:1], in1=gain_bc,
                                                      op0=ALU.mult, op1=ALU.mult), rd=[h_in, h_j, h_c], wr=[h_hp])
        for g4 in range(4):
            for j in range(4):
                dc = g4 * 4 + j
                k.op('pe', lambda e, dc=dc, j=j: e.transpose(out=tp[:, j * 128:(j + 1) * 128],
                                                              in_=hp[:, dc * 128:(dc + 1) * 128], identity=ident_b[:]),
                     rd=[h_hp, h_c], wr=[h_tp])
            k.op('act', lambda e, g4=g4: e.activation(out=dstT[:, g4 * 4:(g4 + 1) * 4, col0:col0 + 128],
                                                       in_=tp[:].rearrange("p (j t) -> p j t", j=4), func=AF.Copy),
                 rd=[h_tp], wr=[h_dst])

    state = {'wi': 0, 'ci': 0}

    def cast(out_ap, in_ap, rd, wr):
        state['ci'] += 1
        if state['ci'] % 2 == 0:
            k.op('act', lambda e: e.activation(out=out_ap, in_=in_ap, func=AF.Copy), rd=rd, wr=wr)
        else:
            k.op('dve', lambda e: e.tensor_copy(out=out_ap, in_=in_ap), rd=rd, wr=wr)

    def load_w(src_ap, n, ST, WB, h_st, h_wb, view=None):
        i = state['wi'] % 2
        state['wi'] += 1
        k.dma('sp', ST[i][:, 0:n], src_ap, wr=[h_st[i]])
        cast(WB[i][:, 0:n], ST[i][:, 0:n], [h_st[i]], [h_wb[i]])
        return WB[i], h_wb[i]

    def load_big(src2d, n, dst, h_dst_, ST, h_st):
        for c0 in range(0, n, 2048):
            w = min(2048, n - c0)
            i = state['wi'] % 2
            state['wi'] += 1
            k.dma('sp', ST[i][:, 0:w], src2d[:, c0:c0 + w], wr=[h_st[i]])
            cast(dst[:, c0:c0 + w], ST[i][:, 0:w], [h_st[i]], [h_dst_])

    def tail_block(layer, tb, hblk, h_hblk, yT, h_yT, CC, wo_src, tl):
        (ST, WB, h_st, h_wb, WO, h_wo, hgT, h_hgT, normbufs, pa, pb, h_pa, h_pb, pt_s, ptb, pT, h_pt, h_pT,
         sg, ge, h_sg, gbc) = tl
        for dblk in range(4):
            wo = WO[dblk % len(WO)]
            hwo = h_wo[dblk % len(WO)]
            load_big(wo_src[dblk], CC * 512, wo, hwo, ST, h_st)
            for tile in range(4):
                pp, hp_ = (pa, h_pa) if tile % 2 == 0 else (pb, h_pb)
                for cc in range(CC):
                    k.op('pe', lambda e, cc=cc, tile=tile, pp=pp, wo=wo: e.matmul(
                        pp[:], lhsT=yT[:, cc, tile * 128:(tile + 1) * 128], rhs=wo[:, cc * 512:(cc + 1) * 512],
                        start=(cc == 0), stop=(cc == CC - 1)), rd=[h_yT, hwo], wr=[hp_])
                k.op('dve', lambda e, tile=tile, dblk=dblk, pp=pp: e.tensor_tensor(
                    out=hblk[:, tile, dblk * 512:(dblk + 1) * 512], in0=pp[:], in1=hblk[:, tile, dblk * 512:(dblk + 1) * 512],
                    op=ALU.add), rd=[hp_, h_hblk], wr=[h_hblk])
        for tile in range(4):
            rms_to_T(hblk[:, tile, :], gbc[:], hgT, tile * 128, h_hblk, h_hgT, normbufs)
            k.dma('sp', pt_s[:], p_d[layer, tb * 512 + tile * 128: tb * 512 + (tile + 1) * 128, :], wr=[h_pt])
            k.op('pool', lambda e: e.tensor_copy(out=ptb[:], in_=pt_s[:]), rd=[h_pt], wr=[h_pt])
            tp, h_tp = normbufs[5], normbufs[8]
            for c in range(2):
                k.op('pe', lambda e, c=c: e.transpose(out=tp[:, c * 128:(c + 1) * 128], in_=ptb[:, c * 128:(c + 1) * 128],
                                                      identity=ident_b[:]), rd=[h_pt, h_c], wr=[h_tp])
            k.op('act', lambda e, tile=tile: e.activation(out=pT[:, :, tile * 128:(tile + 1) * 128],
                                                          in_=tp[:, 0:256].rearrange("p (j t) -> p j t", j=2), func=AF.Copy),
                 rd=[h_tp], wr=[h_pT])
        for dblk in range(4):
            load_big(pwg_d[layer, dblk], 8192, WB[0], h_wb[0], ST, h_st)
            load_big(pew_d[layer, dblk], 1024, WB[1], h_wb[1], ST, h_st)
            for tile in range(4):
                for dc in range(16):
                    k.op('pe', lambda e, dc=dc, tile=tile: e.matmul(
                        pa[:], lhsT=hgT[:, dc, tile * 128:(tile + 1) * 128], rhs=WB[0][:, dc * 512:(dc + 1) * 512],
                        start=(dc == 0), stop=(dc == 15)), rd=[h_hgT, h_wb[0]], wr=[h_pa])
                for c in range(2):
                    k.op('pe', lambda e, c=c, tile=tile: e.matmul(
                        pb[:], lhsT=pT[:, c, tile * 128:(tile + 1) * 128], rhs=WB[1][:, c * 512:(c + 1) * 512],
                        start=(c == 0), stop=(c == 1)), rd=[h_pT, h_wb[1]], wr=[h_pb])
                k.op('act', lambda e: e.activation(out=sg[:], in_=pa[:], func=AF.Sigmoid), rd=[h_pa], wr=[h_sg])
                k.op('dve', lambda e: e.tensor_tensor(out=ge[:], in0=pb[:], in1=sg[:], op=ALU.mult), rd=[h_pb, h_sg], wr=[h_sg])
                k.op('dve', lambda e, tile=tile, dblk=dblk: e.tensor_tensor(
                    out=hblk[:, tile, dblk * 512:(dblk + 1) * 512], in0=ge[:], in1=hblk[:, tile, dblk * 512:(dblk + 1) * 512],
                    op=ALU.add), rd=[h_sg, h_hblk], wr=[h_hblk])


    def alloc_tail(ls, CC):
        sbl = lambda name, shape, dt: ls.enter_context(nc.sbuf_tensor(un(name), list(shape), dt))
        psl = lambda name, shape, dt: ls.enter_context(nc.psum_tensor(un(name), list(shape), dt))
        ST = [sbl(f"ST{i}", [128, 2048], F32) for i in range(2)]
        WB = [sbl("WB0", [128, 8192], BF16), sbl("WB1", [128, 2048], BF16)]
        WO = [sbl("WO0", [128, CC * 512], BF16)]
        hgT = sbl("hgT", [128, 16, 512], BF16)
        ss = sbl("ss", [128, 1], F32); sd = sbl("sd", [128, 1], F32); rs = sbl("rs", [128, 1], F32)
        hp = sbl("hp", [128, 2048], BF16)
        tp = psl("tp", [128, 512], BF16)
        h_hp_ = k.h()
        normbufs = (hp, ss, sd, rs, hp, tp, h_hp_, h_hp_, k.h())
        pa = psl("pa", [128, 512], F32); pb = psl("pb", [128, 512], F32)
        pt_s = sbl("pt_s", [128, 256], F32); ptb = sbl("ptb", [128, 256], BF16)
        pT = sbl("pT", [128, 2, 512], BF16)
        sg = sbl("sg", [128, 512], F32); ge = sbl("ge", [128, 512], F32)
        gbc = sbl("gbc", [128, 2048], F32)
        return (ST, WB, k.hs(2), k.hs(2), WO, k.hs(1), hgT, k.h(), normbufs, pa, pb, k.h(), k.h(), pt_s, ptb, pT,
                k.h(), k.h(), sg, ge, k.h(), gbc)

    def odd_layer(layer, j, h_src, h_dst):
        with ExitStack() as ls:
            sbl = lambda name, shape, dt: ls.enter_context(nc.sbuf_tensor(un(name), list(shape), dt))
            psl = lambda name, shape, dt: ls.enter_context(nc.psum_tensor(un(name), list(shape), dt))
            tl = alloc_tail(ls, 16)
            ST, WB, h_st, h_wb = tl[0], tl[1], tl[2], tl[3]
            normbufs = tl[8]
            pa, pb, h_pa, h_pb = tl[9], tl[10], tl[11], tl[12]
            gbc = tl[21]
            hblk = sbl("hblk", [128, 4, 2048], F32); h_hblk = k.h()
            hnT = tl[6]; h_hnT = tl[7]
            usg = sbl("usg", [128, 16, 512], BF16); h_usg = k.h()
            tmpb = sbl("tmpb", [128, 512], BF16); h_tmpb = k.h()
            WC = [sbl(f"WC{i}", [128, 2048], BF16) for i in range(2)]; h_wc = k.hs(2)
            gv = sbl("gv", [128, 4, 2048], F32); h_gv = k.h()
            z = normbufs[4]; h_z = normbufs[7]
            yT = usg; h_yT = h_usg
            nbc = sbl("nbc", [128, 2048], F32)
            ovec = sbl("ovec", [128, 32], F32)
            wsT_f = gv[:, 0, :]
            wsT_b = sbl("wsT_b", [128, 16, 128], BF16)
            bsbc = gv[:, 1, :]
            Cg = sbl("Cg", [128, 16, 128], F32)
            st6 = sbl("st6", [128, 4, 6], F32); mv = sbl("mv", [128, 2], F32); h_st6 = k.h()
            lsd = sbl("lsd", [128, 1], F32); lrs = sbl("lrs", [128, 1], F32); nmr = sbl("nmr", [128, 1], F32)
            mx = sbl("mx", [128, 128], F32); h_mx = k.h()
            mps = psl("mps", [128, 512], F32); h_mps = k.h()
            h_lc = k.h()
            k.dma('sp', nbc[:], ng_d[layer, :].partition_broadcast(128), wr=[h_c])
            k.dma('sp', gbc[:], pgn_d[layer, :].partition_broadcast(128), wr=[h_c])
            k.dma('sp', ovec[:], ovec_d[j], wr=[h_lc])
            k.dma('sp', wsT_f, wst_d[j], wr=[h_lc])
            for g in range(16):
                k.dma('sp', bsbc[:, g * 128:(g + 1) * 128], bs_d[j, g, :].partition_broadcast(128), wr=[h_lc])
            wv = wsT_f.rearrange("s (g t) -> s g t", g=16)
            k.op('dve', lambda e: e.memset(wv[64:128, :, 0:64], 0.0), rd=[h_lc], wr=[h_lc])
            k.op('dve', lambda e: e.tensor_copy(out=wsT_b[:], in_=wv), rd=[h_lc], wr=[h_lc])
            for g in range(16):
                k.op('pe', lambda e, g=g: e.matmul(mps[:, 0:128], lhsT=ones1[:], rhs=wv[:, g, :], start=True, stop=True),
                     rd=[h_lc, h_c], wr=[h_mps])
                k.op('dve', lambda e, g=g: e.scalar_tensor_tensor(out=Cg[:, g, :], in0=mps[:, 0:128], scalar=ovec[:, 16 + g:17 + g],
                                                                   in1=bsbc[:, g * 128:(g + 1) * 128], op0=ALU.mult, op1=ALU.add),
                     rd=[h_mps, h_lc], wr=[h_lc])
            k.barrier()
            for tb in range(8):
                k.maybe_barrier()
                t0 = tb * 512
                k.dma('sp', hblk[:], h_src[t0:t0 + 512, :].rearrange("(i p) d -> p i d", p=128), wr=[h_hblk])
                for tile in range(4):
                    rms_to_T(hblk[:, tile, :], nbc[:], hnT, tile * 128, h_hblk, h_hnT, normbufs)
                for ch in range(32):
                    wb, hwb = load_w(owf_d[j, ch], 2048, ST, WC, h_st, h_wc)
                    pp, hp_ = (pa, h_pa) if ch % 2 == 0 else (pb, h_pb)
                    for dc in range(16):
                        k.op('pe', lambda e, dc=dc, wb=wb, pp=pp: e.matmul(pp[:], lhsT=wb[:, dc * 128:(dc + 1) * 128], rhs=hnT[:, dc, :],
                                                                           start=(dc == 0), stop=(dc == 15)), rd=[hwb, h_hnT], wr=[hp_])
                    if ch < 16:
                        k.op('act', lambda e, ch=ch, pp=pp: e.activation(out=usg[:, ch, :], in_=pp[:], func=AF.Gelu_apprx_tanh),
                             rd=[hp_], wr=[h_usg])
                    else:
                        k.op('act', lambda e, pp=pp: e.activation(out=tmpb[:], in_=pp[:], func=AF.Silu), rd=[hp_], wr=[h_tmpb])
                        k.op('dve', lambda e, ch=ch: e.tensor_tensor(out=usg[:, ch - 16, :], in0=usg[:, ch - 16, :], in1=tmpb[:], op=ALU.mult),
                             rd=[h_tmpb, h_usg], wr=[h_usg])
                for vb in range(4):
                    load_big(owv_d[j, vb], 8192, WB[0], h_wb[0], ST, h_st)
                    for tile in range(4):
                        pp, hp_ = (pa, h_pa) if tile % 2 == 0 else (pb, h_pb)
                        for dc in range(16):
                            k.op('pe', lambda e, dc=dc, tile=tile, pp=pp: e.matmul(
                                pp[:], lhsT=hnT[:, dc, tile * 128:(tile + 1) * 128], rhs=WB[0][:, dc * 512:(dc + 1) * 512],
                                start=(dc == 0), stop=(dc == 15)), rd=[h_wb[0], h_hnT], wr=[hp_])
                        k.op('act', lambda e, tile=tile, vb=vb, pp=pp: e.activation(out=gv[:, tile, vb * 512:(vb + 1) * 512], in_=pp[:],
                                                                                   func=AF.Gelu_apprx_tanh), rd=[hp_], wr=[h_gv])
                for tile in range(4):
                    for q4 in range(4):
                        k.op('dve', lambda e, q4=q4, tile=tile: e.bn_stats(out=st6[:, q4, :], in_=gv[:, tile, q4 * 512:(q4 + 1) * 512]),
                             rd=[h_gv], wr=[h_st6])
                    k.op('dve', lambda e: e.bn_aggr(out=mv[:], in_=st6[:].rearrange("p a b -> p (a b)")), rd=[h_st6], wr=[h_st6])
                    k.op('act', lambda e: e.activation(out=lsd[:], in_=mv[:, 1:2], func=AF.Sqrt, bias=epsc[:, 0:1]), rd=[h_st6, h_c], wr=[h_st6])
                    k.op('dve', lambda e: e.reciprocal(out=lrs[:], in_=lsd[:]), rd=[h_st6], wr=[h_st6])
                    k.op('dve', lambda e: e.scalar_tensor_tensor(out=nmr[:], in0=mv[:, 0:1], scalar=-1.0, in1=lrs[:], op0=ALU.mult, op1=ALU.mult),
                         rd=[h_st6], wr=[h_st6])
                    k.op('dve', lambda e, tile=tile: e.tensor_scalar(out=z[:], in0=gv[:, tile, :], scalar1=lrs[:, 0:1], scalar2=nmr[:, 0:1],
                                                                      op0=ALU.mult, op1=ALU.add), rd=[h_gv, h_st6], wr=[h_z])
                    for g in range(16):
                        k.op('pe', lambda e, g=g: e.matmul(mps[:, (g % 4) * 128:(g % 4 + 1) * 128], lhsT=z[:, g * 128:(g + 1) * 128],
                                                            rhs=wsT_b[:, g, :], start=True, stop=True), rd=[h_z, h_lc], wr=[h_mps])
                        k.op('dve', lambda e, g=g: e.scalar_tensor_tensor(out=mx[:], in0=mps[:, (g % 4) * 128:(g % 4 + 1) * 128],
                                                                           scalar=ovec[:, g:g + 1], in1=Cg[:, g, :], op0=ALU.mult, op1=ALU.add),
                             rd=[h_mps, h_lc], wr=[h_mx])
                        k.op('dve', lambda e, g=g, tile=tile: e.tensor_tensor(out=yT[:, g, tile * 128:(tile + 1) * 128], in0=mx[:],
                                                                              in1=usg[:, g, tile * 128:(tile + 1) * 128], op=ALU.mult),
                             rd=[h_mx, h_usg], wr=[h_yT])
                tail_block(layer, tb, hblk, h_hblk, yT, h_yT, 16, owo_d[j], tl)
                k.dma('sp', h_dst[t0:t0 + 512, :].rearrange("(i p) d -> p i d", p=128), hblk[:], rd=[h_hblk])
            k.barrier()


    def even_layer(layer, j, h_src, h_dst):
        CH = ([('xa', i) for i in range(16)] + [('ga', i) for i in range(16)] + [('q', i) for i in range(16)] +
              [('k', i) for i in range(4)] + [('gb', i) for i in range(16)] + [('iq', i) for i in range(8)] + [('ik', 0)])
        with ExitStack() as ls:
          if 'E1' in PH:
              sbl = lambda name, shape, dt: ls.enter_context(nc.sbuf_tensor(un(name), list(shape), dt))
              psl = lambda name, shape, dt: ls.enter_context(nc.psum_tensor(un(name), list(shape), dt))
              ST = [sbl(f"ST{i}", [128, 2048], F32) for i in range(2)]; h_st = k.hs(2)
              WB = [sbl("WB0", [128, 8192], BF16), sbl("WB1", [128, 2048], BF16)]; h_wb = k.hs(2)
              WC = [sbl(f"WC{i}", [128, 2048], BF16) for i in range(2)]; h_wc = k.hs(2)
              hnT = sbl("hnT", [128, 16, 1024], BF16); h_hnT = k.h()
              ht = [sbl(f"ht{i}", [128, 2048], F32) for i in range(2)]; h_ht = k.hs(2)
              ss = sbl("ss", [128, 1], F32); sd = sbl("sd", [128, 1], F32); rs = sbl("rs", [128, 1], F32)
              hp = sbl("hp", [128, 2048], BF16)
              tp = psl("tp", [128, 512], BF16)
              h_hp_ = k.h()
              normbufs = (hp, ss, sd, rs, hp, tp, h_hp_, h_hp_, k.h())
              pa = psl("pa", [128, 512], F32); pb = psl("pb", [128, 512], F32); h_pa, h_pb = k.h(), k.h()
              pm = psl("pm", [128, 512], F32); h_pm = k.h()
              nbc = sbl("nbc", [128, 2048], F32)
              qkn = sbl("qkn", [128, 2], F32); qg = sbl("qg", [128, 2], F32)
              of = [sbl(f"of{i}", [128, 512], F32) for i in range(2)]
              ob = [sbl(f"ob{i}", [128, 512], BF16) for i in range(2)]; h_o = k.hs(2)
              sq = sbl("sq", [128, 512], F32); rsd = sbl("rsd", [128, 512], F32); h_sq = k.h()
              vt = sbl("vt", [128, 4, 130], BF16); h_vt = k.h()
              iwt = sbl("iwt", [128, 16], F32); h_iwt = k.h()
              h_lc = k.h()
              k.dma('sp', nbc[:], ng_d[layer, :].partition_broadcast(128), wr=[h_c])
              k.dma('sp', qkn[:], qkn_d[j], wr=[h_lc])
              k.op('dve', lambda e: e.tensor_scalar(out=qg[:, 0:1], in0=qkn[:, 0:1], scalar1=float(128 ** -0.5), scalar2=None, op0=ALU.mult),
                   rd=[h_lc], wr=[h_lc])
              k.op('dve', lambda e: e.tensor_copy(out=qg[:, 1:2], in_=qkn[:, 1:2]), rd=[h_lc], wr=[h_lc])
              k.op('dve', lambda e: e.memset(vt[:], 1.0), wr=[h_vt])
              dst = {'xa': xa_d, 'ga': sga_d, 'gb': sgb_d, 'q': q_d, 'k': kk_d, 'iq': iq_d, 'ik': ik_d}
              oi = 0
              for blk in range(4):
                  k.maybe_barrier()
                  T0 = blk * 1024
                  for tile in range(8):
                      i = tile % 2
                      k.dma('sp', ht[i][:], h_src[T0 + tile * 128:T0 + (tile + 1) * 128, :], wr=[h_ht[i]])
                      rms_to_T(ht[i][:], nbc[:], hnT, tile * 128, h_ht[i], h_hnT, normbufs)
                  for ci, (kind, idx) in enumerate(CH):
                      wb, hwb = load_w(ewf_d[j, ci], 2048, ST, WC, h_st, h_wc)
                      for half in range(2):
                          pp, hp_ = (pa, h_pa) if half == 0 else (pb, h_pb)
                          for dc in range(16):
                              k.op('pe', lambda e, dc=dc, wb=wb, pp=pp, half=half: e.matmul(
                                  pp[:], lhsT=wb[:, dc * 128:(dc + 1) * 128], rhs=hnT[:, dc, half * 512:(half + 1) * 512],
                                  start=(dc == 0), stop=(dc == 15)), rd=[hwb, h_hnT], wr=[hp_])
                          o = oi % 2; oi += 1
                          rows = slice(idx * 128, (idx + 1) * 128)
                          cols = slice(T0 + half * 512, T0 + (half + 1) * 512)
                          if kind == 'xa':
                              k.op('act', lambda e, pp=pp, o=o: e.activation(out=of[o][:], in_=pp[:], func=AF.Copy), rd=[hp_], wr=[h_o[o]])
                              k.dma('act', xa_d[rows, cols], of[o][:], rd=[h_o[o]])
                          elif kind in ('ga', 'gb'):
                              k.op('act', lambda e, pp=pp, o=o: e.activation(out=ob[o][:], in_=pp[:], func=AF.Silu), rd=[hp_], wr=[h_o[o]])
                              k.dma('act', dst[kind][rows, cols], ob[o][:], rd=[h_o[o]])
                          elif kind in ('iq', 'ik'):
                              k.op('act', lambda e, pp=pp, o=o: e.activation(out=ob[o][:], in_=pp[:], func=AF.Copy), rd=[hp_], wr=[h_o[o]])
                              k.dma('act', dst[kind][rows, cols], ob[o][:], rd=[h_o[o]])
                          else:
                              gi = 0 if kind == 'q' else 1
                              k.op('act', lambda e, pp=pp: e.activation(out=sq[:], in_=pp[:], func=AF.Square), rd=[hp_], wr=[h_sq])
                              k.op('pe', lambda e: e.matmul(pm[:], lhsT=onesm[:], rhs=sq[:], start=True, stop=True), rd=[h_sq, h_c], wr=[h_pm])
                              k.op('act', lambda e: e.activation(out=rsd[:], in_=pm[:], func=AF.Sqrt, bias=epsc[:, 0:1]), rd=[h_pm, h_c], wr=[h_sq])
                              k.op('dve', lambda e: e.reciprocal(out=rsd[:], in_=rsd[:]), rd=[h_sq], wr=[h_sq])
                              k.op('dve', lambda e, pp=pp, o=o, gi=gi: e.scalar_tensor_tensor(
                                  out=ob[o][:], in0=pp[:], scalar=qg[:, gi:gi + 1], in1=rsd[:], op0=ALU.mult, op1=ALU.mult),
                                  rd=[hp_, h_sq, h_lc], wr=[h_o[o]])
                              k.dma('sp', dst[kind][rows, cols], ob[o][:], rd=[h_o[o]])
                  load_big(ewv_d[j], 8192, WB[0], h_wb[0], ST, h_st)
                  load_big(ewiw_d[j], 256, WB[1], h_wb[1], ST, h_st)
                  for tile in range(8):
                      for dc in range(16):
                          k.op('pe', lambda e, dc=dc, tile=tile: e.matmul(pa[:], lhsT=hnT[:, dc, tile * 128:(tile + 1) * 128],
                                                                        rhs=WB[0][:, dc * 512:(dc + 1) * 512], start=(dc == 0), stop=(dc == 15)),
                               rd=[h_wb[0], h_hnT], wr=[h_pa])
                      k.op('act', lambda e: e.activation(out=vt[:, :, 0:128], in_=pa[:].rearrange("p (g d) -> p g d", g=4), func=AF.Copy),
                           rd=[h_pa], wr=[h_vt])
                      k.dma('act', v_d[T0 + tile * 128:T0 + (tile + 1) * 128, :], vt[:].rearrange("p g d -> p (g d)"), rd=[h_vt])
                      for dc in range(16):
                          k.op('pe', lambda e, dc=dc, tile=tile: e.matmul(pb[:, 0:16], lhsT=hnT[:, dc, tile * 128:(tile + 1) * 128],
                                                                        rhs=WB[1][:, dc * 16:(dc + 1) * 16], start=(dc == 0), stop=(dc == 15)),
                               rd=[h_wb[1], h_hnT], wr=[h_pb])
                      k.op('act', lambda e: e.activation(out=iwt[:], in_=pb[:, 0:16], func=AF.Copy), rd=[h_pb], wr=[h_iwt])
                      k.dma('act', iw_d[T0 + tile * 128:T0 + (tile + 1) * 128, :], iwt[:], rd=[h_iwt])
              k.barrier()
        with ExitStack() as ls:
          if 'E2' in PH:
              sbl = lambda name, shape, dt: ls.enter_context(nc.sbuf_tensor(un(name), list(shape), dt))
              psl = lambda name, shape, dt: ls.enter_context(nc.psum_tensor(un(name), list(shape), dt))
              L = 1024
              evec = sbl("evec", [128, 8, 16], F32)
              kap = sbl("kap", [128, 16], F32); kap2 = sbl("kap2", [128, 16], F32); tmpk = sbl("tmpk", [128, 16], F32)
              wr_f = sbl("wr_f", [128, 2048], F32); wi_f = sbl("wi_f", [128, 2048], F32)
              wr_b = sbl("wr_b", [128, 16, 128], BF16); wi_b = sbl("wi_b", [128, 16, 128], BF16)
              h_lc = k.h()
              xin = [sbl(f"xin{i}", [128, 3 + L], F32) for i in range(2)]; h_xin = k.hs(2)
              xc = sbl("xc", [128, L], F32); h_xc = k.h()
              xcb = sbl("xcb", [128, L], BF16); h_xcb = k.h()
              rg = sbl("rg", [128, L], F32); ig = sbl("ig", [128, L], F32); h_rg, h_ig = k.h(), k.h()
              aa = sbl("aa", [128, L], F32); a2 = sbl("a2", [128, L], F32); h_aa, h_a2 = k.h(), k.h()
              bx = sbl("bx", [128, L], F32); h_bx = k.h()
              hsb = [sbl(f"hs{i}", [128, L], F32) for i in range(2)]; h_hs = k.hs(2)
              sga = [sbl(f"sga{i}", [128, L], BF16) for i in range(2)]; h_sga = k.hs(2)
              yb = [sbl(f"yb{i}", [128, L], BF16) for i in range(2)]; h_yb = k.hs(2)
              pa = psl("pa", [128, 512], F32); pb = psl("pb", [128, 512], F32); h_pa, h_pb = k.h(), k.h()
              k.dma('sp', evec[:].rearrange("p a g -> p (a g)"), evec_d[j], wr=[h_lc])
              k.dma('sp', wr_f[:], lwr_d[j], wr=[h_lc])
              k.dma('sp', wi_f[:], lwi_d[j], wr=[h_lc])
              k.op('dve', lambda e: e.tensor_copy(out=wr_b[:].rearrange("p g j -> p (g j)"), in_=wr_f[:]), rd=[h_lc], wr=[h_lc])
              k.op('dve', lambda e: e.tensor_copy(out=wi_b[:].rearrange("p g j -> p (g j)"), in_=wi_f[:]), rd=[h_lc], wr=[h_lc])
              k.op('act', lambda e: e.activation(out=tmpk[:], in_=evec[:, 7, :], func=AF.Exp, scale=-1.0), rd=[h_lc], wr=[h_lc])
              k.op('act', lambda e: e.activation(out=tmpk[:], in_=tmpk[:], func=AF.Ln, bias=onec[:, 0:1]), rd=[h_lc, h_c], wr=[h_lc])
              k.op('dve', lambda e: e.tensor_scalar(out=kap[:], in0=tmpk[:], scalar1=-8.0, scalar2=None, op0=ALU.mult), rd=[h_lc], wr=[h_lc])
              k.op('dve', lambda e: e.tensor_scalar(out=kap2[:], in0=tmpk[:], scalar1=-16.0, scalar2=None, op0=ALU.mult), rd=[h_lc], wr=[h_lc])
              it = 0
              for g in range(16):
                  k.maybe_barrier()
                  rows = slice(g * 128, (g + 1) * 128)
                  for pc in range(T // L):
                      i = it % 2; it += 1
                      t0 = pc * L
                      if pc == 0:
                          k.op('dve', lambda e, i=i: e.memset(xin[i][:, 0:3], 0.0), wr=[h_xin[i]])
                          k.dma('sp', xin[i][:, 3:3 + L], xa_d[rows, t0:t0 + L], wr=[h_xin[i]])
                      else:
                          k.dma('sp', xin[i][:, 0:3 + L], xa_d[rows, t0 - 3:t0 + L], wr=[h_xin[i]])
                      k.dma('sp', sga[i][:], sga_d[rows, t0:t0 + L], wr=[h_sga[i]])
                      k.op('dve', lambda e, i=i, g=g: e.tensor_scalar(out=xc[:], in0=xin[i][:, 3:3 + L], scalar1=evec[:, 3, g:g + 1],
                                                                      scalar2=evec[:, 4, g:g + 1], op0=ALU.mult, op1=ALU.add),
                           rd=[h_xin[i], h_lc], wr=[h_xc])
                      for tap in range(3):
                          k.op('dve', lambda e, i=i, g=g, tap=tap: e.scalar_tensor_tensor(
                              out=xc[:], in0=xin[i][:, tap:tap + L], scalar=evec[:, tap, g:g + 1], in1=xc[:], op0=ALU.mult, op1=ALU.add),
                              rd=[h_xin[i], h_lc, h_xc], wr=[h_xc])
                      k.op('act', lambda e: e.activation(out=xcb[:], in_=xc[:], func=AF.Copy), rd=[h_xc], wr=[h_xcb])
                      for half in range(L // 512):
                          sl = slice(half * 512, (half + 1) * 512)
                          k.op('pe', lambda e, g=g, sl=sl: e.matmul(pa[:], lhsT=wr_b[:, g, :], rhs=xcb[:, sl], start=True, stop=True),
                               rd=[h_xcb, h_lc], wr=[h_pa])
                          k.op('pe', lambda e, g=g, sl=sl: e.matmul(pb[:], lhsT=wi_b[:, g, :], rhs=xcb[:, sl], start=True, stop=True),
                               rd=[h_xcb, h_lc], wr=[h_pb])
                          k.op('act', lambda e, g=g, sl=sl: e.activation(out=rg[:, sl], in_=pa[:], func=AF.Sigmoid, bias=evec[:, 5, g:g + 1]),
                               rd=[h_pa, h_lc], wr=[h_rg])
                          k.op('act', lambda e, g=g, sl=sl: e.activation(out=ig[:, sl], in_=pb[:], func=AF.Sigmoid, bias=evec[:, 6, g:g + 1]),
                               rd=[h_pb, h_lc], wr=[h_ig])
                      k.op('act', lambda e, g=g: e.activation(out=aa[:], in_=rg[:], func=AF.Exp, scale=kap[:, g:g + 1]), rd=[h_rg, h_lc], wr=[h_aa])
                      k.op('act', lambda e, g=g: e.activation(out=a2[:], in_=rg[:], func=AF.Exp, scale=kap2[:, g:g + 1]), rd=[h_rg, h_lc], wr=[h_a2])
                      k.op('act', lambda e: e.activation(out=a2[:], in_=a2[:], func=AF.Sqrt, scale=-1.0, bias=onec[:, 0:1]), rd=[h_a2, h_c], wr=[h_a2])
                      k.op('pool', lambda e: e.tensor_tensor(out=bx[:], in0=a2[:], in1=ig[:], op=ALU.mult), rd=[h_a2, h_ig], wr=[h_bx])
                      k.op('pool', lambda e: e.tensor_tensor(out=bx[:], in0=bx[:], in1=xc[:], op=ALU.mult), rd=[h_bx, h_xc], wr=[h_bx])
                      if pc == 0:
                          k.op('dve', lambda e, i=i: e.tensor_tensor_scan(out=hsb[i][:], data0=aa[:], data1=bx[:], initial=0.0,
                                                                          op0=ALU.mult, op1=ALU.add), rd=[h_aa, h_bx], wr=[h_hs[i]])
                      else:
                          k.op('dve', lambda e, i=i: e.tensor_tensor_scan(out=hsb[i][:], data0=aa[:], data1=bx[:], initial=hsb[1 - i][:, L - 1:L],
                                                                          op0=ALU.mult, op1=ALU.add), rd=[h_aa, h_bx, h_hs[1 - i]], wr=[h_hs[i]])
                      k.op('pool', lambda e, i=i: e.tensor_tensor(out=yb[i][:], in0=hsb[i][:], in1=sga[i][:], op=ALU.mult),
                           rd=[h_hs[i], h_sga[i]], wr=[h_yb[i]])
                      k.dma('sp', yT_d[rows, t0:t0 + L], yb[i][:], rd=[h_yb[i]])
              k.barrier()
        with ExitStack() as ls:
          if 'E3' in PH:
              sbl = lambda name, shape, dt: ls.enter_context(nc.sbuf_tensor(un(name), list(shape), dt))
              psl = lambda name, shape, dt: ls.enter_context(nc.psum_tensor(un(name), list(shape), dt))
              kT = sbl("kT", [128, 4, T], BF16); vA = sbl("vA", [128, 32, 4 * 130], BF16); ik2 = sbl("ik2", [128, T], BF16)
              h_kv = k.h()
              for g in range(4):
                  k.dma('sp', kT[:, g, :], kk_d[g * 128:(g + 1) * 128, :], wr=[h_kv])
              for c in range(4):
                  k.dma('sp', vA[:, c * 8:(c + 1) * 8, :], v_d[c * 1024:(c + 1) * 1024, :].rearrange("(i p) d -> p i d", p=128), wr=[h_kv])
              k.dma('sp', ik2[:], ik_d, wr=[h_kv])
              qt = [sbl(f"qt{i}", [128, 16, 128], BF16) for i in range(2)]
              iqt = [sbl(f"iqt{i}", [128, 8, 128], BF16) for i in range(2)]
              iwt = [sbl(f"iwt{i}", [128, 16], F32) for i in range(2)]
              sgbt = [sbl(f"sgbt{i}", [128, 16, 128], BF16) for i in range(2)]
              h_ld = k.hs(2)
              dg = sbl("dg", [128, 16, 128], BF16); h_dg = k.h()
              Rb = [sbl(f"Rb{i}", [128, 512], BF16) for i in range(2)]; h_R = k.hs(2)
              acc = sbl("acc", [128, T], F32); h_acc = k.h()
              cj = sbl("cj", [128, T], BF16); h_cj = k.h()
              am = sbl("am", [128, 1], F32); lo = sbl("lo", [128, 1], F32); w0 = sbl("w0", [128, 1], F32)
              mid = sbl("mid", [128, 1], F32); cnt = sbl("cnt", [128, 1], F32); t1 = sbl("t1", [128, 1], F32); h_bs = k.h()
              mask = sbl("mask", [128, T], BF16); h_mask = k.h()
              mT4 = sbl("mT4", [128, 32, 4, 128], BF16); h_mT = k.h()
              Mn = sbl("Mn", [128, 2, 16, 128], BF16); h_Mn = k.h()
              Eb = [sbl(f"Eb{i}", [128, 512], BF16) for i in range(3)]; h_E = k.hs(3)
              Pb = [sbl(f"Pb{i}", [128, 4, 128], BF16) for i in range(3)]; h_P = k.hs(3)
              ri = sbl("ri", [128, 4], F32); h_ri = k.h()
              ot = sbl("ot", [128, 16, 128], BF16); h_ot = k.h()
              yt = [sbl(f"yt{i}", [128, 16, 128], BF16) for i in range(2)]; h_yt = k.hs(2)
              pA = psl("pA", [128, 512], F32); pB = psl("pB", [128, 512], F32); pC = psl("pC", [128, 512], F32)
              h_pA, h_pB, h_pC = k.h(), k.h(), k.h()
              mtp = psl("mtp", [128, 512], BF16); h_mtp = k.h()
              oacc = [psl(f"oacc{i}", [128, 512], F32) for i in range(4)]; h_oa = k.hs(4)
              EBv = EB[:].rearrange("s (k h t) -> s k h t", k=2, h=16)
              zi = 0
              for n in range(NT):
                  k.maybe_barrier()
                  b = n % 2
                  S = 128 * (n + 1)
                  c0 = n * 128
                  k.dma('sp', qt[b][:], q_d[:, c0:c0 + 128].rearrange("(h p) t -> p h t", p=128), wr=[h_ld[b]])
                  k.dma('sp', iqt[b][:], iq_d[:, c0:c0 + 128].rearrange("(h p) t -> p h t", p=128), wr=[h_ld[b]])
                  k.dma('sp', iwt[b][:], iw_d[c0:c0 + 128, :], wr=[h_ld[b]])
                  k.dma('sp', sgbt[b][:], sgb_d[:, c0:c0 + 128].rearrange("(h p) t -> p h t", p=128), wr=[h_ld[b]])
                  for h in range(16):
                      k.op('pool', lambda e, h=h, b=b: e.tensor_scalar(out=dg[:, h, :], in0=ident_f[:], scalar1=iwt[b][:, h:h + 1], scalar2=0.0,
                                                                       op0=ALU.mult, op1=ALU.add), rd=[h_ld[b], h_c], wr=[h_dg])
                  nkb = (S + 511) // 512
                  steps = [(kb, h) for kb in range(nkb) for h in range(16)]

                  def idxA(jj):
                      kb, h = steps[jj]
                      w = min(512, S - kb * 512)
                      c = h // 2; p0 = 64 * (h % 2)
                      zp, hz = (pA, h_pA) if jj % 2 == 0 else (pB, h_pB)
                      R, hR = Rb[jj % 2], h_R[jj % 2]
                      k.op('pe', lambda e: e.matmul(
                          zp[:, 0:w], lhsT=iqt[b][p0:p0 + 64, c, :], rhs=ik2[p0:p0 + 64, kb * 512:kb * 512 + w], start=True, stop=True),
                          rd=[h_ld[b], h_kv], wr=[hz])
                      k.op('act', lambda e: e.activation(out=R[:, 0:w], in_=zp[:, 0:w], func=AF.Relu), rd=[hz], wr=[hR])

                  def idxB(jj):
                      kb, h = steps[jj]
                      w = min(512, S - kb * 512)
                      R, hR = Rb[jj % 2], h_R[jj % 2]
                      k.op('pe', lambda e: e.matmul(pC[:, 0:w], lhsT=dg[:, h, :], rhs=R[:, 0:w], start=(h == 0), stop=(h == 15)),
                           rd=[h_dg, hR], wr=[h_pC])
                      if h == 15:
                          k.op('dve', lambda e: e.tensor_copy(out=acc[:, kb * 512:kb * 512 + w], in_=pC[:, 0:w]), rd=[h_pC], wr=[h_acc])

                  idxA(0)
                  for jj in range(len(steps)):
                      if jj + 1 < len(steps):
                          idxA(jj + 1)
                      idxB(jj)
                  k.op('dve', lambda e, S=S: e.tensor_reduce(out=am[:], in_=acc[:, 0:S], axis=AX.X, op=ALU.max, apply_absolute_value=True),
                       rd=[h_acc], wr=[h_bs])
                  k.op('dve', lambda e: e.tensor_scalar(out=lo[:], in0=am[:], scalar1=-1.001, scalar2=-1e-20, op0=ALU.mult, op1=ALU.add), rd=[h_bs], wr=[h_bs])
                  k.op('dve', lambda e: e.tensor_scalar(out=w0[:], in0=am[:], scalar1=2.002, scalar2=2e-20, op0=ALU.mult, op1=ALU.add), rd=[h_bs], wr=[h_bs])
                  k.op('dve', lambda e, S=S: e.tensor_tensor(out=acc[:, S - 128:S], in0=acc[:, S - 128:S], in1=diagneg[:], op=ALU.add),
                       rd=[h_acc, h_c], wr=[h_acc])
                  if S > 256:
                      for itn in range(1, NIT + 1):
                          f = 2.0 ** (-itn)
                          k.op('dve', lambda e, f=f: e.scalar_tensor_tensor(out=mid[:], in0=w0[:], scalar=f, in1=lo[:], op0=ALU.mult, op1=ALU.add),
                               rd=[h_bs], wr=[h_bs])
                          k.op('dve', lambda e, S=S: e.tensor_scalar(out=cj[:, 0:S], in0=acc[:, 0:S], scalar1=mid[:, 0:1], scalar2=None,
                                                                     op0=ALU.is_ge, op1=ALU.add, accum_out=cnt[:, 0:1]), rd=[h_acc, h_bs], wr=[h_cj, h_bs])
                          k.op('dve', lambda e, f=f: e.tensor_scalar(out=t1[:], in0=cnt[:], scalar1=255.5, scalar2=f, op0=ALU.is_ge, op1=ALU.mult),
                               rd=[h_bs], wr=[h_bs])
                          k.op('dve', lambda e: e.scalar_tensor_tensor(out=lo[:], in0=t1[:], scalar=w0[:, 0:1], in1=lo[:], op0=ALU.mult, op1=ALU.add),
                               rd=[h_bs], wr=[h_bs])
                  k.op('dve', lambda e, S=S: e.tensor_scalar(out=mask[:, 0:S], in0=acc[:, 0:S], scalar1=lo[:, 0:1], scalar2=None, op0=ALU.is_ge),
                       rd=[h_acc, h_bs], wr=[h_mask])
                  for k0 in range(0, n + 1, 4):
                      nk = min(4, n + 1 - k0)
                      for q in range(nk):
                          k.op('pe', lambda e, q=q, k0=k0: e.transpose(out=mtp[:, q * 128:(q + 1) * 128], in_=mask[:, (k0 + q) * 128:(k0 + q + 1) * 128],
                                                                       identity=ident_b[:]), rd=[h_mask, h_c], wr=[h_mtp])
                      for r in range(4):
                          eng = 'act' if r % 2 == 0 else 'dve'
                          if eng == 'act':
                              k.op('act', lambda e, r=r, k0=k0, nk=nk: e.activation(out=mT4[:, k0:k0 + nk, r, :],
                                                                                   in_=mtp[:, 0:nk * 128].rearrange("p (q t) -> p q t", q=nk), func=AF.Copy),
                                   rd=[h_mtp], wr=[h_mT])
                          else:
                              k.op('dve', lambda e, r=r, k0=k0, nk=nk: e.tensor_copy(out=mT4[:, k0:k0 + nk, r, :],
                                                                                    in_=mtp[:, 0:nk * 128].rearrange("p (q t) -> p q t", q=nk)),
                                   rd=[h_mtp], wr=[h_mT])
                  for kind in range(2):
                      if n - kind < 0:
                          continue
                      for g4 in range(4):
                          k.op('pool', lambda e, kind=kind, g4=g4, n=n: e.tensor_tensor(out=Mn[:, kind, g4 * 4:(g4 + 1) * 4, :],
                                                                                        in0=EBv[:, kind, g4 * 4:(g4 + 1) * 4, :],
                                                                                        in1=mT4[:, n - kind, :, :], op=ALU.mult),
                               rd=[h_mT, h_c], wr=[h_Mn])
                  pairs = [(g, kt) for g in range(4) for kt in range(n + 1)]

                  def attA(ii):
                      g, kt = pairs[ii]
                      sp_, hs_ = [(pA, h_pA), (pB, h_pB), (pC, h_pC)][ii % 3]
                      E_, hE = Eb[ii % 3], h_E[ii % 3]
                      P_, hP = Pb[ii % 3], h_P[ii % 3]
                      k.op('pe', lambda e: e.matmul(sp_[:], lhsT=kT[:, g, kt * 128:(kt + 1) * 128],
                                                    rhs=qt[b][:, g * 4:(g + 1) * 4, :], start=True, stop=True),
                           rd=[h_kv, h_ld[b]], wr=[hs_])
                      k.op('act', lambda e: e.activation(out=E_[:], in_=sp_[:], func=AF.Exp), rd=[hs_], wr=[hE])
                      kind = n - kt
                      if kind < 2:
                          k.op('dve', lambda e: e.tensor_tensor(
                              out=P_[:], in0=E_[:].rearrange("p (r t) -> p r t", r=4), in1=Mn[:, kind, g * 4:(g + 1) * 4, :], op=ALU.mult),
                              rd=[hE, h_Mn], wr=[hP])
                      else:
                          k.op('dve', lambda e: e.tensor_tensor(
                              out=P_[:], in0=E_[:].rearrange("p (r t) -> p r t", r=4), in1=mT4[:, kt, :, :], op=ALU.mult),
                              rd=[hE, h_mT], wr=[hP])

                  def attB(ii):
                      g, kt = pairs[ii]
                      P_, hP = Pb[ii % 3], h_P[ii % 3]
                      for r in range(4):
                          k.op('pe', lambda e, r=r: e.matmul(
                              oacc[r][:, 0:129], lhsT=P_[:, r, :], rhs=vA[:, kt, g * 130:g * 130 + 129], start=(kt == 0), stop=(kt == n)),
                              rd=[hP, h_kv], wr=[h_oa[r]])
                      if kt == n:
                          for r in range(4):
                              k.op('dve', lambda e, r=r: e.reciprocal(out=ri[:, r:r + 1], in_=oacc[r][:, 128:129]), rd=[h_oa[r]], wr=[h_ri])
                              k.op('dve', lambda e, r=r: e.tensor_scalar(out=ot[:, g * 4 + r, :], in0=oacc[r][:, 0:128], scalar1=ri[:, r:r + 1],
                                                                         scalar2=None, op0=ALU.mult), rd=[h_oa[r], h_ri], wr=[h_ot])

                  attA(0)
                  if len(pairs) > 1:
                      attA(1)
                  for ii in range(len(pairs)):
                      if ii + 2 < len(pairs):
                          attA(ii + 2)
                      attB(ii)
                  for g4 in range(4):
                      for r in range(4):
                          k.op('pe', lambda e, g4=g4, r=r: e.transpose(out=mtp[:, r * 128:(r + 1) * 128], in_=ot[:, g4 * 4 + r, :], identity=ident_b[:]),
                               rd=[h_ot, h_c], wr=[h_mtp])
                      k.op('dve', lambda e, g4=g4, b=b: e.tensor_tensor(out=yt[b][:, g4 * 4:(g4 + 1) * 4, :], in0=mtp[:].rearrange("p (r t) -> p r t", r=4),
                                                                        in1=sgbt[b][:, g4 * 4:(g4 + 1) * 4, :], op=ALU.mult),
                           rd=[h_mtp, h_ld[b]], wr=[h_yt[b]])
                  k.dma('sp', yT_d[2048:4096, c0:c0 + 128].rearrange("(h p) t -> p h t", p=128), yt[b][:], rd=[h_yt[b]])
              k.barrier()
        with ExitStack() as ls:
          if 'E4' in PH:
              sbl = lambda name, shape, dt: ls.enter_context(nc.sbuf_tensor(un(name), list(shape), dt))
              tl = alloc_tail(ls, 32)
              gbc = tl[21]
              hblk = sbl("hblk", [128, 4, 2048], F32); h_hblk = k.h()
              yT = sbl("yTs", [128, 32, 512], BF16); h_yT = k.h()
              h_lc = k.h()
              k.dma('sp', gbc[:], pgn_d[layer, :].partition_broadcast(128), wr=[h_c])
              for tb in range(8):
                  k.maybe_barrier()
                  t0 = tb * 512
                  k.dma('sp', hblk[:], h_src[t0:t0 + 512, :].rearrange("(i p) d -> p i d", p=128), wr=[h_hblk])
                  for c in range(4):
                      k.dma('sp', yT[:, c * 8:(c + 1) * 8, :], yT_d[c * 1024:(c + 1) * 1024, t0:t0 + 512].rearrange("(c p) t -> p c t", p=128), wr=[h_yT])
                  tail_block(layer, tb, hblk, h_hblk, yT, h_yT, 32, ewo_d[j], tl)
                  k.dma('sp', h_dst[t0:t0 + 512, :].rearrange("(i p) d -> p i d", p=128), hblk[:], rd=[h_hblk])
              k.barrier()

    for layer in layer_list:
        j = layer // 2
        h_src = x_d if layer == layer_list[0] else hbuf_d
        h_dst = y_d if layer == layer_list[-1] else hbuf_d
        if layer % 2 == 0:
            even_layer(layer, j, h_src, h_dst)
        else:
            odd_layer(layer, j, h_src, h_dst)

    k.barrier()
    es.close()
    return nc


def _prep_inputs(inp):
    f = lambda a: np.ascontiguousarray(np.asarray(a, dtype=np.float32))
    W = {kk: f(v) for kk, v in inp.items()}
    rng = np.arange
    shared = {}
    shared["norm_gain"] = W["norm_gain"]
    shared["pe_gate_norm"] = W["pe_gate_norm"]
    shared["consts"] = np.eye(128, dtype=np.float32)
    s_ = rng(128)[:, None]
    t_ = rng(128)[None, :]
    oh = np.zeros((32, 2, 128, 128), np.float32)
    for kind in range(2):
        bk = t5_bucket_np((s_ - t_ - 128 * kind).astype(np.int32))
        for b in range(32):
            oh[b, kind] = (bk == b).astype(np.float32)
    oh[15] -= 1.0
    shared["oh"] = oh.reshape(32, -1)
    shared["rel_bias"] = W["rel_bias"]
    cidx = np.concatenate([rng(0, 2048), rng(2048, 4096), rng(4096, 6144), rng(6144, 6656), rng(7168, 9216),
                           rng(9216, 10240), rng(10240, 10304), rng(10240, 10304)])
    ewf = np.empty((2, 77, 128, 2048), np.float32)
    ewv = np.empty((2, 128, 8192), np.float32)
    ewiw = np.empty((2, 128, 256), np.float32)
    ewo = np.empty((2, 4, 128, 32 * 512), np.float32)
    lwr = np.empty((2, 128, 2048), np.float32)
    lwi = np.empty((2, 128, 2048), np.float32)
    evec = np.empty((2, 128, 8, 16), np.float32)
    qkn = np.empty((2, 128, 2), np.float32)
    owf = np.empty((2, 32, 128, 2048), np.float32)
    owv = np.empty((2, 4, 128, 8192), np.float32)
    owo = np.empty((2, 4, 128, 8192), np.float32)
    wst = np.empty((2, 128, 2048), np.float32)
    ovec = np.empty((2, 128, 32), np.float32)
    for j in range(2):
        Wi = W["even_w_in"][j]
        ewf[j] = Wi[:, cidx].reshape(16, 128, 77, 128).transpose(2, 1, 0, 3).reshape(77, 128, 2048)
        ewv[j] = Wi[:, 6656:7168].reshape(16, 128, 512).transpose(1, 0, 2).reshape(128, 8192)
        ewiw[j] = Wi[:, 10304:10320].reshape(16, 128, 16).transpose(1, 0, 2).reshape(128, 256)
        ewo[j] = W["even_w_out"][j].reshape(32, 128, 4, 512).transpose(2, 1, 0, 3).reshape(4, 128, 32 * 512)
        lwr[j] = W["lru_w_r"][j].transpose(1, 0, 2).reshape(128, 2048)
        lwi[j] = W["lru_w_i"][j].transpose(1, 0, 2).reshape(128, 2048)
        pc = lambda v: v.reshape(16, 128).T
        for a in range(4):
            evec[j, :, a, :] = pc(W["conv_w"][j, a])
        evec[j, :, 4, :] = pc(W["conv_b"][j])
        evec[j, :, 5, :] = pc(W["lru_b_r"][j].reshape(-1))
        evec[j, :, 6, :] = pc(W["lru_b_i"][j].reshape(-1))
        evec[j, :, 7, :] = pc(W["lru_lambda"][j])
        qkn[j, :, 0] = W["q_norm"][j]
        qkn[j, :, 1] = W["k_norm"][j]
        Wo = W["odd_w_in"][j]
        oidx = np.concatenate([rng(0, 2048), rng(4096, 6144)])
        owf[j] = Wo[:, oidx].reshape(16, 128, 32, 128).transpose(2, 1, 0, 3).reshape(32, 128, 2048)
        owv[j] = Wo[:, 2048:4096].reshape(16, 128, 4, 512).transpose(2, 1, 0, 3).reshape(4, 128, 8192)
        owo[j] = W["odd_w_out"][j].reshape(16, 128, 4, 512).transpose(2, 1, 0, 3).reshape(4, 128, 8192)
        wst[j] = W["sg_w_s"][j].transpose(2, 0, 1).reshape(128, 2048)
        ovec[j, :, 0:16] = pc(W["sg_ln_g"][j])
        ovec[j, :, 16:32] = pc(W["sg_ln_b"][j])
    shared.update(ewf=ewf, ewv=ewv, ewiw=ewiw, ewo=ewo, lwr=lwr, lwi=lwi, evec=evec.reshape(2, 128, 128), qkn=qkn,
                  owf=owf, owv=owv, owo=owo, wst=wst, ovec=ovec, bs=W["sg_b_s"])
    shared["pew"] = np.ascontiguousarray(W["pe_w"].reshape(4, 2, 128, 4, 512).transpose(0, 3, 2, 1, 4).reshape(4, 4, 128, 1024))
    shared["pwg"] = np.ascontiguousarray(W["pe_w_gate"].reshape(4, 16, 128, 4, 512).transpose(0, 3, 2, 1, 4).reshape(4, 4, 128, 8192))
    shared = {kk: np.ascontiguousarray(v, dtype=np.float32) for kk, v in shared.items()}
    maps = []
    for b in range(2):
        m = dict(shared)
        m["x"] = np.ascontiguousarray(W["x"][b])
        m["p"] = np.ascontiguousarray(W["p"][:, b])
        maps.append(m)
    return maps


def kernel(**inputs):
    ll = os.environ.get("KLAYERS")
    layer_list = [int(v) for v in ll.split(",")] if ll else [0, 1, 2, 3]
    maps = _prep_inputs(inputs)
    nc = build_program(layer_list)
    res = run_bass_kernel_spmd(nc, maps, core_ids=[0, 1], trace=bool(os.environ.get('KTRACE')))
    if os.environ.get('KTRACE'):
        print('EXEC_TIME_NS', res.exec_time_ns, flush=True)
    out = np.stack([np.asarray(res.results[b]["y"], dtype=np.float32) for b in range(2)], axis=0)
    if os.environ.get("KDEBUG"):
        global LAST_RESULTS
        LAST_RESULTS = res.results
    return out
```

```python
import os
from contextlib import ExitStack
import numpy as np
import concourse.bass as bass
import concourse.mybir as mybir
from concourse.bass_utils import run_bass_kernel_spmd

F32 = mybir.dt.float32
BF16 = mybir.dt.bfloat16
AF = mybir.ActivationFunctionType
ALU = mybir.AluOpType
AX = mybir.AxisListType

T = 4096
D = 2048
NT = T // 128
EPS = 1e-6
NEG = -1.0e30
NIT = 16
NDS = 12
EPOCH_LIMIT = 9000
SAME = {'pe': False, 'act': True, 'dve': True, 'pool': True, 'sp': False}


class H:
    __slots__ = ('w', 'r')

    def __init__(self):
        self.w = None
        self.r = {}


class K:
    def __init__(self, nc, es):
        self.nc = nc
        self.E = {'pe': nc.tensor, 'act': nc.scalar, 'dve': nc.vector, 'pool': nc.gpsimd, 'sp': nc.sync}
        self.sems = [{k: es.enter_context(nc.semaphore(f"e{s}_{k}")) for k in self.E} for s in range(3)]
        self.dsems = [[es.enter_context(nc.semaphore(f"d{s}_{i}")) for i in range(NDS)] for s in range(3)]
        self.ep = 0
        self.cnt = {k: 0 for k in self.E}
        self.dval = [0] * NDS
        self.dnext = 0
        self.seen = {k: {} for k in self.E}
        self.handles = []

    def h(self):
        x = H()
        self.handles.append(x)
        return x

    def hs(self, n):
        return [self.h() for _ in range(n)]

    def _wait(self, e, key, val):
        if key[0] == 'e' and key[1] == e and not SAME[e]:
            return
        if self.seen[e].get(key, 0) >= val:
            return
        s = self.ep % 3
        sem = self.sems[s][key[1]] if key[0] == 'e' else self.dsems[s][key[1]]
        self.E[e].wait_ge(sem, val)
        self.seen[e][key] = val

    def _deps(self, e, rd, wr):
        d = {}
        for h in rd:
            if h.w is not None:
                d[h.w[0]] = max(d.get(h.w[0], 0), h.w[1])
        for h in wr:
            if h.w is not None:
                d[h.w[0]] = max(d.get(h.w[0], 0), h.w[1])
            for kk, v in h.r.items():
                d[kk] = max(d.get(kk, 0), v)
        for kk, v in d.items():
            self._wait(e, kk, v)

    def _mark(self, tok, rd, wr):
        for h in wr:
            h.w = tok
            h.r = {}
        for h in rd:
            if h not in wr:
                h.r[tok[0]] = max(h.r.get(tok[0], 0), tok[1])

    def op(self, e, fn, rd=(), wr=()):
        self._deps(e, rd, wr)
        ins = fn(self.E[e])
        self.cnt[e] += 1
        ins.then_inc(self.sems[self.ep % 3][e], 1)
        self._mark((('e', e), self.cnt[e]), rd, wr)

    def dma(self, q, out, in_, rd=(), wr=()):
        i = self.dnext
        self.dnext = (i + 1) % NDS
        if self.dval[i] > 0:
            self._wait(q, ('d', i), self.dval[i])
        self._deps(q, rd, wr)
        ins = self.E[q].dma_start(out=out, in_=in_)
        self.dval[i] += 16
        ins.then_inc(self.dsems[self.ep % 3][i], 16)
        self._mark((('d', i), self.dval[i]), rd, wr)

    def barrier(self):
        for e in self.E:
            for o in self.E:
                if o != e and self.cnt[o] > 0:
                    self._wait(e, ('e', o), self.cnt[o])
            for i in range(NDS):
                if self.dval[i] > 0:
                    self._wait(e, ('d', i), self.dval[i])
        self.ep += 1
        self.cnt = {k: 0 for k in self.E}
        self.dval = [0] * NDS
        self.seen = {k: {} for k in self.E}
        for h in self.handles:
            h.w = None
            h.r = {}
        if self.ep >= 2:
            o = (self.ep + 1) % 3
            for e in ('pe', 'act', 'dve', 'pool', 'sp'):
                self.op(e, lambda eng, e=e: eng.sem_clear(self.sems[o][e]))
            for i in range(NDS):
                self.op('sp', lambda eng, i=i: eng.sem_clear(self.dsems[o][i]))

    def maybe_barrier(self):
        if max(self.cnt.values()) > EPOCH_LIMIT or max(self.dval) > EPOCH_LIMIT * 16:
            self.barrier()


_UN = [0]


def un(name):
    _UN[0] += 1
    return f"{name}_u{_UN[0]}"


def t5_bucket_np(rel):
    half = 16
    max_exact = 8
    ret = np.where(rel > 0, half, 0)
    n = np.abs(rel)
    nf = np.maximum(n, 1).astype(np.float32)
    large = max_exact + (np.log(nf / max_exact) / np.float32(np.log(128 / max_exact)) * (half - max_exact)).astype(np.int32)
    large = np.minimum(large, half - 1)
    return ret + np.where(n < max_exact, n, large)


def build_program(layer_list):
    nc = bass.Bass("TRN2", target_bir_lowering=False)
    es = ExitStack()
    k = K(nc, es)

    def din(name, shape, dt=F32):
        return nc.dram_tensor(name, list(shape), dt, kind="ExternalInput").ap()

    DBG = bool(os.environ.get("KDEBUG"))
    PH = os.environ.get("KPHASES", "E1,E2,E3,E4").split(",")

    def dscr(name, shape, dt):
        return nc.dram_tensor(name, list(shape), dt, kind=("ExternalOutput" if DBG else "Internal")).ap()

    x_d = din("x", [T, D])
    p_d = din("p", [4, T, 256])
    ng_d = din("norm_gain", [4, D])
    pgn_d = din("pe_gate_norm", [4, D])
    consts_d = din("consts", [128, 128])
    oh_d = din("oh", [32, 2 * 128 * 128])
    rb_d = din("rel_bias", [32, 16])
    ewf_d = din("ewf", [2, 77, 128, 16 * 128])
    ewv_d = din("ewv", [2, 128, 16 * 512])
    ewiw_d = din("ewiw", [2, 128, 16 * 16])
    ewo_d = din("ewo", [2, 4, 128, 32 * 512])
    lwr_d = din("lwr", [2, 128, 16 * 128])
    lwi_d = din("lwi", [2, 128, 16 * 128])
    evec_d = din("evec", [2, 128, 8 * 16])
    qkn_d = din("qkn", [2, 128, 2])
    owf_d = din("owf", [2, 32, 128, 16 * 128])
    owv_d = din("owv", [2, 4, 128, 16 * 512])
    owo_d = din("owo", [2, 4, 128, 16 * 512])
    wst_d = din("wst", [2, 128, 16 * 128])
    ovec_d = din("ovec", [2, 128, 2 * 16])
    bs_d = din("bs", [2, 16, 128])
    pew_d = din("pew", [4, 4, 128, 2 * 512])
    pwg_d = din("pwg", [4, 4, 128, 16 * 512])
    y_d = nc.dram_tensor("y", [T, D], F32, kind="ExternalOutput").ap()

    hbuf_d = dscr("hbuf", [T, D], F32)
    xa_d = dscr("xaT", [2048, T], F32)
    sga_d = dscr("sgaT", [2048, T], BF16)
    sgb_d = dscr("sgbT", [2048, T], BF16)
    q_d = dscr("qT", [2048, T], BF16)
    kk_d = dscr("kT", [512, T], BF16)
    iq_d = dscr("iqT", [1024, T], BF16)
    ik_d = dscr("ikT", [128, T], BF16)
    v_d = dscr("vaug", [T, 4 * 130], BF16)
    iw_d = dscr("iw", [T, 16], F32)
    yT_d = dscr("yT", [4096, T], BF16)

    sb = lambda name, shape, dt: es.enter_context(nc.sbuf_tensor(un(name), list(shape), dt))
    ps = lambda name, shape, dt: es.enter_context(nc.psum_tensor(un(name), list(shape), dt))

    ident_f = sb("ident_f", [128, 128], F32); h_ident = k.h()
    ident_b = sb("ident_b", [128, 128], BF16)
    onesm = sb("onesm", [128, 128], F32)
    ones1 = sb("ones1", [128, 128], F32)
    diagneg = sb("diagneg", [128, 128], F32)
    epsc = sb("epsc", [128, 1], F32)
    onec = sb("onec", [128, 1], F32)
    EB = sb("EB", [128, 2 * 16 * 128], BF16)
    h_c = k.h()

    k.dma('sp', ident_f[:], consts_d, wr=[h_c])
    k.op('dve', lambda e: e.tensor_copy(out=ident_b[:], in_=ident_f[:]), rd=[h_c], wr=[h_c])
    k.op('dve', lambda e: e.memset(onesm[:], 1.0 / 128), wr=[h_c])
    k.op('dve', lambda e: e.memset(ones1[:], 1.0), wr=[h_c])
    k.op('dve', lambda e: e.memset(diagneg[:], 0.0), wr=[h_c])
    k.op('dve', lambda e: e.memset(diagneg[0:64, 64:128], NEG), wr=[h_c])
    k.op('dve', lambda e: e.memset(epsc[:], EPS), wr=[h_c])
    k.op('dve', lambda e: e.memset(onec[:], 1.0), wr=[h_c])

    with nc.sbuf_tensor("oh_s0", [32, 2 * 128 * 128], F32) as oh_s, nc.sbuf_tensor("rb_s0", [32, 16], F32) as rb_s, \
            nc.psum_tensor("bps", [128, 512], F32) as bps:
        h_oh, h_bps = k.h(), k.h()
        k.dma('sp', oh_s[:], oh_d, wr=[h_oh])
        k.dma('sp', rb_s[:], rb_d, wr=[h_oh])
        EBv = EB[:].rearrange("s (k h t) -> s k h t", k=2, h=16)
        ohv = oh_s[:].rearrange("b (k s t) -> b k s t", k=2, s=128)
        for kind in range(2):
            for t0 in range(0, 128, 32):
                for tt in range(32):
                    k.op('pe', lambda e, kind=kind, t=t0 + tt, tt=tt: e.matmul(
                        bps[:, tt * 16:(tt + 1) * 16], lhsT=ohv[:, kind, :, t], rhs=rb_s[:], start=True, stop=True),
                        rd=[h_oh], wr=[h_bps])
                k.op('act', lambda e, kind=kind, t0=t0: e.activation(
                    out=EBv[:, kind, :, t0:t0 + 32].rearrange("s h t -> s t h"),
                    in_=bps[:].rearrange("s (t h) -> s t h", h=16), func=AF.Exp), rd=[h_bps], wr=[h_c])
        k.barrier()

    def rms_to_T(htile, gain_bc, dstT, col0, h_in, h_dst, bufs):
        junk, ss, sd, rs, hp, tp, h_j, h_hp, h_tp = bufs
        k.op('act', lambda e: e.activation(out=junk[:], in_=htile, func=AF.Square, accum_out=ss[:, 0:1]),
             rd=[h_in], wr=[h_j])
        k.op('act', lambda e: e.activation(out=sd[:, 0:1], in_=ss[:, 0:1], func=AF.Sqrt, scale=1.0 / D, bias=epsc[:, 0:1]),
             rd=[h_j, h_c], wr=[h_j])
        k.op('dve', lambda e: e.reciprocal(out=rs[:, 0:1], in_=sd[:, 0:1]), rd=[h_j], wr=[h_j])
        k.op('dve', lambda e: e.scalar_tensor_tensor(out=hp[:], in0=htile, scalar=rs[:, 0:1], in1=gain_bc,
                                                      op0=ALU.mult, op1=ALU.mult), rd=[h_in, h_j, h_c], wr=[h_hp])
        for g4 in range(4):
            for j in range(4):
                dc = g4 * 4 + j
                k.op('pe', lambda e, dc=dc, j=j: e.transpose(out=tp[:, j * 128:(j + 1) * 128],
                                                              in_=hp[:, dc * 128:(dc + 1) * 128], identity=ident_b[:]),
                     rd=[h_hp, h_c], wr=[h_tp])
            k.op('act', lambda e, g4=g4: e.activation(out=dstT[:, g4 * 4:(g4 + 1) * 4, col0:col0 + 128],
                                                       in_=tp[:].rearrange("p (j t) -> p j t", j=4), func=AF.Copy),
                 rd=[h_tp], wr=[h_dst])

    state = {'wi': 0, 'ci': 0}

    def cast(out_ap, in_ap, rd, wr):
        state['ci'] += 1
        if state['ci'] % 2 == 0:
            k.op('act', lambda e: e.activation(out=out_ap, in_=in_ap, func=AF.Copy), rd=rd, wr=wr)
        else:
            k.op('dve', lambda e: e.tensor_copy(out=out_ap, in_=in_ap), rd=rd, wr=wr)

    def load_w(src_ap, n, ST, WB, h_st, h_wb, view=None):
        i = state['wi'] % len(ST)
        jb = state['wi'] % len(WB)
        state['wi'] += 1
        k.dma('sp', ST[i][:, 0:n], src_ap, wr=[h_st[i]])
        cast(WB[jb][:, 0:n], ST[i][:, 0:n], [h_st[i]], [h_wb[jb]])
        return WB[jb], h_wb[jb]

    def load_big(src2d, n, dst, h_dst_, ST, h_st):
        for c0 in range(0, n, 2048):
            w = min(2048, n - c0)
            i = state['wi'] % len(ST)
            state['wi'] += 1
            k.dma('sp', ST[i][:, 0:w], src2d[:, c0:c0 + w], wr=[h_st[i]])
            cast(dst[:, c0:c0 + w], ST[i][:, 0:w], [h_st[i]], [h_dst_])

    def tail_block(layer, tb, hblk, h_hblk, yT, h_yT, CC, wo_src, tl):
        (ST, WB, h_st, h_wb, WO, h_wo, hgT, h_hgT, normbufs, pa, pb, h_pa, h_pb, pt_s, ptb, pT, h_pt, h_pT,
         sg, ge, h_sg, gbc) = tl
        for dblk in range(4):
            wo = WO[dblk % len(WO)]
            hwo = h_wo[dblk % len(WO)]
            load_big(wo_src[dblk], CC * 512, wo, hwo, ST, h_st)
            for tile in range(4):
                pp, hp_ = (pa, h_pa) if tile % 2 == 0 else (pb, h_pb)
                for cc in range(CC):
                    k.op('pe', lambda e, cc=cc, tile=tile, pp=pp, wo=wo: e.matmul(
                        pp[:], lhsT=yT[:, cc, tile * 128:(tile + 1) * 128], rhs=wo[:, cc * 512:(cc + 1) * 512],
                        start=(cc == 0), stop=(cc == CC - 1)), rd=[h_yT, hwo], wr=[hp_])
                k.op('dve', lambda e, tile=tile, dblk=dblk, pp=pp: e.tensor_tensor(
                    out=hblk[:, tile, dblk * 512:(dblk + 1) * 512], in0=pp[:], in1=hblk[:, tile, dblk * 512:(dblk + 1) * 512],
                    op=ALU.add), rd=[hp_, h_hblk], wr=[h_hblk])
        for tile in range(4):
            rms_to_T(hblk[:, tile, :], gbc[:], hgT, tile * 128, h_hblk, h_hgT, normbufs)
            k.dma('sp', pt_s[:], p_d[layer, tb * 512 + tile * 128: tb * 512 + (tile + 1) * 128, :], wr=[h_pt])
            k.op('pool', lambda e: e.tensor_copy(out=ptb[:], in_=pt_s[:]), rd=[h_pt], wr=[h_pt])
            tp, h_tp = normbufs[5], normbufs[8]
            for c in range(2):
                k.op('pe', lambda e, c=c: e.transpose(out=tp[:, c * 128:(c + 1) * 128], in_=ptb[:, c * 128:(c + 1) * 128],
                                                      identity=ident_b[:]), rd=[h_pt, h_c], wr=[h_tp])
            k.op('act', lambda e, tile=tile: e.activation(out=pT[:, :, tile * 128:(tile + 1) * 128],
                                                          in_=tp[:, 0:256].rearrange("p (j t) -> p j t", j=2), func=AF.Copy),
                 rd=[h_tp], wr=[h_pT])
        for dblk in range(4):
            load_big(pwg_d[layer, dblk], 8192, WB[0], h_wb[0], ST, h_st)
            load_big(pew_d[layer, dblk], 1024, WB[1], h_wb[1], ST, h_st)
            for tile in range(4):
                for dc in range(16):
                    k.op('pe', lambda e, dc=dc, tile=tile: e.matmul(
                        pa[:], lhsT=hgT[:, dc, tile * 128:(tile + 1) * 128], rhs=WB[0][:, dc * 512:(dc + 1) * 512],
                        start=(dc == 0), stop=(dc == 15)), rd=[h_hgT, h_wb[0]], wr=[h_pa])
                for c in range(2):
                    k.op('pe', lambda e, c=c, tile=tile: e.matmul(
                        pb[:], lhsT=pT[:, c, tile * 128:(tile + 1) * 128], rhs=WB[1][:, c * 512:(c + 1) * 512],
                        start=(c == 0), stop=(c == 1)), rd=[h_pT, h_wb[1]], wr=[h_pb])
                k.op('act', lambda e: e.activation(out=sg[:], in_=pa[:], func=AF.Sigmoid), rd=[h_pa], wr=[h_sg])
                k.op('dve', lambda e: e.tensor_tensor(out=ge[:], in0=pb[:], in1=sg[:], op=ALU.mult), rd=[h_pb, h_sg], wr=[h_sg])
                k.op('dve', lambda e, tile=tile, dblk=dblk: e.tensor_tensor(
                    out=hblk[:, tile, dblk * 512:(dblk + 1) * 512], in0=ge[:], in1=hblk[:, tile, dblk * 512:(dblk + 1) * 512],
                    op=ALU.add), rd=[h_sg, h_hblk], wr=[h_hblk])


    def alloc_tail(ls, CC, NST):
        sbl = lambda name, shape, dt: ls.enter_context(nc.sbuf_tensor(un(name), list(shape), dt))
        psl = lambda name, shape, dt: ls.enter_context(nc.psum_tensor(un(name), list(shape), dt))
        ST = [sbl(f"ST{i}", [128, 2048], F32) for i in range(NST)]
        WB = [sbl("WB0", [128, 8192], BF16), sbl("WB1", [128, 2048], BF16)]
        WO = [sbl("WO0", [128, CC * 512], BF16)]
        hgT = sbl("hgT", [128, 16, 512], BF16)
        ss = sbl("ss", [128, 1], F32); sd = sbl("sd", [128, 1], F32); rs = sbl("rs", [128, 1], F32)
        hp = sbl("hp", [128, 2048], BF16)
        tp = psl("tp", [128, 512], BF16)
        h_hp_ = k.h()
        normbufs = (hp, ss, sd, rs, hp, tp, h_hp_, h_hp_, k.h())
        pa = psl("pa", [128, 512], F32); pb = psl("pb", [128, 512], F32)
        pt_s = sbl("pt_s", [128, 256], F32); ptb = sbl("ptb", [128, 256], BF16)
        pT = sbl("pT", [128, 2, 512], BF16)
        sg = sbl("sg", [128, 512], F32); ge = sg
        gbc = sbl("gbc", [128, 2048], F32)
        return (ST, WB, k.hs(NST), k.hs(2), WO, k.hs(1), hgT, k.h(), normbufs, pa, pb, k.h(), k.h(), pt_s, ptb, pT,
                k.h(), k.h(), sg, ge, k.h(), gbc)

    def odd_layer(layer, j, h_src, h_dst):
        with ExitStack() as ls:
            sbl = lambda name, shape, dt: ls.enter_context(nc.sbuf_tensor(un(name), list(shape), dt))
            psl = lambda name, shape, dt: ls.enter_context(nc.psum_tensor(un(name), list(shape), dt))
            tl = alloc_tail(ls, 16, 2)
            ST, WB, h_st, h_wb = tl[0], tl[1], tl[2], tl[3]
            normbufs = tl[8]
            pa, pb, h_pa, h_pb = tl[9], tl[10], tl[11], tl[12]
            gbc = tl[21]
            hblk = sbl("hblk", [128, 4, 2048], F32); h_hblk = k.h()
            hnT = tl[6]; h_hnT = tl[7]
            usg = sbl("usg", [128, 16, 512], BF16); h_usg = k.h()
            tmpb = sbl("tmpb", [128, 512], BF16); h_tmpb = k.h()
            WC = [sbl(f"WC{i}", [128, 2048], BF16) for i in range(2)]; h_wc = k.hs(2)
            gv = sbl("gv", [128, 4, 2048], F32); h_gv = k.h()
            z = normbufs[4]; h_z = normbufs[7]
            yT = usg; h_yT = h_usg
            nbc = sbl("nbc", [128, 2048], F32)
            ovec = sbl("ovec", [128, 32], F32)
            wsT_f = gv[:, 0, :]
            wsT_b = sbl("wsT_b", [128, 16, 128], BF16)
            bsbc = gv[:, 1, :]
            Cg = sbl("Cg", [128, 16, 128], F32)
            st6 = sbl("st6", [128, 4, 6], F32); mv = sbl("mv", [128, 2], F32); h_st6 = k.h()
            lsd = sbl("lsd", [128, 1], F32); lrs = sbl("lrs", [128, 1], F32); nmr = sbl("nmr", [128, 1], F32)
            mx = sbl("mx", [128, 128], F32); h_mx = k.h()
            mps = psl("mps", [128, 512], F32); h_mps = k.h()
            h_lc = k.h()
            k.dma('sp', nbc[:], ng_d[layer, :].partition_broadcast(128), wr=[h_c])
            k.dma('sp', gbc[:], pgn_d[layer, :].partition_broadcast(128), wr=[h_c])
            k.dma('sp', ovec[:], ovec_d[j], wr=[h_lc])
            k.dma('sp', wsT_f, wst_d[j], wr=[h_lc])
            for g in range(16):
                k.dma('sp', bsbc[:, g * 128:(g + 1) * 128], bs_d[j, g, :].partition_broadcast(128), wr=[h_lc])
            wv = wsT_f.rearrange("s (g t) -> s g t", g=16)
            k.op('dve', lambda e: e.memset(wv[64:128, :, 0:64], 0.0), rd=[h_lc], wr=[h_lc])
            k.op('dve', lambda e: e.tensor_copy(out=wsT_b[:], in_=wv), rd=[h_lc], wr=[h_lc])
            for g in range(16):
                k.op('pe', lambda e, g=g: e.matmul(mps[:, 0:128], lhsT=ones1[:], rhs=wv[:, g, :], start=True, stop=True),
                     rd=[h_lc, h_c], wr=[h_mps])
                k.op('dve', lambda e, g=g: e.scalar_tensor_tensor(out=Cg[:, g, :], in0=mps[:, 0:128], scalar=ovec[:, 16 + g:17 + g],
                                                                   in1=bsbc[:, g * 128:(g + 1) * 128], op0=ALU.mult, op1=ALU.add),
                     rd=[h_mps, h_lc], wr=[h_lc])
            k.barrier()
            for tb in range(8):
                k.maybe_barrier()
                t0 = tb * 512
                k.dma('sp', hblk[:], h_src[t0:t0 + 512, :].rearrange("(i p) d -> p i d", p=128), wr=[h_hblk])
                for tile in range(4):
                    rms_to_T(hblk[:, tile, :], nbc[:], hnT, tile * 128, h_hblk, h_hnT, normbufs)
                for ch in range(32):
                    wb, hwb = load_w(owf_d[j, ch], 2048, ST, WC, h_st, h_wc)
                    pp, hp_ = (pa, h_pa) if ch % 2 == 0 else (pb, h_pb)
                    for dc in range(16):
                        k.op('pe', lambda e, dc=dc, wb=wb, pp=pp: e.matmul(pp[:], lhsT=wb[:, dc * 128:(dc + 1) * 128], rhs=hnT[:, dc, :],
                                                                           start=(dc == 0), stop=(dc == 15)), rd=[hwb, h_hnT], wr=[hp_])
                    if ch < 16:
                        k.op('act', lambda e, ch=ch, pp=pp: e.activation(out=usg[:, ch, :], in_=pp[:], func=AF.Gelu_apprx_tanh),
                             rd=[hp_], wr=[h_usg])
                    else:
                        k.op('act', lambda e, pp=pp: e.activation(out=tmpb[:], in_=pp[:], func=AF.Silu), rd=[hp_], wr=[h_tmpb])
                        k.op('dve', lambda e, ch=ch: e.tensor_tensor(out=usg[:, ch - 16, :], in0=usg[:, ch - 16, :], in1=tmpb[:], op=ALU.mult),
                             rd=[h_tmpb, h_usg], wr=[h_usg])
                for vb in range(4):
                    load_big(owv_d[j, vb], 8192, WB[0], h_wb[0], ST, h_st)
                    for tile in range(4):
                        pp, hp_ = (pa, h_pa) if tile % 2 == 0 else (pb, h_pb)
                        for dc in range(16):
                            k.op('pe', lambda e, dc=dc, tile=tile, pp=pp: e.matmul(
                                pp[:], lhsT=hnT[:, dc, tile * 128:(tile + 1) * 128], rhs=WB[0][:, dc * 512:(dc + 1) * 512],
                                start=(dc == 0), stop=(dc == 15)), rd=[h_wb[0], h_hnT], wr=[hp_])
                        k.op('act', lambda e, tile=tile, vb=vb, pp=pp: e.activation(out=gv[:, tile, vb * 512:(vb + 1) * 512], in_=pp[:],
                                                                                   func=AF.Gelu_apprx_tanh), rd=[hp_], wr=[h_gv])
                for tile in range(4):
                    for q4 in range(4):
                        k.op('dve', lambda e, q4=q4, tile=tile: e.bn_stats(out=st6[:, q4, :], in_=gv[:, tile, q4 * 512:(q4 + 1) * 512]),
                             rd=[h_gv], wr=[h_st6])
                    k.op('dve', lambda e: e.bn_aggr(out=mv[:], in_=st6[:].rearrange("p a b -> p (a b)")), rd=[h_st6], wr=[h_st6])
                    k.op('act', lambda e: e.activation(out=lsd[:], in_=mv[:, 1:2], func=AF.Sqrt, bias=epsc[:, 0:1]), rd=[h_st6, h_c], wr=[h_st6])
                    k.op('dve', lambda e: e.reciprocal(out=lrs[:], in_=lsd[:]), rd=[h_st6], wr=[h_st6])
                    k.op('dve', lambda e: e.scalar_tensor_tensor(out=nmr[:], in0=mv[:, 0:1], scalar=-1.0, in1=lrs[:], op0=ALU.mult, op1=ALU.mult),
                         rd=[h_st6], wr=[h_st6])
                    k.op('dve', lambda e, tile=tile: e.tensor_scalar(out=z[:], in0=gv[:, tile, :], scalar1=lrs[:, 0:1], scalar2=nmr[:, 0:1],
                                                                      op0=ALU.mult, op1=ALU.add), rd=[h_gv, h_st6], wr=[h_z])
                    for g in range(16):
                        k.op('pe', lambda e, g=g: e.matmul(mps[:, (g % 4) * 128:(g % 4 + 1) * 128], lhsT=z[:, g * 128:(g + 1) * 128],
                                                            rhs=wsT_b[:, g, :], start=True, stop=True), rd=[h_z, h_lc], wr=[h_mps])
                        k.op('dve', lambda e, g=g: e.scalar_tensor_tensor(out=mx[:], in0=mps[:, (g % 4) * 128:(g % 4 + 1) * 128],
                                                                           scalar=ovec[:, g:g + 1], in1=Cg[:, g, :], op0=ALU.mult, op1=ALU.add),
                             rd=[h_mps, h_lc], wr=[h_mx])
                        k.op('dve', lambda e, g=g, tile=tile: e.tensor_tensor(out=yT[:, g, tile * 128:(tile + 1) * 128], in0=mx[:],
                                                                              in1=usg[:, g, tile * 128:(tile + 1) * 128], op=ALU.mult),
                             rd=[h_mx, h_usg], wr=[h_yT])
                tail_block(layer, tb, hblk, h_hblk, yT, h_yT, 16, owo_d[j], tl)
                k.dma('sp', h_dst[t0:t0 + 512, :].rearrange("(i p) d -> p i d", p=128), hblk[:], rd=[h_hblk])
            k.barrier()


    def even_layer(layer, j, h_src, h_dst):
        CH = ([('xa', i) for i in range(16)] + [('ga', i) for i in range(16)] + [('q', i) for i in range(16)] +
              [('k', i) for i in range(4)] + [('gb', i) for i in range(16)] + [('iq', i) for i in range(8)] + [('ik', 0)])
        with ExitStack() as ls:
          if 'E1' in PH:
              sbl = lambda name, shape, dt: ls.enter_context(nc.sbuf_tensor(un(name), list(shape), dt))
              psl = lambda name, shape, dt: ls.enter_context(nc.psum_tensor(un(name), list(shape), dt))
              ST = [sbl(f"ST{i}", [128, 2048], F32) for i in range(4)]; h_st = k.hs(4)
              WB = [sbl("WB0", [128, 8192], BF16), sbl("WB1", [128, 2048], BF16)]; h_wb = k.hs(2)
              WC = [sbl(f"WC{i}", [128, 2048], BF16) for i in range(3)]; h_wc = k.hs(3)
              hnT = sbl("hnT", [128, 16, 1024], BF16); h_hnT = k.h()
              ht = [sbl(f"ht{i}", [128, 2048], F32) for i in range(2)]; h_ht = k.hs(2)
              ss = sbl("ss", [128, 1], F32); sd = sbl("sd", [128, 1], F32); rs = sbl("rs", [128, 1], F32)
              hp = sbl("hp", [128, 2048], BF16)
              tp = psl("tp", [128, 512], BF16)
              h_hp_ = k.h()
              normbufs = (hp, ss, sd, rs, hp, tp, h_hp_, h_hp_, k.h())
              pa = psl("pa", [128, 512], F32); pb = psl("pb", [128, 512], F32); h_pa, h_pb = k.h(), k.h()
              pm = psl("pm", [128, 512], F32); h_pm = k.h()
              nbc = sbl("nbc", [128, 2048], F32)
              qkn = sbl("qkn", [128, 2], F32); qg = sbl("qg", [128, 2], F32)
              of = [sbl(f"of{i}", [128, 512], F32) for i in range(2)]
              ob = [sbl(f"ob{i}", [128, 512], BF16) for i in range(2)]; h_o = k.hs(2)
              sq = sbl("sq", [128, 512], F32); rsd = sbl("rsd", [128, 512], F32); h_sq = k.h()
              vt = sbl("vt", [128, 4, 130], BF16); h_vt = k.h()
              iwt = sbl("iwt", [128, 16], F32); h_iwt = k.h()
              h_lc = k.h()
              k.dma('sp', nbc[:], ng_d[layer, :].partition_broadcast(128), wr=[h_c])
              k.dma('sp', qkn[:], qkn_d[j], wr=[h_lc])
              k.op('dve', lambda e: e.tensor_scalar(out=qg[:, 0:1], in0=qkn[:, 0:1], scalar1=float(128 ** -0.5), scalar2=None, op0=ALU.mult),
                   rd=[h_lc], wr=[h_lc])
              k.op('dve', lambda e: e.tensor_copy(out=qg[:, 1:2], in_=qkn[:, 1:2]), rd=[h_lc], wr=[h_lc])
              k.op('dve', lambda e: e.memset(vt[:], 1.0), wr=[h_vt])
              dst = {'xa': xa_d, 'ga': sga_d, 'gb': sgb_d, 'q': q_d, 'k': kk_d, 'iq': iq_d, 'ik': ik_d}
              oi = 0
              for blk in range(4):
                  k.maybe_barrier()
                  T0 = blk * 1024
                  for tile in range(8):
                      i = tile % 2
                      k.dma('sp', ht[i][:], h_src[T0 + tile * 128:T0 + (tile + 1) * 128, :], wr=[h_ht[i]])
                      rms_to_T(ht[i][:], nbc[:], hnT, tile * 128, h_ht[i], h_hnT, normbufs)
                  for ci, (kind, idx) in enumerate(CH):
                      wb, hwb = load_w(ewf_d[j, ci], 2048, ST, WC, h_st, h_wc)
                      for half in range(2):
                          pp, hp_ = (pa, h_pa) if half == 0 else (pb, h_pb)
                          for dc in range(16):
                              k.op('pe', lambda e, dc=dc, wb=wb, pp=pp, half=half: e.matmul(
                                  pp[:], lhsT=wb[:, dc * 128:(dc + 1) * 128], rhs=hnT[:, dc, half * 512:(half + 1) * 512],
                                  start=(dc == 0), stop=(dc == 15)), rd=[hwb, h_hnT], wr=[hp_])
                          o = oi % 2; oi += 1
                          rows = slice(idx * 128, (idx + 1) * 128)
                          cols = slice(T0 + half * 512, T0 + (half + 1) * 512)
                          if kind == 'xa':
                              k.op('act', lambda e, pp=pp, o=o: e.activation(out=of[o][:], in_=pp[:], func=AF.Copy), rd=[hp_], wr=[h_o[o]])
                              k.dma('act', xa_d[rows, cols], of[o][:], rd=[h_o[o]])
                          elif kind in ('ga', 'gb'):
                              k.op('act', lambda e, pp=pp, o=o: e.activation(out=ob[o][:], in_=pp[:], func=AF.Silu), rd=[hp_], wr=[h_o[o]])
                              k.dma('act', dst[kind][rows, cols], ob[o][:], rd=[h_o[o]])
                          elif kind in ('iq', 'ik'):
                              k.op('act', lambda e, pp=pp, o=o: e.activation(out=ob[o][:], in_=pp[:], func=AF.Copy), rd=[hp_], wr=[h_o[o]])
                              k.dma('act', dst[kind][rows, cols], ob[o][:], rd=[h_o[o]])
                          else:
                              gi = 0 if kind == 'q' else 1
                              k.op('act', lambda e, pp=pp: e.activation(out=sq[:], in_=pp[:], func=AF.Square), rd=[hp_], wr=[h_sq])
                              k.op('pe', lambda e: e.matmul(pm[:], lhsT=onesm[:], rhs=sq[:], start=True, stop=True), rd=[h_sq, h_c], wr=[h_pm])
                              k.op('act', lambda e: e.activation(out=rsd[:], in_=pm[:], func=AF.Sqrt, bias=epsc[:, 0:1]), rd=[h_pm, h_c], wr=[h_sq])
                              k.op('dve', lambda e: e.reciprocal(out=rsd[:], in_=rsd[:]), rd=[h_sq], wr=[h_sq])
                              k.op('dve', lambda e, pp=pp, o=o, gi=gi: e.scalar_tensor_tensor(
                                  out=ob[o][:], in0=pp[:], scalar=qg[:, gi:gi + 1], in1=rsd[:], op0=ALU.mult, op1=ALU.mult),
                                  rd=[hp_, h_sq, h_lc], wr=[h_o[o]])
                              k.dma('sp', dst[kind][rows, cols], ob[o][:], rd=[h_o[o]])
                  load_big(ewv_d[j], 8192, WB[0], h_wb[0], ST, h_st)
                  load_big(ewiw_d[j], 256, WB[1], h_wb[1], ST, h_st)
                  for tile in range(8):
                      for dc in range(16):
                          k.op('pe', lambda e, dc=dc, tile=tile: e.matmul(pa[:], lhsT=hnT[:, dc, tile * 128:(tile + 1) * 128],
                                                                        rhs=WB[0][:, dc * 512:(dc + 1) * 512], start=(dc == 0), stop=(dc == 15)),
                               rd=[h_wb[0], h_hnT], wr=[h_pa])
                      k.op('act', lambda e: e.activation(out=vt[:, :, 0:128], in_=pa[:].rearrange("p (g d) -> p g d", g=4), func=AF.Copy),
                           rd=[h_pa], wr=[h_vt])
                      k.dma('act', v_d[T0 + tile * 128:T0 + (tile + 1) * 128, :], vt[:].rearrange("p g d -> p (g d)"), rd=[h_vt])
                      for dc in range(16):
                          k.op('pe', lambda e, dc=dc, tile=tile: e.matmul(pb[:, 0:16], lhsT=hnT[:, dc, tile * 128:(tile + 1) * 128],
                                                                        rhs=WB[1][:, dc * 16:(dc + 1) * 16], start=(dc == 0), stop=(dc == 15)),
                               rd=[h_wb[1], h_hnT], wr=[h_pb])
                      k.op('act', lambda e: e.activation(out=iwt[:], in_=pb[:, 0:16], func=AF.Copy), rd=[h_pb], wr=[h_iwt])
                      k.dma('act', iw_d[T0 + tile * 128:T0 + (tile + 1) * 128, :], iwt[:], rd=[h_iwt])
              k.barrier()
        with ExitStack() as ls:
          if 'E2' in PH:
              sbl = lambda name, shape, dt: ls.enter_context(nc.sbuf_tensor(un(name), list(shape), dt))
              psl = lambda name, shape, dt: ls.enter_context(nc.psum_tensor(un(name), list(shape), dt))
              L = 1024
              evec = sbl("evec", [128, 8, 16], F32)
              kap = sbl("kap", [128, 16], F32); kap2 = sbl("kap2", [128, 16], F32); tmpk = sbl("tmpk", [128, 16], F32)
              wr_f = sbl("wr_f", [128, 2048], F32); wi_f = sbl("wi_f", [128, 2048], F32)
              wr_b = sbl("wr_b", [128, 16, 128], BF16); wi_b = sbl("wi_b", [128, 16, 128], BF16)
              h_lc = k.h()
              xin = [sbl(f"xin{i}", [128, 3 + L], F32) for i in range(2)]; h_xin = k.hs(2)
              xc = sbl("xc", [128, L], F32); h_xc = k.h()
              xcb = sbl("xcb", [128, L], BF16); h_xcb = k.h()
              rg = sbl("rg", [128, L], F32); ig = sbl("ig", [128, L], F32); h_rg, h_ig = k.h(), k.h()
              aa = sbl("aa", [128, L], F32); a2 = sbl("a2", [128, L], F32); h_aa, h_a2 = k.h(), k.h()
              bx = sbl("bx", [128, L], F32); h_bx = k.h()
              hsb = [sbl(f"hs{i}", [128, L], F32) for i in range(2)]; h_hs = k.hs(2)
              sga = [sbl(f"sga{i}", [128, L], BF16) for i in range(2)]; h_sga = k.hs(2)
              yb = [sbl(f"yb{i}", [128, L], BF16) for i in range(2)]; h_yb = k.hs(2)
              pa = psl("pa", [128, 512], F32); pb = psl("pb", [128, 512], F32); h_pa, h_pb = k.h(), k.h()
              k.dma('sp', evec[:].rearrange("p a g -> p (a g)"), evec_d[j], wr=[h_lc])
              k.dma('sp', wr_f[:], lwr_d[j], wr=[h_lc])
              k.dma('sp', wi_f[:], lwi_d[j], wr=[h_lc])
              k.op('dve', lambda e: e.tensor_copy(out=wr_b[:].rearrange("p g j -> p (g j)"), in_=wr_f[:]), rd=[h_lc], wr=[h_lc])
              k.op('dve', lambda e: e.tensor_copy(out=wi_b[:].rearrange("p g j -> p (g j)"), in_=wi_f[:]), rd=[h_lc], wr=[h_lc])
              k.op('act', lambda e: e.activation(out=tmpk[:], in_=evec[:, 7, :], func=AF.Exp, scale=-1.0), rd=[h_lc], wr=[h_lc])
              k.op('act', lambda e: e.activation(out=tmpk[:], in_=tmpk[:], func=AF.Ln, bias=onec[:, 0:1]), rd=[h_lc, h_c], wr=[h_lc])
              k.op('dve', lambda e: e.tensor_scalar(out=kap[:], in0=tmpk[:], scalar1=-8.0, scalar2=None, op0=ALU.mult), rd=[h_lc], wr=[h_lc])
              k.op('dve', lambda e: e.tensor_scalar(out=kap2[:], in0=tmpk[:], scalar1=-16.0, scalar2=None, op0=ALU.mult), rd=[h_lc], wr=[h_lc])
              it = 0
              for g in range(16):
                  k.maybe_barrier()
                  rows = slice(g * 128, (g + 1) * 128)
                  for pc in range(T // L):
                      i = it % 2; it += 1
                      t0 = pc * L
                      if pc == 0:
                          k.op('dve', lambda e, i=i: e.memset(xin[i][:, 0:3], 0.0), wr=[h_xin[i]])
                          k.dma('sp', xin[i][:, 3:3 + L], xa_d[rows, t0:t0 + L], wr=[h_xin[i]])
                      else:
                          k.dma('sp', xin[i][:, 0:3 + L], xa_d[rows, t0 - 3:t0 + L], wr=[h_xin[i]])
                      k.dma('sp', sga[i][:], sga_d[rows, t0:t0 + L], wr=[h_sga[i]])
                      k.op('dve', lambda e, i=i, g=g: e.tensor_scalar(out=xc[:], in0=xin[i][:, 3:3 + L], scalar1=evec[:, 3, g:g + 1],
                                                                      scalar2=evec[:, 4, g:g + 1], op0=ALU.mult, op1=ALU.add),
                           rd=[h_xin[i], h_lc], wr=[h_xc])
                      for tap in range(3):
                          k.op('dve', lambda e, i=i, g=g, tap=tap: e.scalar_tensor_tensor(
                              out=xc[:], in0=xin[i][:, tap:tap + L], scalar=evec[:, tap, g:g + 1], in1=xc[:], op0=ALU.mult, op1=ALU.add),
                              rd=[h_xin[i], h_lc, h_xc], wr=[h_xc])
                      k.op('act', lambda e: e.activation(out=xcb[:], in_=xc[:], func=AF.Copy), rd=[h_xc], wr=[h_xcb])
                      for half in range(L // 512):
                          sl = slice(half * 512, (half + 1) * 512)
                          k.op('pe', lambda e, g=g, sl=sl: e.matmul(pa[:], lhsT=wr_b[:, g, :], rhs=xcb[:, sl], start=True, stop=True),
                               rd=[h_xcb, h_lc], wr=[h_pa])
                          k.op('pe', lambda e, g=g, sl=sl: e.matmul(pb[:], lhsT=wi_b[:, g, :], rhs=xcb[:, sl], start=True, stop=True),
                               rd=[h_xcb, h_lc], wr=[h_pb])
                          k.op('act', lambda e, g=g, sl=sl: e.activation(out=rg[:, sl], in_=pa[:], func=AF.Sigmoid, bias=evec[:, 5, g:g + 1]),
                               rd=[h_pa, h_lc], wr=[h_rg])
                          k.op('act', lambda e, g=g, sl=sl: e.activation(out=ig[:, sl], in_=pb[:], func=AF.Sigmoid, bias=evec[:, 6, g:g + 1]),
                               rd=[h_pb, h_lc], wr=[h_ig])
                      k.op('act', lambda e, g=g: e.activation(out=aa[:], in_=rg[:], func=AF.Exp, scale=kap[:, g:g + 1]), rd=[h_rg, h_lc], wr=[h_aa])
                      k.op('act', lambda e, g=g: e.activation(out=a2[:], in_=rg[:], func=AF.Exp, scale=kap2[:, g:g + 1]), rd=[h_rg, h_lc], wr=[h_a2])
                      k.op('act', lambda e: e.activation(out=a2[:], in_=a2[:], func=AF.Sqrt, scale=-1.0, bias=onec[:, 0:1]), rd=[h_a2, h_c], wr=[h_a2])
                      k.op('pool', lambda e: e.tensor_tensor(out=bx[:], in0=a2[:], in1=ig[:], op=ALU.mult), rd=[h_a2, h_ig], wr=[h_bx])
                      k.op('pool', lambda e: e.tensor_tensor(out=bx[:], in0=bx[:], in1=xc[:], op=ALU.mult), rd=[h_bx, h_xc], wr=[h_bx])
                      if pc == 0:
                          k.op('dve', lambda e, i=i: e.tensor_tensor_scan(out=hsb[i][:], data0=aa[:], data1=bx[:], initial=0.0,
                                                                          op0=ALU.mult, op1=ALU.add), rd=[h_aa, h_bx], wr=[h_hs[i]])
                      else:
                          k.op('dve', lambda e, i=i: e.tensor_tensor_scan(out=hsb[i][:], data0=aa[:], data1=bx[:], initial=hsb[1 - i][:, L - 1:L],
                                                                          op0=ALU.mult, op1=ALU.add), rd=[h_aa, h_bx, h_hs[1 - i]], wr=[h_hs[i]])
                      k.op('pool', lambda e, i=i: e.tensor_tensor(out=yb[i][:], in0=hsb[i][:], in1=sga[i][:], op=ALU.mult),
                           rd=[h_hs[i], h_sga[i]], wr=[h_yb[i]])
                      k.dma('sp', yT_d[rows, t0:t0 + L], yb[i][:], rd=[h_yb[i]])
              k.barrier()
        with ExitStack() as ls:
          if 'E3' in PH:
              sbl = lambda name, shape, dt: ls.enter_context(nc.sbuf_tensor(un(name), list(shape), dt))
              psl = lambda name, shape, dt: ls.enter_context(nc.psum_tensor(un(name), list(shape), dt))
              kT = sbl("kT", [128, 4, T], BF16); vA = sbl("vA", [128, 32, 4 * 130], BF16); ik2 = sbl("ik2", [128, T], BF16)
              h_kv = k.h()
              for g in range(4):
                  k.dma('sp', kT[:, g, :], kk_d[g * 128:(g + 1) * 128, :], wr=[h_kv])
              for c in range(4):
                  k.dma('sp', vA[:, c * 8:(c + 1) * 8, :], v_d[c * 1024:(c + 1) * 1024, :].rearrange("(i p) d -> p i d", p=128), wr=[h_kv])
              k.dma('sp', ik2[:], ik_d, wr=[h_kv])
              qt = [sbl(f"qt{i}", [128, 16, 128], BF16) for i in range(2)]
              iqt = [sbl(f"iqt{i}", [128, 8, 128], BF16) for i in range(2)]
              iwt = [sbl(f"iwt{i}", [128, 16], F32) for i in range(2)]
              sgbt = [sbl(f"sgbt{i}", [128, 16, 128], BF16) for i in range(2)]
              h_ld = k.hs(2)
              dg = sbl("dg", [128, 16, 128], BF16); h_dg = k.h()
              Rb = [sbl(f"Rb{i}", [128, 512], BF16) for i in range(2)]; h_R = k.hs(2)
              acc = sbl("acc", [128, T], F32); h_acc = k.h()
              cj = sbl("cj", [128, T], BF16); h_cj = k.h()
              am = sbl("am", [128, 1], F32); lo = sbl("lo", [128, 1], F32); w0 = sbl("w0", [128, 1], F32)
              mid = sbl("mid", [128, 1], F32); cnt = sbl("cnt", [128, 1], F32); t1 = sbl("t1", [128, 1], F32); h_bs = k.h()
              mask = sbl("mask", [128, T], BF16); h_mask = k.h()
              mT4 = sbl("mT4", [128, 32, 4, 128], BF16); h_mT = k.h()
              Mn = sbl("Mn", [128, 2, 16, 128], BF16); h_Mn = k.h()
              Eb = [sbl(f"Eb{i}", [128, 512], BF16) for i in range(3)]; h_E = k.hs(3)
              Pb = [sbl(f"Pb{i}", [128, 4, 128], BF16) for i in range(3)]; h_P = k.hs(3)
              ri = sbl("ri", [128, 4], F32); h_ri = k.h()
              ot = sbl("ot", [128, 16, 128], BF16); h_ot = k.h()
              yt = [sbl(f"yt{i}", [128, 16, 128], BF16) for i in range(2)]; h_yt = k.hs(2)
              pA = psl("pA", [128, 512], F32); pB = psl("pB", [128, 512], F32); pC = psl("pC", [128, 512], F32)
              h_pA, h_pB, h_pC = k.h(), k.h(), k.h()
              mtp = psl("mtp", [128, 512], BF16); h_mtp = k.h()
              oacc = [psl(f"oacc{i}", [128, 512], F32) for i in range(4)]; h_oa = k.hs(4)
              EBv = EB[:].rearrange("s (k h t) -> s k h t", k=2, h=16)
              zi = 0
              for n in range(NT):
                  k.maybe_barrier()
                  b = n % 2
                  S = 128 * (n + 1)
                  c0 = n * 128
                  k.dma('sp', qt[b][:], q_d[:, c0:c0 + 128].rearrange("(h p) t -> p h t", p=128), wr=[h_ld[b]])
                  k.dma('sp', iqt[b][:], iq_d[:, c0:c0 + 128].rearrange("(h p) t -> p h t", p=128), wr=[h_ld[b]])
                  k.dma('sp', iwt[b][:], iw_d[c0:c0 + 128, :], wr=[h_ld[b]])
                  k.dma('sp', sgbt[b][:], sgb_d[:, c0:c0 + 128].rearrange("(h p) t -> p h t", p=128), wr=[h_ld[b]])
                  for h in range(16):
                      k.op('pool', lambda e, h=h, b=b: e.tensor_scalar(out=dg[:, h, :], in0=ident_f[:], scalar1=iwt[b][:, h:h + 1], scalar2=0.0,
                                                                       op0=ALU.mult, op1=ALU.add), rd=[h_ld[b], h_c], wr=[h_dg])
                  nkb = (S + 511) // 512
                  steps = [(kb, h) for kb in range(nkb) for h in range(16)]

                  def idxA(jj):
                      kb, h = steps[jj]
                      w = min(512, S - kb * 512)
                      c = h // 2; p0 = 64 * (h % 2)
                      zp, hz = (pA, h_pA) if jj % 2 == 0 else (pB, h_pB)
                      R, hR = Rb[jj % 2], h_R[jj % 2]
                      k.op('pe', lambda e: e.matmul(
                          zp[:, 0:w], lhsT=iqt[b][p0:p0 + 64, c, :], rhs=ik2[p0:p0 + 64, kb * 512:kb * 512 + w], start=True, stop=True),
                          rd=[h_ld[b], h_kv], wr=[hz])
                      k.op('act', lambda e: e.activation(out=R[:, 0:w], in_=zp[:, 0:w], func=AF.Relu), rd=[hz], wr=[hR])

                  def idxB(jj):
                      kb, h = steps[jj]
                      w = min(512, S - kb * 512)
                      R, hR = Rb[jj % 2], h_R[jj % 2]
                      k.op('pe', lambda e: e.matmul(pC[:, 0:w], lhsT=dg[:, h, :], rhs=R[:, 0:w], start=(h == 0), stop=(h == 15)),
                           rd=[h_dg, hR], wr=[h_pC])
                      if h == 15:
                          k.op('dve', lambda e: e.tensor_copy(out=acc[:, kb * 512:kb * 512 + w], in_=pC[:, 0:w]), rd=[h_pC], wr=[h_acc])

                  idxA(0)
                  for jj in range(len(steps)):
                      if jj + 1 < len(steps):
                          idxA(jj + 1)
                      idxB(jj)
                  k.op('dve', lambda e, S=S: e.tensor_reduce(out=am[:], in_=acc[:, 0:S], axis=AX.X, op=ALU.max, apply_absolute_value=True),
                       rd=[h_acc], wr=[h_bs])
                  k.op('dve', lambda e: e.tensor_scalar(out=lo[:], in0=am[:], scalar1=-1.001, scalar2=-1e-20, op0=ALU.mult, op1=ALU.add), rd=[h_bs], wr=[h_bs])
                  k.op('dve', lambda e: e.tensor_scalar(out=w0[:], in0=am[:], scalar1=2.002, scalar2=2e-20, op0=ALU.mult, op1=ALU.add), rd=[h_bs], wr=[h_bs])
                  k.op('dve', lambda e, S=S: e.tensor_tensor(out=acc[:, S - 128:S], in0=acc[:, S - 128:S], in1=diagneg[:], op=ALU.add),
                       rd=[h_acc, h_c], wr=[h_acc])
                  if S > 256:
                      for itn in range(1, NIT + 1):
                          f = 2.0 ** (-itn)
                          k.op('dve', lambda e, f=f: e.scalar_tensor_tensor(out=mid[:], in0=w0[:], scalar=f, in1=lo[:], op0=ALU.mult, op1=ALU.add),
                               rd=[h_bs], wr=[h_bs])
                          k.op('dve', lambda e, S=S: e.tensor_scalar(out=cj[:, 0:S], in0=acc[:, 0:S], scalar1=mid[:, 0:1], scalar2=None,
                                                                     op0=ALU.is_ge, op1=ALU.add, accum_out=cnt[:, 0:1]), rd=[h_acc, h_bs], wr=[h_cj, h_bs])
                          k.op('dve', lambda e, f=f: e.tensor_scalar(out=t1[:], in0=cnt[:], scalar1=255.5, scalar2=f, op0=ALU.is_ge, op1=ALU.mult),
                               rd=[h_bs], wr=[h_bs])
                          k.op('dve', lambda e: e.scalar_tensor_tensor(out=lo[:], in0=t1[:], scalar=w0[:, 0:1], in1=lo[:], op0=ALU.mult, op1=ALU.add),
                               rd=[h_bs], wr=[h_bs])
                  k.op('dve', lambda e, S=S: e.tensor_scalar(out=mask[:, 0:S], in0=acc[:, 0:S], scalar1=lo[:, 0:1], scalar2=None, op0=ALU.is_ge),
                       rd=[h_acc, h_bs], wr=[h_mask])
                  for k0 in range(0, n + 1, 4):
                      nk = min(4, n + 1 - k0)
                      for q in range(nk):
                          k.op('pe', lambda e, q=q, k0=k0: e.transpose(out=mtp[:, q * 128:(q + 1) * 128], in_=mask[:, (k0 + q) * 128:(k0 + q + 1) * 128],
                                                                       identity=ident_b[:]), rd=[h_mask, h_c], wr=[h_mtp])
                      for r in range(4):
                          eng = 'act' if r % 2 == 0 else 'dve'
                          if eng == 'act':
                              k.op('act', lambda e, r=r, k0=k0, nk=nk: e.activation(out=mT4[:, k0:k0 + nk, r, :],
                                                                                   in_=mtp[:, 0:nk * 128].rearrange("p (q t) -> p q t", q=nk), func=AF.Copy),
                                   rd=[h_mtp], wr=[h_mT])
                          else:
                              k.op('dve', lambda e, r=r, k0=k0, nk=nk: e.tensor_copy(out=mT4[:, k0:k0 + nk, r, :],
                                                                                    in_=mtp[:, 0:nk * 128].rearrange("p (q t) -> p q t", q=nk)),
                                   rd=[h_mtp], wr=[h_mT])
                  for kind in range(2):
                      if n - kind < 0:
                          continue
                      for g4 in range(4):
                          k.op('pool', lambda e, kind=kind, g4=g4, n=n: e.tensor_tensor(out=Mn[:, kind, g4 * 4:(g4 + 1) * 4, :],
                                                                                        in0=EBv[:, kind, g4 * 4:(g4 + 1) * 4, :],
                                                                                        in1=mT4[:, n - kind, :, :], op=ALU.mult),
                               rd=[h_mT, h_c], wr=[h_Mn])
                  pairs = [(g, kt) for g in range(4) for kt in range(n + 1)]

                  def attA(ii):
                      g, kt = pairs[ii]
                      sp_, hs_ = [(pA, h_pA), (pB, h_pB), (pC, h_pC)][ii % 3]
                      E_, hE = Eb[ii % 3], h_E[ii % 3]
                      P_, hP = Pb[ii % 3], h_P[ii % 3]
                      k.op('pe', lambda e: e.matmul(sp_[:], lhsT=kT[:, g, kt * 128:(kt + 1) * 128],
                                                    rhs=qt[b][:, g * 4:(g + 1) * 4, :], start=True, stop=True),
                           rd=[h_kv, h_ld[b]], wr=[hs_])
                      k.op('act', lambda e: e.activation(out=E_[:], in_=sp_[:], func=AF.Exp), rd=[hs_], wr=[hE])
                      kind = n - kt
                      if kind < 2:
                          k.op('dve', lambda e: e.tensor_tensor(
                              out=P_[:], in0=E_[:].rearrange("p (r t) -> p r t", r=4), in1=Mn[:, kind, g * 4:(g + 1) * 4, :], op=ALU.mult),
                              rd=[hE, h_Mn], wr=[hP])
                      else:
                          k.op('dve', lambda e: e.tensor_tensor(
                              out=P_[:], in0=E_[:].rearrange("p (r t) -> p r t", r=4), in1=mT4[:, kt, :, :], op=ALU.mult),
                              rd=[hE, h_mT], wr=[hP])

                  def attB(ii):
                      g, kt = pairs[ii]
                      P_, hP = Pb[ii % 3], h_P[ii % 3]
                      for r in range(4):
                          k.op('pe', lambda e, r=r: e.matmul(
                              oacc[r][:, 0:129], lhsT=P_[:, r, :], rhs=vA[:, kt, g * 130:g * 130 + 129], start=(kt == 0), stop=(kt == n)),
                              rd=[hP, h_kv], wr=[h_oa[r]])
                      if kt == n:
                          for r in range(4):
                              k.op('dve', lambda e, r=r: e.reciprocal(out=ri[:, r:r + 1], in_=oacc[r][:, 128:129]), rd=[h_oa[r]], wr=[h_ri])
                              k.op('dve', lambda e, r=r: e.tensor_scalar(out=ot[:, g * 4 + r, :], in0=oacc[r][:, 0:128], scalar1=ri[:, r:r + 1],
                                                                         scalar2=None, op0=ALU.mult), rd=[h_oa[r], h_ri], wr=[h_ot])

                  attA(0)
                  if len(pairs) > 1:
                      attA(1)
                  for ii in range(len(pairs)):
                      if ii + 2 < len(pairs):
                          attA(ii + 2)
                      attB(ii)
                  for g4 in range(4):
                      for r in range(4):
                          k.op('pe', lambda e, g4=g4, r=r: e.transpose(out=mtp[:, r * 128:(r + 1) * 128], in_=ot[:, g4 * 4 + r, :], identity=ident_b[:]),
                               rd=[h_ot, h_c], wr=[h_mtp])
                      k.op('dve', lambda e, g4=g4, b=b: e.tensor_tensor(out=yt[b][:, g4 * 4:(g4 + 1) * 4, :], in0=mtp[:].rearrange("p (r t) -> p r t", r=4),
                                                                        in1=sgbt[b][:, g4 * 4:(g4 + 1) * 4, :], op=ALU.mult),
                           rd=[h_mtp, h_ld[b]], wr=[h_yt[b]])
                  k.dma('sp', yT_d[2048:4096, c0:c0 + 128].rearrange("(h p) t -> p h t", p=128), yt[b][:], rd=[h_yt[b]])
              k.barrier()
        with ExitStack() as ls:
          if 'E4' in PH:
              sbl = lambda name, shape, dt: ls.enter_context(nc.sbuf_tensor(un(name), list(shape), dt))
              tl = alloc_tail(ls, 32, 4)
              gbc = tl[21]
              hblk = sbl("hblk", [128, 4, 2048], F32); h_hblk = k.h()
              yT = sbl("yTs", [128, 32, 512], BF16); h_yT = k.h()
              h_lc = k.h()
              k.dma('sp', gbc[:], pgn_d[layer, :].partition_broadcast(128), wr=[h_c])
              for tb in range(8):
                  k.maybe_barrier()
                  t0 = tb * 512
                  k.dma('sp', hblk[:], h_src[t0:t0 + 512, :].rearrange("(i p) d -> p i d", p=128), wr=[h_hblk])
                  for c in range(4):
                      k.dma('sp', yT[:, c * 8:(c + 1) * 8, :], yT_d[c * 1024:(c + 1) * 1024, t0:t0 + 512].rearrange("(c p) t -> p c t", p=128), wr=[h_yT])
                  tail_block(layer, tb, hblk, h_hblk, yT, h_yT, 32, ewo_d[j], tl)
                  k.dma('sp', h_dst[t0:t0 + 512, :].rearrange("(i p) d -> p i d", p=128), hblk[:], rd=[h_hblk])
              k.barrier()

    for layer in layer_list:
        j = layer // 2
        h_src = x_d if layer == layer_list[0] else hbuf_d
        h_dst = y_d if layer == layer_list[-1] else hbuf_d
        if layer % 2 == 0:
            even_layer(layer, j, h_src, h_dst)
        else:
            odd_layer(layer, j, h_src, h_dst)

    k.barrier()
    es.close()
    return nc


def _prep_inputs(inp):
    f = lambda a: np.ascontiguousarray(np.asarray(a, dtype=np.float32))
    W = {kk: f(v) for kk, v in inp.items()}
    rng = np.arange
    shared = {}
    shared["norm_gain"] = W["norm_gain"]
    shared["pe_gate_norm"] = W["pe_gate_norm"]
    shared["consts"] = np.eye(128, dtype=np.float32)
    s_ = rng(128)[:, None]
    t_ = rng(128)[None, :]
    oh = np.zeros((32, 2, 128, 128), np.float32)
    for kind in range(2):
        bk = t5_bucket_np((s_ - t_ - 128 * kind).astype(np.int32))
        for b in range(32):
            oh[b, kind] = (bk == b).astype(np.float32)
    oh[15] -= 1.0
    shared["oh"] = oh.reshape(32, -1)
    shared["rel_bias"] = W["rel_bias"]
    cidx = np.concatenate([rng(0, 2048), rng(2048, 4096), rng(4096, 6144), rng(6144, 6656), rng(7168, 9216),
                           rng(9216, 10240), rng(10240, 10304), rng(10240, 10304)])
    ewf = np.empty((2, 77, 128, 2048), np.float32)
    ewv = np.empty((2, 128, 8192), np.float32)
    ewiw = np.empty((2, 128, 256), np.float32)
    ewo = np.empty((2, 4, 128, 32 * 512), np.float32)
    lwr = np.empty((2, 128, 2048), np.float32)
    lwi = np.empty((2, 128, 2048), np.float32)
    evec = np.empty((2, 128, 8, 16), np.float32)
    qkn = np.empty((2, 128, 2), np.float32)
    owf = np.empty((2, 32, 128, 2048), np.float32)
    owv = np.empty((2, 4, 128, 8192), np.float32)
    owo = np.empty((2, 4, 128, 8192), np.float32)
    wst = np.empty((2, 128, 2048), np.float32)
    ovec = np.empty((2, 128, 32), np.float32)
    for j in range(2):
        Wi = W["even_w_in"][j]
        ewf[j] = Wi[:, cidx].reshape(16, 128, 77, 128).transpose(2, 1, 0, 3).reshape(77, 128, 2048)
        ewv[j] = Wi[:, 6656:7168].reshape(16, 128, 512).transpose(1, 0, 2).reshape(128, 8192)
        ewiw[j] = Wi[:, 10304:10320].reshape(16, 128, 16).transpose(1, 0, 2).reshape(128, 256)
        ewo[j] = W["even_w_out"][j].reshape(32, 128, 4, 512).transpose(2, 1, 0, 3).reshape(4, 128, 32 * 512)
        lwr[j] = W["lru_w_r"][j].transpose(1, 0, 2).reshape(128, 2048)
        lwi[j] = W["lru_w_i"][j].transpose(1, 0, 2).reshape(128, 2048)
        pc = lambda v: v.reshape(16, 128).T
        for a in range(4):
            evec[j, :, a, :] = pc(W["conv_w"][j, a])
        evec[j, :, 4, :] = pc(W["conv_b"][j])
        evec[j, :, 5, :] = pc(W["lru_b_r"][j].reshape(-1))
        evec[j, :, 6, :] = pc(W["lru_b_i"][j].reshape(-1))
        evec[j, :, 7, :] = pc(W["lru_lambda"][j])
        qkn[j, :, 0] = W["q_norm"][j]
        qkn[j, :, 1] = W["k_norm"][j]
        Wo = W["odd_w_in"][j]
        oidx = np.concatenate([rng(0, 2048), rng(4096, 6144)])
        owf[j] = Wo[:, oidx].reshape(16, 128, 32, 128).transpose(2, 1, 0, 3).reshape(32, 128, 2048)
        owv[j] = Wo[:, 2048:4096].reshape(16, 128, 4, 512).transpose(2, 1, 0, 3).reshape(4, 128, 8192)
        owo[j] = W["odd_w_out"][j].reshape(16, 128, 4, 512).transpose(2, 1, 0, 3).reshape(4, 128, 8192)
        wst[j] = W["sg_w_s"][j].transpose(2, 0, 1).reshape(128, 2048)
        ovec[j, :, 0:16] = pc(W["sg_ln_g"][j])
        ovec[j, :, 16:32] = pc(W["sg_ln_b"][j])
    shared.update(ewf=ewf, ewv=ewv, ewiw=ewiw, ewo=ewo, lwr=lwr, lwi=lwi, evec=evec.reshape(2, 128, 128), qkn=qkn,
                  owf=owf, owv=owv, owo=owo, wst=wst, ovec=ovec, bs=W["sg_b_s"])
    shared["pew"] = np.ascontiguousarray(W["pe_w"].reshape(4, 2, 128, 4, 512).transpose(0, 3, 2, 1, 4).reshape(4, 4, 128, 1024))
    shared["pwg"] = np.ascontiguousarray(W["pe_w_gate"].reshape(4, 16, 128, 4, 512).transpose(0, 3, 2, 1, 4).reshape(4, 4, 128, 8192))
    shared = {kk: np.ascontiguousarray(v, dtype=np.float32) for kk, v in shared.items()}
    maps = []
    for b in range(2):
        m = dict(shared)
        m["x"] = np.ascontiguousarray(W["x"][b])
        m["p"] = np.ascontiguousarray(W["p"][:, b])
        maps.append(m)
    return maps


def kernel(**inputs):
    ll = os.environ.get("KLAYERS")
    layer_list = [int(v) for v in ll.split(",")] if ll else [0, 1, 2, 3]
    maps = _prep_inputs(inputs)
    nc = build_program(layer_list)
    res = run_bass_kernel_spmd(nc, maps, core_ids=[0, 1], trace=bool(os.environ.get('KTRACE')))
    if os.environ.get('KTRACE'):
        print('EXEC_TIME_NS', res.exec_time_ns, flush=True)
    out = np.stack([np.asarray(res.results[b]["y"], dtype=np.float32) for b in range(2)], axis=0)
    if os.environ.get("KDEBUG"):
        global LAST_RESULTS
        LAST_RESULTS = res.results
    return out
```
